# Optimizing a Trainium2 kernel written in Bass

```python
import math
import jax, jax.numpy as jnp
from jax import lax
import numpy as np


D_MODEL = 2048
BATCH = 4
SEQ = 2048
DEPTH = 1

MEM_LEN = 256
ML_HEADS = 4
ML_QK = 256
ML_V = 512
ML_CHUNK = 64
CONV_W = 4
ML_QK_W = ML_HEADS * ML_QK
ML_V_W = ML_HEADS * ML_V
FOX_HEADS = 16
FOX_HD = 128
FOX_W = FOX_HEADS * FOX_HD
Q_BLOCK = 128
X_HEADS = 4
X_HD = D_MODEL // X_HEADS
PEER_HEADS = 8
N_KEYS = 128
N_EXPERTS = N_KEYS * N_KEYS
PEER_DK = 256
PEER_TOPK = 16
PEER_TOK_BLOCK = 128
LN_EPS = 1e-5
DN_ALPHA = (2 * DEPTH) ** 0.25
DN_BETA = (8 * DEPTH) ** -0.25
SPLIT_SIZES = (ML_QK_W, ML_QK_W, ML_V_W, ML_V_W, ML_HEADS, ML_HEADS,
               FOX_W, FOX_W, FOX_W, FOX_HEADS, D_MODEL, D_MODEL)
IN_PROJ_W = sum(SPLIT_SIZES)

kernel_name = 'hybrid_mlstm_fox_peer_deepnorm'

F32 = jnp.float32


def layer_norm(x, w, b):
    xf = x.astype(F32)
    mu = jnp.mean(xf, -1, keepdims=True)
    var = jnp.mean(jnp.square(xf - mu), -1, keepdims=True)
    return ((xf - mu) * lax.rsqrt(var + LN_EPS) * w.astype(F32) + b.astype(F32)).astype(x.dtype)


def _heads(t, n):
    B, S, _ = t.shape
    return t.reshape(B, S, n, -1).transpose(0, 2, 1, 3)


def causal_dwconv(x, w, b):
    C = x.shape[-1]
    y = lax.conv_general_dilated(x, w[:, None, :].astype(x.dtype), window_strides=(1,),
                                 padding=((CONV_W - 1, 0),), dimension_numbers=('NWC', 'WIO', 'NWC'),
                                 feature_group_count=C)
    return y + b.astype(x.dtype)


def mlstm_chunkwise(q, k, v, i_pre, f_pre):
    B, H, S, dk = q.shape
    dv = v.shape[-1]
    L = ML_CHUNK
    nc = S // L
    q = q.astype(F32).reshape(B, H, nc, L, dk)
    k = (k.astype(F32) * dk ** -0.5).reshape(B, H, nc, L, dk)
    v = v.astype(F32).reshape(B, H, nc, L, dv)
    ig = i_pre.reshape(B, H, nc, L)
    b = jnp.cumsum(jax.nn.log_sigmoid(f_pre).reshape(B, H, nc, L), -1)
    b_last = b[..., -1]
    w_end = b_last[..., None] - b + ig
    m_end_intra = jnp.max(w_end, -1)

    def step(carry, inp):
        C, n, m = carry
        k_c, v_c, w_c, bl_c, mi_c = inp
        m_new = jnp.maximum(bl_c + m, mi_c)
        decay = jnp.exp(bl_c + m - m_new)
        wk = jnp.exp(w_c - m_new[..., None])[..., None] * k_c
        C_new = decay[..., None, None] * C + jnp.einsum('bhld,bhle->bhde', wk, v_c)
        n_new = decay[..., None] * n + jnp.sum(wk, -2)
        return (C_new, n_new, m_new), (C, n, m)

    init = (jnp.zeros((B, H, dk, dv), F32), jnp.zeros((B, H, dk), F32), jnp.zeros((B, H), F32))
    xs = (jnp.moveaxis(k, 2, 0), jnp.moveaxis(v, 2, 0), jnp.moveaxis(w_end, 2, 0),
          jnp.moveaxis(b_last, 2, 0), jnp.moveaxis(m_end_intra, 2, 0))
    _, (C_prev, n_prev, m_prev) = lax.scan(step, init, xs)
    C_prev = jnp.moveaxis(C_prev, 0, 2)
    n_prev = jnp.moveaxis(n_prev, 0, 2)
    m_prev = jnp.moveaxis(m_prev, 0, 2)

    causal = jnp.tril(jnp.ones((L, L), dtype=bool))
    Dlog = jnp.where(causal, b[..., :, None] - b[..., None, :] + ig[..., None, :], -jnp.inf)
    a = b + m_prev[..., None]
    m_t = jnp.maximum(a, jnp.max(Dlog, -1))
    P = jnp.exp(Dlog - m_t[..., None]) * jnp.einsum('bhctd,bhcsd->bhcts', q, k)
    inter = jnp.exp(a - m_t)
    num = jnp.einsum('bhcts,bhcse->bhcte', P, v) + inter[..., None] * jnp.einsum('bhctd,bhcde->bhcte', q, C_prev)
    den = jnp.sum(P, -1) + inter * jnp.einsum('bhctd,bhcd->bhct', q, n_prev)
    h = num / jnp.maximum(jnp.abs(den), jnp.exp(-m_t))[..., None]
    return h.reshape(B, H, S, dv)


def forgetting_attention(q, k, v, f_pre):
    B, H, S, d = q.shape
    F = jnp.cumsum(jax.nn.log_sigmoid(f_pre), -1)
    nb = S // Q_BLOCK
    qb = q.reshape(B, H, nb, Q_BLOCK, d).transpose(2, 0, 1, 3, 4)
    Fqb = F.reshape(B, H, nb, Q_BLOCK).transpose(2, 0, 1, 3)
    pos_k = jnp.arange(S)
    scale = d ** -0.5

    def block(args):
        qi, Fi, start = args
        s = jnp.einsum('bhqd,bhkd->bhqk', qi, k, preferred_element_type=F32) * scale
        s = s + Fi[..., :, None] - F[..., None, :]
        pos_q = start + jnp.arange(Q_BLOCK)
        s = jnp.where(pos_k[None, :] <= pos_q[:, None], s, -jnp.inf)
        p = jax.nn.softmax(s, -1)
        return jnp.einsum('bhqk,bhkd->bhqd', p.astype(v.dtype), v)

    out = lax.map(block, (qb, Fqb, jnp.arange(nb) * Q_BLOCK))
    return out.transpose(1, 2, 0, 3, 4).reshape(B, H, S, d)


def hybrid_mixer(x, w_in, conv_w, conv_b, gate_b, norm_w, fox_f_b, w_br_ml, w_br_fox, w_out):
    B, S, _ = x.shape
    cuts = [int(c) for c in np.cumsum(SPLIT_SIZES)[:-1]]
    (ml_q, ml_k, ml_v, ml_o, ml_i, ml_f,
     fx_q, fx_k, fx_v, fx_f, g_ml, g_fx) = jnp.split(x @ w_in, cuts, axis=-1)
    qk = jax.nn.silu(causal_dwconv(jnp.concatenate([ml_q, ml_k], -1), conv_w, conv_b))
    ml_q, ml_k = qk[..., :ML_QK_W], qk[..., ML_QK_W:]
    i_pre = (ml_i.astype(F32) + gate_b[0].astype(F32)).transpose(0, 2, 1)
    f_pre = (ml_f.astype(F32) + gate_b[1].astype(F32)).transpose(0, 2, 1)
    h = mlstm_chunkwise(_heads(ml_q, ML_HEADS), _heads(ml_k, ML_HEADS), _heads(ml_v, ML_HEADS), i_pre, f_pre)
    mu = jnp.mean(h, -1, keepdims=True)
    var = jnp.mean(jnp.square(h - mu), -1, keepdims=True)
    h = (h - mu) * lax.rsqrt(var + LN_EPS) * norm_w.astype(F32).reshape(ML_HEADS, 1, ML_V)
    h = h.transpose(0, 2, 1, 3).reshape(B, S, ML_V_W).astype(x.dtype) * jax.nn.sigmoid(ml_o)
    fox_f_pre = (fx_f.astype(F32) + fox_f_b.astype(F32)).transpose(0, 2, 1)
    y = forgetting_attention(_heads(fx_q, FOX_HEADS), _heads(fx_k, FOX_HEADS), _heads(fx_v, FOX_HEADS), fox_f_pre)
    y = y.transpose(0, 2, 1, 3).reshape(B, S, FOX_W)
    merged = jax.nn.sigmoid(g_ml) * (h @ w_br_ml) + jax.nn.sigmoid(g_fx) * (y @ w_br_fox)
    return merged @ w_out


def memory_cross_attention(x, mem, w_q, w_kv, w_o):
    B, S, _ = x.shape
    q = (x @ w_q).reshape(B, S, X_HEADS, X_HD)
    kv = mem @ w_kv
    k = kv[..., :D_MODEL].reshape(B, -1, X_HEADS, X_HD)
    v = kv[..., D_MODEL:].reshape(B, -1, X_HEADS, X_HD)
    s = jnp.einsum('bshd,bmhd->bhsm', q, k, preferred_element_type=F32) * X_HD ** -0.5
    p = jax.nn.softmax(s, -1)
    o = jnp.einsum('bhsm,bmhd->bshd', p.astype(v.dtype), v).reshape(B, S, D_MODEL)
    return o @ w_o


def peer_ffn(x, w_pq, sub_keys, expert_u, expert_v):
    B, S, D = x.shape
    q = (x @ w_pq).reshape(B, S, PEER_HEADS, 2, PEER_DK // 2)
    sc = jnp.einsum('bshpd,hpnd->bshpn', q, sub_keys, preferred_element_type=F32)
    top_s, top_i = lax.top_k(sc, PEER_TOPK)
    cand_s = (top_s[..., 0, :, None] + top_s[..., 1, None, :]).reshape(B, S, PEER_HEADS, PEER_TOPK * PEER_TOPK)
    cand_i = (top_i[..., 0, :, None] * N_KEYS + top_i[..., 1, None, :]).reshape(B, S, PEER_HEADS, PEER_TOPK * PEER_TOPK)
    best_s, best_pos = lax.top_k(cand_s, PEER_TOPK)
    idx = jnp.take_along_axis(cand_i, best_pos, -1)
    g = jax.nn.softmax(best_s, -1)
    nblk = (B * S) // PEER_TOK_BLOCK
    HK = PEER_HEADS * PEER_TOPK
    xb = x.reshape(nblk, PEER_TOK_BLOCK, D)
    ib = idx.reshape(nblk, PEER_TOK_BLOCK, HK)
    gb = g.reshape(nblk, PEER_TOK_BLOCK, HK)

    def block(args):
        xt, it, gt = args
        u = expert_u[it]
        a = jnp.einsum('tkd,td->tk', u, xt, preferred_element_type=F32)
        hact = jax.nn.gelu(a, approximate=False) * gt
        return jnp.einsum('tk,tkd->td', hact.astype(x.dtype), expert_v[it])

    return lax.map(block, (xb, ib, gb)).reshape(B, S, D)


def setup_inputs(seed: int = 0) -> dict:
    key = jax.random.key(seed)
    ks = jax.random.split(key, 24)
    nrm = lambda k, shape, s: jax.random.normal(k, shape, F32) * s
    dinv = D_MODEL ** -0.5
    gate_b = jnp.stack([nrm(ks[5], (DEPTH, ML_HEADS), 0.1),
                        jnp.linspace(3.0, 6.0, ML_HEADS)[None, :] + nrm(ks[6], (DEPTH, ML_HEADS), 0.1)], axis=1)
    return {
        'x': jax.random.normal(ks[0], (BATCH, SEQ, D_MODEL), F32),
        'mem': jax.random.normal(ks[1], (BATCH, MEM_LEN, D_MODEL), F32),
        'w_in': nrm(ks[2], (DEPTH, D_MODEL, IN_PROJ_W), dinv),
        'ml_conv_w': nrm(ks[3], (DEPTH, CONV_W, 2 * ML_QK_W), CONV_W ** -0.5),
        'ml_conv_b': nrm(ks[4], (DEPTH, 2 * ML_QK_W), 0.02),
        'ml_gate_b': gate_b,
        'ml_norm_w': 1.0 + nrm(ks[7], (DEPTH, ML_V_W), 0.02),
        'fox_f_b': jnp.linspace(1.0, 4.0, FOX_HEADS)[None, :] + nrm(ks[8], (DEPTH, FOX_HEADS), 0.1),
        'w_branch_ml': nrm(ks[9], (DEPTH, ML_V_W, D_MODEL), ML_V_W ** -0.5 * DN_BETA),
        'w_branch_fox': nrm(ks[10], (DEPTH, FOX_W, D_MODEL), FOX_W ** -0.5 * DN_BETA),
        'w_out': nrm(ks[11], (DEPTH, D_MODEL, D_MODEL), dinv * DN_BETA),
        'w_xq': nrm(ks[12], (DEPTH, D_MODEL, D_MODEL), dinv),
        'w_xkv': nrm(ks[13], (DEPTH, D_MODEL, 2 * D_MODEL), dinv),
        'w_xo': nrm(ks[14], (DEPTH, D_MODEL, D_MODEL), dinv * DN_BETA),
        'peer_wq': nrm(ks[15], (DEPTH, D_MODEL, PEER_HEADS * PEER_DK), dinv),
        'peer_sub_keys': nrm(ks[16], (DEPTH, PEER_HEADS, 2, N_KEYS, PEER_DK // 2), (PEER_DK // 2) ** -0.5),
        'peer_u': nrm(ks[17], (DEPTH, N_EXPERTS, D_MODEL), dinv),
        'peer_v': nrm(ks[18], (DEPTH, N_EXPERTS, D_MODEL), DN_BETA * PEER_HEADS ** -0.5),
        'ln_w': 1.0 + nrm(ks[19], (DEPTH, 3, D_MODEL), 0.02),
        'ln_b': nrm(ks[20], (DEPTH, 3, D_MODEL), 0.02),
    }


def reference(x, mem, w_in, ml_conv_w, ml_conv_b, ml_gate_b, ml_norm_w, fox_f_b, w_branch_ml, w_branch_fox,
              w_out, w_xq, w_xkv, w_xo, peer_wq, peer_sub_keys, peer_u, peer_v, ln_w, ln_b):
    for l in range(DEPTH):
        h = hybrid_mixer(x, w_in[l], ml_conv_w[l], ml_conv_b[l], ml_gate_b[l], ml_norm_w[l], fox_f_b[l],
                         w_branch_ml[l], w_branch_fox[l], w_out[l])
        x = layer_norm(DN_ALPHA * x + h, ln_w[l, 0], ln_b[l, 0])
        h = memory_cross_attention(x, mem, w_xq[l], w_xkv[l], w_xo[l])
        x = layer_norm(DN_ALPHA * x + h, ln_w[l, 1], ln_b[l, 1])
        h = peer_ffn(x, peer_wq[l], peer_sub_keys[l], peer_u[l], peer_v[l])
        x = layer_norm(DN_ALPHA * x + h, ln_w[l, 2], ln_b[l, 2])
    return x
```

```python
import math
import numpy as np
from contextlib import ExitStack
import concourse.bass as bass
import concourse.mybir as mybir
from concourse.bass_utils import run_bass_kernel_spmd

F32 = mybir.dt.float32
BF16 = mybir.dt.bfloat16
U32 = mybir.dt.uint32
U8 = mybir.dt.uint8
AF = mybir.ActivationFunctionType
ALU = mybir.AluOpType
AX = mybir.AxisListType
DTSIZE = {F32: 4, BF16: 2, U32: 4, U8: 1}

ENG = ['pe', 'act', 'dve', 'pool', 'sp']
NDS = 32
SAME_ENGINE_SYNC = ('act', 'dve', 'pool')

D = 2048
T = 1024
NT = 8
LN_EPS = 1e-5
DN_ALPHA = 2 ** 0.25
C_MLQ, C_MLK, C_MLV, C_MLO, C_MLI, C_MLF = 0, 1024, 2048, 4096, 6144, 6148
C_FXQ, C_FXK, C_FXV, C_FXF, C_GML, C_GFX = 6152, 8200, 10248, 12296, 12312, 14360
IN_W = 16408


def _base(k):
    return k[0] if isinstance(k, tuple) else k


class _Rec:
    def __init__(self):
        self.calls = []

    def __getattr__(self, name):
        def f(*a, **k):
            self.calls.append((name, a, k))
            return self
        return f


class Prog:
    def __init__(self, nc, es):
        self.nc = nc
        self.q = {e: [] for e in ENG}
        self.sem = {e: es.enter_context(nc.semaphore("s_" + e)) for e in ENG}
        self.cnt = {e: 0 for e in ENG}
        self.dsem = [es.enter_context(nc.semaphore("d%d" % i)) for i in range(NDS)]
        self.dcnt = [0] * NDS
        self.dnext = 0
        self.dnx = [0, 0]
        self.waited = {e: {} for e in ENG}
        self.hist = {}
        self.inherit = {}

    def need(self, eng, ev):
        sem, val = ev
        if sem is self.sem.get(eng) and eng not in SAME_ENGINE_SYNC:
            return
        w = self.waited[eng]
        if w.get(sem.name, 0) >= val:
            return
        w[sem.name] = val
        self.q[eng].append(('wait', sem, val))

    def _deps(self, eng, reads, writes):
        evs = []
        for k in reads:
            h = self.hist.get(k)
            if h and h['w']:
                evs.append(h['w'])
        for k in writes:
            h = self.hist.get(k)
            if h:
                if h['w']:
                    evs.append(h['w'])
                evs.extend(h['r'].values())
            evs.extend(self.inherit.get(_base(k), ()))
        for ev in evs:
            self.need(eng, ev)

    def _record(self, me, reads, writes):
        for k in reads:
            h = self.hist.setdefault(k, {'w': None, 'r': {}})
            h['r'][me[0].name] = me
        for k in writes:
            self.hist[k] = {'w': me, 'r': {}}

    def op(self, eng, fn, reads=(), writes=()):
        self._deps(eng, reads, writes)
        self.cnt[eng] += 1
        me = (self.sem[eng], self.cnt[eng])
        r = _Rec()
        fn(r)
        assert len(r.calls) == 1
        self.q[eng].append(('op', r.calls[0], self.sem[eng], 1))
        self._record(me, reads, writes)
        return me

    def dma(self, eng, fn, reads=(), writes=()):
        half = NDS // 2
        pi = 0 if eng == 'sp' else 1
        i = pi * half + self.dnx[pi]
        self.dnx[pi] = (self.dnx[pi] + 1) % half
        self._deps(eng, reads, writes)
        self.need(eng, (self.dsem[i], self.dcnt[i]))
        self.dcnt[i] += 16
        me = (self.dsem[i], self.dcnt[i])
        r = _Rec()
        fn(r)
        assert len(r.calls) == 1
        self.q[eng].append(('op', r.calls[0], self.dsem[i], 16))
        self._record(me, reads, writes)
        return me

    def all_events(self, keybase):
        evs = []
        for k, h in self.hist.items():
            if _base(k) == keybase:
                if h['w']:
                    evs.append(h['w'])
                evs.extend(h['r'].values())
        evs.extend(self.inherit.get(keybase, ()))
        return evs

    def emit(self):
        nc = self.nc
        q = self.q

        def replay(engobj, e):
            for it in q[e]:
                if it[0] == 'wait':
                    engobj.wait_ge(it[1], it[2])
                else:
                    name, a, k = it[1]
                    ins = getattr(engobj, name)(*a, **k)
                    ins.then_inc(it[2], it[3])

        with nc.Block() as block:
            @block.tensor
            def _(pe):
                replay(pe, 'pe')

            @block.scalar
            def _(act):
                replay(act, 'act')

            @block.vector
            def _(dve):
                replay(dve, 'dve')

            @block.gpsimd
            def _(pool):
                replay(pool, 'pool')

            @block.sync
            def _(sp):
                replay(sp, 'sp')


class Arena:
    def __init__(self, nc, es, prog, nbytes):
        self.prog = prog
        self.cap = nbytes
        self.t = es.enter_context(nc.sbuf_tensor("arena", [128, nbytes], U8))
        self.live = {}
        self.dead = []

    def alloc(self, key, shape, dtype):
        size = (int(np.prod(shape)) * DTSIZE[dtype] + 63) // 64 * 64
        ivs = sorted(self.live.values())
        lo = 0
        for (l2, h2) in ivs:
            if lo + size <= l2:
                break
            lo = max(lo, h2)
        hi = lo + size
        assert hi <= self.cap, ("arena overflow", key, size, sorted((v, k) for k, v in self.live.items()))
        assert key not in self.live
        self.live[key] = (lo, hi)
        best = {}
        nd = []
        for (l2, h2, evs) in self.dead:
            if l2 < hi and lo < h2:
                for (s, v) in evs:
                    if best.get(s.name, (None, 0))[1] < v:
                        best[s.name] = (s, v)
            nd.append((l2, h2, evs))
        self.dead = nd
        self.prog.inherit[key] = list(best.values())
        for k in [k for k in self.prog.hist if _base(k) == key]:
            del self.prog.hist[k]
        ap = self.t[:, lo:lo + int(np.prod(shape)) * DTSIZE[dtype]].bitcast(dtype)
        if len(shape) == 2:
            ap = ap.rearrange("p (a b) -> p a b", a=shape[0], b=shape[1])
        elif len(shape) == 3:
            ap = ap.rearrange("p (a b c) -> p a b c", a=shape[0], b=shape[1], c=shape[2])
        return ap

    def free(self, *keys):
        for key in keys:
            lo, hi = self.live.pop(key)
            self.dead.append((lo, hi, self.prog.all_events(key)))


def build_nc(stage=99):
    nc = bass.Bass("TRN2", target_bir_lowering=False)
    dt_in = lambda n, s: nc.dram_tensor(n, s, F32, kind="ExternalInput").ap()
    x_own = dt_in("x_own", [T, D])
    x_prev = dt_in("x_prev", [T, D])
    mem = dt_in("mem", [256, D])
    cst = dt_in("cst", [128, 6 * 128])
    w_in = dt_in("w_in", [D, IN_W])
    conv_w = dt_in("conv_w", [128, 64])
    conv_b = dt_in("conv_b", [128, 16])
    gate_b = dt_in("gate_b", [1, 8])
    norm_w = dt_in("norm_w", [1, 2048])
    fox_fb = dt_in("fox_fb", [1, 16])
    w_bml = dt_in("w_bml", [D, D])
    w_bfx = dt_in("w_bfx", [D, D])
    w_out = dt_in("w_out", [D, D])
    w_xq = dt_in("w_xq", [D, D])
    w_xkv = dt_in("w_xkv", [D, 2 * D])
    w_xo = dt_in("w_xo", [D, D])
    w_pq = dt_in("w_pq", [D, D])
    subk = dt_in("subk", [16 * 128, 128])
    peer_u = dt_in("peer_u", [16384 if stage >= 5 else 128, D])
    peer_v = dt_in("peer_v", [16384 if stage >= 5 else 128, D])
    ln_w = dt_in("ln_w", [3, D])
    ln_b = dt_in("ln_b", [3, D])
    out = nc.dram_tensor("out", [T, D], F32, kind="ExternalOutput").ap()
    uv16 = nc.dram_tensor("uv16", [16384, 2 * D], BF16).ap() if stage >= 5 else None

    with ExitStack() as es:
        P = Prog(nc, es)
        A = Arena(nc, es, P, 207 * 1024)
        ps = [es.enter_context(nc.psum_tensor("ps%d" % i, [128, 512], F32)) for i in range(8)]
        psb = [p[:].bitcast(BF16) for p in ps]
        rot = [0]

        def nb():
            rot[0] = (rot[0] + 1) % 4
            return rot[0]

        altc = [0]

        def evac_eng():
            altc[0] ^= 1
            return 'act' if altc[0] else 'dve'

        def copy(eng, o, i, reads, writes):
            if eng == 'act':
                P.op('act', lambda e: e.activation(out=o, in_=i, func=AF.Copy), reads=reads, writes=writes)
            else:
                P.op(eng, lambda e: e.tensor_copy(out=o, in_=i), reads=reads, writes=writes)

        cstf = A.alloc("cstf", [6, 128], F32)
        P.dma('sp', lambda e: e.dma_start(out=cstf, in_=cst.rearrange("p (a b) -> p a b", a=6)), writes=["cstf"])
        iota_f = cstf[:, 4:6, :].rearrange("p a b -> p (a b)")
        ident_f, tri_f, ones_f, vfl_f = cstf[:, 0, :], cstf[:, 1, :], cstf[:, 2, :], cstf[:, 3, :]
        cstb = A.alloc("cstb", [4, 128], BF16)
        P.op('dve', lambda e: e.tensor_copy(out=cstb, in_=cstf[:, 0:4, :]), reads=["cstf"], writes=["cstb"])
        ident_b, tri_b, ones_b, vfl_b = cstb[:, 0, :], cstb[:, 1, :], cstb[:, 2, :], cstb[:, 3, :]
        CST = ["cstf", "cstb"]

        convw = A.alloc("convw", [16, 4], F32)
        convb = A.alloc("convb", [16], F32)
        P.dma('sp', lambda e: e.dma_start(out=convw, in_=conv_w.rearrange("p (t j) -> p t j", j=4)), writes=["convw"])
        P.dma('sp', lambda e: e.dma_start(out=convb, in_=conv_b), writes=["convb"])
        gbias = A.alloc("gbias", [24], F32)
        P.dma('sp', lambda e: e.dma_start(out=gbias[:, 0:8], in_=gate_b.broadcast_to([128, 8])), writes=["gbias"])
        P.dma('sp', lambda e: e.dma_start(out=gbias[:, 8:24], in_=fox_fb.broadcast_to([128, 16])), writes=["gbias"])

        WSL = 2
        wt = A.alloc("wt", [WSL, 16, 256], BF16)
        wslot = [0]

        NCH = 256
        cv = {'next': 0, 'per': 0, 'buf': None}

        def conv_chunks(n):
            if stage < 5 or cv['buf'] is None:
                return
            for _ in range(n):
                c = cv['next']
                if c >= NCH:
                    return
                cv['next'] += 1
                src, dst, key = (peer_u, uv16[:, 0:D], "U16") if c < 128 else (peer_v, uv16[:, D:2 * D], "V16")
                r0 = (c % 128) * 128
                sl = c % 4
                CVS = cv['buf']
                P.dma('pool', lambda e: e.dma_start(out=CVS[:, sl, :], in_=src[r0:r0 + 128, :]), writes=[("CVS", sl)])
                P.dma('sp', lambda e: e.dma_start(out=dst[r0:r0 + 128, :], in_=CVS[:, sl, :]), reads=[("CVS", sl)], writes=[(key, c % 128)])

        def load_w(parts):
            conv_chunks(cv['per'])
            s = wslot[0]
            wslot[0] = (s + 1) % WSL
            off = 0
            for (src, n) in parts:
                P.dma('pool', lambda e, src=src, n=n, off=off: e.dma_start(
                    out=wt[:, s, :, off:off + n], in_=src.rearrange("(c p) n -> p c n", p=128)),
                    writes=[("wt", s)])
                off += n
            return s

        def proj_fm(s, c0, xT, xkey, t0, tn, evac):
            b = nb()
            for c in range(16):
                P.op('pe', lambda e, c=c: e.matmul(ps[b][:, 0:tn], lhsT=wt[:, s, c, c0:c0 + 128], rhs=xT[:, c, t0:t0 + tn],
                                                    start=(c == 0), stop=(c == 15)),
                     reads=[("wt", s), xkey], writes=[("ps", b)])
            evac(b, ps[b][:, 0:tn])

        def proj_tm(s, n, xT, xkey, tt, evac):
            b = nb()
            for c in range(16):
                P.op('pe', lambda e, c=c: e.matmul(ps[b][:, 0:n], lhsT=xT[:, c, tt * 128:(tt + 1) * 128], rhs=wt[:, s, c, 0:n],
                                                    start=(c == 0), stop=(c == 15)),
                     reads=[("wt", s), xkey], writes=[("ps", b)])
            evac(b, ps[b][:, 0:n])

        xTp = A.alloc("xTp", [16, T], BF16)
        xTo = A.alloc("xTo", [16, T], BF16)
        xTh = A.alloc("xTh", [16, 4], BF16)

        def load_xT(src, rows, dst, dkey, srckey_reads=()):
            xt = A.alloc("xt", [2, D], F32)
            for tt in range(rows // 128):
                sl = tt % 2
                P.dma('sp', lambda e, tt=tt, sl=sl: e.dma_start(out=xt[:, sl, :], in_=src[tt * 128:(tt + 1) * 128, :]),
                      writes=[("xt", sl)])
                for g in range(4):
                    b = nb()
                    for j in range(4):
                        c = g * 4 + j
                        P.op('pe', lambda e, c=c, j=j, b=b, sl=sl: e.transpose(ps[b][:, j * 128:(j + 1) * 128], xt[:, sl, c * 128:(c + 1) * 128], ident_f),
                             reads=[("xt", sl), "cstf"], writes=[("ps", b)])
                    copy(evac_eng(), dst[:, 4 * g:4 * g + 4, tt * 128:(tt + 1) * 128],
                         ps[b][:].rearrange("p (a b) -> p a b", a=4), [("ps", b)], [dkey])
            A.free("xt")

        load_xT(x_prev, T, xTp, "xTp")
        P.op('dve', lambda e: e.tensor_copy(out=xTh[:, :, 0:3], in_=xTp[:, :, T - 3:T]), reads=["xTp"], writes=["xTh"])
        load_xT(x_own, T, xTo, "xTo")

        G = A.alloc("G", [16, 24], F32)
        sg = load_w([(w_in[:, C_MLI:C_MLI + 8], 8), (w_in[:, C_FXF:C_FXF + 16], 16)])
        for tt in range(16):
            xT, xk = (xTp, "xTp") if tt < 8 else (xTo, "xTo")
            proj_tm(sg, 24, xT, xk, tt % 8,
                    lambda b, ap, tt=tt: P.op('dve', lambda e: e.tensor_tensor(out=G[:, tt, :], in0=ap, in1=gbias, op=ALU.add),
                                              reads=[("ps", b), "gbias"], writes=["G"]))
        LF = A.alloc("LF", [16, 20], F32)
        P.op('act', lambda e: e.activation(out=LF, in_=G[:, :, 4:24], func=AF.Exp, scale=-1.0), reads=["G"], writes=["LF"])
        P.op('act', lambda e: e.activation(out=LF, in_=LF, func=AF.Ln, bias=1.0), reads=["LF"], writes=["LF"])
        P.op('dve', lambda e: e.tensor_scalar(out=LF, in0=LF, scalar1=-1.0, scalar2=None, op0=ALU.mult), reads=["LF"], writes=["LF"])
        BC = A.alloc("BC", [16, 20], F32)
        TOT = A.alloc("TOT", [16, 20], F32)
        P.op('pe', lambda e: e.matmul(ps[4][:, 0:320], lhsT=tri_f, rhs=LF.rearrange("p a b -> p (a b)"), start=True, stop=True),
             reads=["LF", "cstf"], writes=[("ps", 4)])
        P.op('pe', lambda e: e.matmul(ps[5][:, 0:320], lhsT=ones_f, rhs=LF.rearrange("p a b -> p (a b)"), start=True, stop=True),
             reads=["LF", "cstf"], writes=[("ps", 5)])
        P.op('dve', lambda e: e.tensor_copy(out=BC.rearrange("p a b -> p (a b)"), in_=ps[4][:, 0:320]), reads=[("ps", 4)], writes=["BC"])
        P.op('dve', lambda e: e.tensor_copy(out=TOT.rearrange("p a b -> p (a b)"), in_=ps[5][:, 0:320]), reads=[("ps", 5)], writes=["TOT"])
        UP = A.alloc("UP", [16, 4], F32)
        WW = A.alloc("WW", [16, 4], F32)
        GD = A.alloc("GD", [16, 4], F32)
        P.op('dve', lambda e: e.tensor_tensor(out=UP, in0=G[:, :, 0:4], in1=BC[:, :, 0:4], op=ALU.subtract), reads=["G", "BC"], writes=["UP"])
        P.op('act', lambda e: e.activation(out=UP, in_=UP, func=AF.Exp, bias=-math.log(16.0)), reads=["UP"], writes=["UP"])
        P.op('act', lambda e: e.activation(out=WW, in_=BC[:, :, 0:4], func=AF.Exp), reads=["BC"], writes=["WW"])
        P.op('act', lambda e: e.activation(out=GD, in_=TOT[:, :, 0:4], func=AF.Exp), reads=["TOT"], writes=["GD"])
        GC = A.alloc("GC", [17, 16], F32)
        FG = A.alloc("FG", [16, 16], F32)
        P.op('dve', lambda e: e.memset(GC[:, 0, :], 0.0), writes=["GC"])
        for j in range(16):
            P.op('dve', lambda e, j=j: e.tensor_tensor(out=GC[:, j + 1, :], in0=GC[:, j, :], in1=TOT[:, j, 4:20], op=ALU.add),
                 reads=["GC", "TOT"], writes=["GC"])
        P.op('dve', lambda e: e.tensor_tensor(out=FG, in0=GC[:, 0:16, :], in1=BC[:, :, 4:20], op=ALU.add), reads=["GC", "BC"], writes=["FG"])
        A.free("G", "LF", "TOT")

        hT = A.alloc("hT", [16, T], BF16)
        yT = A.alloc("yT", [16, T], BF16)

        normw = A.alloc("normw", [2048], F32)
        P.dma('sp', lambda e: e.dma_start(out=normw, in_=norm_w.broadcast_to([128, 2048])), writes=["normw"])
        QT = A.alloc("QT", [2, T], BF16)
        KT = A.alloc("KT", [2, T], BF16)
        VA = A.alloc("VA", [8, 520], BF16)
        SO = A.alloc("SO", [8, 512], BF16)
        PRE = A.alloc("PRE", [T + 4], F32)
        ACC = A.alloc("ACC", [T], F32)
        KH = A.alloc("KH", [2, 4], F32)
        CF = A.alloc("CF", [2, 520], F32)
        CB = A.alloc("CB", [2, 520], BF16)
        MT = A.alloc("MT", [128], BF16)
        KU = A.alloc("KU", [256], BF16)
        HH = A.alloc("HH", [512], F32)
        HB = A.alloc("HB", [512], BF16)
        SM = A.alloc("SM", [16], F32)
        ST = A.alloc("ST", [6], F32)

        def convsilu(xT, xk, col0, ct, dst, dkey, dc, halo):
            raise NotImplementedError

        def mlstm_qk(h, own):
            xT, xk = (xTo, "xTo") if own else (xTp, "xTp")
            todo = ([("q", C_MLQ, 0, QT, "QT")] if own else []) + [("k", C_MLK, 8, KT, "KT")]
            for (nm, cb, ctb, dst, dkey) in todo:
                s = load_w([(w_in[:, cb + h * 256: cb + (h + 1) * 256], 256)])
                for dc in range(2):
                    ct = ctb + h * 2 + dc
                    if not own:
                        P.op('dve', lambda e: e.memset(PRE[:, 0:3], 0.0), writes=["PRE"])
                    elif nm == "k":
                        P.op('dve', lambda e, dc=dc: e.tensor_copy(out=PRE[:, 0:3], in_=KH[:, dc, 0:3]), reads=["KH"], writes=["PRE"])
                    else:
                        b = nb()
                        for c in range(16):
                            P.op('pe', lambda e, c=c, dc=dc, b=b: e.matmul(ps[b][:, 0:3], lhsT=wt[:, s, c, dc * 128:(dc + 1) * 128], rhs=xTh[:, c, 0:3],
                                                                           start=(c == 0), stop=(c == 15)),
                                 reads=[("wt", s), "xTh"], writes=[("ps", b)])
                        P.op('dve', lambda e, b=b: e.tensor_copy(out=PRE[:, 0:3], in_=ps[b][:, 0:3]), reads=[("ps", b)], writes=["PRE"])
                    for tb in range(2):
                        proj_fm(s, dc * 128, xT, xk, tb * 512, 512,
                                lambda b, ap, tb=tb: copy('act', PRE[:, 3 + tb * 512: 3 + (tb + 1) * 512], ap, [("ps", b)], ["PRE"]))
                    if nm == "k" and not own:
                        P.op('dve', lambda e, dc=dc: e.tensor_copy(out=KH[:, dc, 0:3], in_=PRE[:, T:T + 3]), reads=["PRE"], writes=["KH"])
                    P.op('dve', lambda e, ct=ct: e.tensor_scalar(out=ACC, in0=PRE[:, 0:T], scalar1=convw[:, ct, 0:1], scalar2=None, op0=ALU.mult),
                         reads=["PRE", "convw"], writes=["ACC"])
                    for j in range(1, 4):
                        P.op('dve', lambda e, ct=ct, j=j: e.scalar_tensor_tensor(out=ACC, in0=PRE[:, j:T + j], scalar=convw[:, ct, j:j + 1], in1=ACC,
                                                                              op0=ALU.mult, op1=ALU.add),
                             reads=["PRE", "convw", "ACC"], writes=["ACC"])
                    P.op('act', lambda e, ct=ct, dc=dc, dst=dst: e.activation(out=dst[:, dc, :], in_=ACC, func=AF.Silu, bias=convb[:, ct:ct + 1]),
                         reads=["ACC", "convb"], writes=[dkey])

        def mlstm_v(h, own):
            xT, xk = (xTo, "xTo") if own else (xTp, "xTp")
            for hf in range(2):
                s = load_w([(w_in[:, C_MLV + h * 512 + hf * 256: C_MLV + h * 512 + (hf + 1) * 256], 256)])
                for tt in range(8):
                    proj_tm(s, 256, xT, xk, tt,
                            lambda b, ap, tt=tt, hf=hf: copy(evac_eng(), VA[:, tt, hf * 256:(hf + 1) * 256], ap, [("ps", b)], ["VA"]))
            if own:
                P.op('dve', lambda e: e.memset(VA[:, :, 512:513], 1.0), writes=["VA"])
            else:
                for tt in range(8):
                    P.op('dve', lambda e, tt=tt: e.tensor_copy(out=VA[:, tt, 512:513], in_=vfl_f[:, 0:1]), reads=["cstf"], writes=["VA"])

        def mlstm_o(h):
            for hf in range(2):
                s = load_w([(w_in[:, C_MLO + h * 512 + hf * 256: C_MLO + h * 512 + (hf + 1) * 256], 256)])
                for tt in range(8):
                    proj_tm(s, 256, xTo, "xTo", tt,
                            lambda b, ap, tt=tt, hf=hf: P.op('act', lambda e: e.activation(out=SO[:, tt, hf * 256:(hf + 1) * 256], in_=ap, func=AF.Sigmoid),
                                                             reads=[("ps", b)], writes=["SO"]))

        def mlstm_scan(h, own):
            for j in range(8):
                gj = j + (8 if own else 0)
                cs = slice(j * 128, (j + 1) * 128)
                if own:
                    for dc in range(2):
                        P.op('pe', lambda e, dc=dc: e.matmul(ps[4][:, 0:128], lhsT=KT[:, dc, cs], rhs=QT[:, dc, cs], start=(dc == 0), stop=(dc == 1)),
                             reads=["KT", "QT"], writes=[("ps", 4)])
                    P.op('dve', lambda e: e.scalar_tensor_tensor(out=MT, in0=ps[4][:, 0:128], scalar=UP[:, gj, h:h + 1], in1=tri_f, op0=ALU.mult, op1=ALU.mult),
                         reads=[("ps", 4), "UP", "cstf"], writes=["MT"])
                    P.op('pe', lambda e: e.matmul(ps[5][:, 0:512], lhsT=MT, rhs=VA[:, j, 0:512], start=True, stop=False),
                         reads=["MT", "VA"], writes=[("ps", 5)])
                    for dc in range(2):
                        P.op('pe', lambda e, dc=dc: e.matmul(ps[5][:, 0:512], lhsT=QT[:, dc, cs], rhs=CB[:, dc, 0:512], start=False, stop=(dc == 1)),
                             reads=["QT", "CB"], writes=[("ps", 5)])
                    P.op('pe', lambda e: e.matmul(ps[6][:, 0:1], lhsT=MT, rhs=VA[:, j, 512:513], start=True, stop=False),
                         reads=["MT", "VA"], writes=[("ps", 6)])
                    for dc in range(2):
                        P.op('pe', lambda e, dc=dc: e.matmul(ps[6][:, 0:1], lhsT=QT[:, dc, cs], rhs=CB[:, dc, 512:513], start=False, stop=(dc == 1)),
                             reads=["QT", "CB"], writes=[("ps", 6)])
                    P.op('dve', lambda e: e.tensor_tensor(out=SM[:, 0:1], in0=ps[6][:, 0:1], in1=WW[:, gj, h:h + 1], op=ALU.mult),
                         reads=[("ps", 6), "WW"], writes=["SM"])
                    P.op('act', lambda e: e.activation(out=SM[:, 1:2], in_=SM[:, 0:1], func=AF.Abs), reads=["SM"], writes=["SM"])
                    P.op('dve', lambda e: e.tensor_scalar(out=SM[:, 1:2], in0=SM[:, 1:2], scalar1=1.0, scalar2=None, op0=ALU.max), reads=["SM"], writes=["SM"])
                    P.op('dve', lambda e: e.reciprocal(out=SM[:, 2:3], in_=SM[:, 1:2]), reads=["SM"], writes=["SM"])
                    P.op('dve', lambda e: e.tensor_tensor(out=SM[:, 3:4], in0=SM[:, 2:3], in1=WW[:, gj, h:h + 1], op=ALU.mult), reads=["SM", "WW"], writes=["SM"])
                    P.op('dve', lambda e: e.tensor_scalar(out=HH, in0=ps[5][:, 0:512], scalar1=SM[:, 3:4], scalar2=None, op0=ALU.mult),
                         reads=[("ps", 5), "SM"], writes=["HH"])
                    P.op('dve', lambda e: e.bn_stats(out=ST, in_=HH), reads=["HH"], writes=["ST"])
                    P.op('dve', lambda e: e.bn_aggr(out=SM[:, 4:6], in_=ST), reads=["ST"], writes=["SM"])
                    P.op('act', lambda e: e.activation(out=SM[:, 6:7], in_=SM[:, 5:6], func=AF.Sqrt, bias=LN_EPS), reads=["SM"], writes=["SM"])
                    P.op('dve', lambda e: e.reciprocal(out=SM[:, 7:8], in_=SM[:, 6:7]), reads=["SM"], writes=["SM"])
                    P.op('dve', lambda e: e.tensor_scalar(out=HH, in0=HH, scalar1=SM[:, 4:5], scalar2=SM[:, 7:8], op0=ALU.subtract, op1=ALU.mult),
                         reads=["HH", "SM"], writes=["HH"])
                    P.op('dve', lambda e: e.tensor_tensor(out=HH, in0=HH, in1=normw[:, h * 512:(h + 1) * 512], op=ALU.mult), reads=["HH", "normw"], writes=["HH"])
                    P.op('dve', lambda e: e.tensor_tensor(out=HB, in0=HH, in1=SO[:, j, :], op=ALU.mult), reads=["HH", "SO"], writes=["HB"])
                    for q4 in range(4):
                        P.op('pe', lambda e, q4=q4: e.transpose(psb[7][:, q4 * 128:(q4 + 1) * 128], HB[:, q4 * 128:(q4 + 1) * 128], ident_b),
                             reads=["HB", "cstb"], writes=[("ps", 7)])
                    copy('act', hT[:, h * 4:(h + 1) * 4, cs], psb[7][:, 0:512].rearrange("p (a b) -> p a b", a=4), [("ps", 7)], ["hT"])
                    if j == 7:
                        continue
                b = nb()
                for dc in range(2):
                    P.op('pe', lambda e, dc=dc, b=b: e.transpose(psb[b][:, dc * 128:(dc + 1) * 128], KT[:, dc, cs], ident_b),
                         reads=["KT", "cstb"], writes=[("ps", b)])
                P.op('dve', lambda e, b=b: e.tensor_scalar(out=KU, in0=psb[b][:, 0:256], scalar1=UP[:, gj, h:h + 1], scalar2=None, op0=ALU.mult),
                     reads=[("ps", b), "UP"], writes=["KU"])
                first = (not own) and j == 0
                for dc in range(2):
                    b = nb()
                    P.op('pe', lambda e, dc=dc, b=b: e.matmul(ps[b][:, 0:512], lhsT=KU[:, dc * 128:(dc + 1) * 128], rhs=VA[:, j, 0:512], start=True, stop=True),
                         reads=["KU", "VA"], writes=[("ps", b)])
                    P.op('pe', lambda e, dc=dc: e.matmul(ps[6][:, 8 + dc:9 + dc], lhsT=KU[:, dc * 128:(dc + 1) * 128], rhs=VA[:, j, 512:513], start=True, stop=True),
                         reads=["KU", "VA"], writes=[("ps", 6)])
                    if first:
                        P.op('dve', lambda e, dc=dc, b=b: e.tensor_scalar(out=CF[:, dc, 0:512], in0=ps[b][:, 0:512], scalar1=GD[:, gj, h:h + 1], scalar2=None, op0=ALU.mult),
                             reads=[("ps", b), "GD"], writes=["CF"])
                        P.op('dve', lambda e, dc=dc: e.tensor_scalar(out=CF[:, dc, 512:513], in0=ps[6][:, 8 + dc:9 + dc], scalar1=GD[:, gj, h:h + 1], scalar2=None, op0=ALU.mult),
                             reads=[("ps", 6), "GD"], writes=["CF"])
                    else:
                        P.op('dve', lambda e, dc=dc: e.tensor_scalar(out=CF[:, dc, 0:513], in0=CF[:, dc, 0:513], scalar1=GD[:, gj, h:h + 1], scalar2=None, op0=ALU.mult),
                             reads=["CF", "GD"], writes=["CF"])
                        P.op('dve', lambda e, dc=dc, b=b: e.scalar_tensor_tensor(out=CF[:, dc, 0:512], in0=ps[b][:, 0:512], scalar=GD[:, gj, h:h + 1], in1=CF[:, dc, 0:512],
                                                                               op0=ALU.mult, op1=ALU.add),
                             reads=[("ps", b), "GD", "CF"], writes=["CF"])
                        P.op('dve', lambda e, dc=dc: e.scalar_tensor_tensor(out=CF[:, dc, 512:513], in0=ps[6][:, 8 + dc:9 + dc], scalar=GD[:, gj, h:h + 1], in1=CF[:, dc, 512:513],
                                                                          op0=ALU.mult, op1=ALU.add),
                             reads=[("ps", 6), "GD", "CF"], writes=["CF"])
                copy('act', CB[:, :, 0:513], CF[:, :, 0:513], ["CF"], ["CB"])

        if stage >= 1:
            for h in range(4):
                mlstm_qk(h, False)
                mlstm_v(h, False)
                mlstm_scan(h, False)
                mlstm_qk(h, True)
                mlstm_v(h, True)
                mlstm_o(h)
                mlstm_scan(h, True)
        A.free("QT", "KT", "VA", "SO", "PRE", "ACC", "KH", "CF", "CB", "MT", "KU", "HH", "HB", "ST", "SM", "normw")

        if stage >= 5:
            cv['buf'] = A.alloc("CVS", [4, D], BF16)
            cv['per'] = 5
        def fox_group(g):
            FK = A.alloc("FK", [2, 2 * T], BF16)
            FV = A.alloc("FV", [16, 256], BF16)
            FQ = A.alloc("FQ", [2, T], BF16)
            PT = A.alloc("PT", [4, 128], BF16)
            FB = A.alloc("FB", [2, 8, 16], F32)
            RC = A.alloc("RC", [128], F32)
            c0 = g * 256
            for (xT, xk, tb0) in ((xTp, "xTp", 0), (xTo, "xTo", 8)):
                s = load_w([(w_in[:, C_FXK + c0:C_FXK + c0 + 256], 256)])
                for hh in range(2):
                    for tb in range(2):
                        proj_fm(s, hh * 128, xT, xk, tb * 512, 512,
                                lambda b, ap, hh=hh, tb=tb, tb0=tb0: copy(evac_eng(), FK[:, hh, tb0 * 128 + tb * 512: tb0 * 128 + (tb + 1) * 512], ap, [("ps", b)], ["FK"]))
                s = load_w([(w_in[:, C_FXV + c0:C_FXV + c0 + 256], 256)])
                for tt in range(8):
                    proj_tm(s, 256, xT, xk, tt,
                            lambda b, ap, tt=tt, tb0=tb0: copy(evac_eng(), FV[:, tb0 + tt, :], ap, [("ps", b)], ["FV"]))
            s = load_w([(w_in[:, C_FXQ + c0:C_FXQ + c0 + 256], 256)])
            for hh in range(2):
                for tb in range(2):
                    proj_fm(s, hh * 128, xTo, "xTo", tb * 512, 512,
                            lambda b, ap, hh=hh, tb=tb: copy(evac_eng(), FQ[:, hh, tb * 512:(tb + 1) * 512], ap, [("ps", b)], ["FQ"]))
            scale = 128.0 ** -0.5
            LOOK = 2
            for hh in range(2):
                h = g * 2 + hh
                for jq in range(8):
                    gq = 8 + jq
                    P.op('dve', lambda e: e.tensor_scalar(out=FB[:, hh, jq, 0:gq + 1], in0=FG[:, 0:gq + 1, h], scalar1=-1.0, scalar2=GC[:, gq, h:h + 1], op0=ALU.mult, op1=ALU.add),
                         reads=["FG", "GC"], writes=[("FB", hh)])
                for jq in range(8):
                    gq = 8 + jq
                    qs = slice(jq * 128, (jq + 1) * 128)
                    acc = 4 + (jq % 2)
                    n = gq + 1

                    def stage_a(jk):
                        b = nb()
                        sl = jk % 4
                        P.op('pe', lambda e: e.matmul(ps[b][:, 0:128], lhsT=FK[:, hh, jk * 128:(jk + 1) * 128], rhs=FQ[:, hh, qs], start=True, stop=True),
                             reads=["FK", "FQ"], writes=[("ps", b)])
                        P.op('act', lambda e: e.activation(out=PT[:, sl, :], in_=ps[b][:, 0:128], func=AF.Exp, bias=FB[:, hh, jq, jk:jk + 1], scale=scale),
                             reads=[("ps", b), ("FB", hh)], writes=[("PT", sl)])
                        if jk == gq:
                            P.op('dve', lambda e: e.tensor_tensor(out=PT[:, sl, :], in0=PT[:, sl, :], in1=tri_b, op=ALU.mult),
                                 reads=[("PT", sl), "cstb"], writes=[("PT", sl)])

                    def stage_b(jk):
                        sl = jk % 4
                        P.op('pe', lambda e: e.matmul(ps[acc][:, 0:128], lhsT=FV[:, jk, hh * 128:(hh + 1) * 128], rhs=PT[:, sl, :], start=(jk == 0), stop=(jk == gq)),
                             reads=["FV", ("PT", sl)], writes=[("ps", acc)])
                        P.op('pe', lambda e: e.matmul(ps[acc + 2][:, 0:128], lhsT=(vfl_b if jk < 8 else ones_b), rhs=PT[:, sl, :], start=(jk == 0), stop=(jk == gq)),
                             reads=["cstb", ("PT", sl)], writes=[("ps", acc + 2)])

                    for i in range(n + LOOK):
                        if i < n:
                            stage_a(i)
                        if i >= LOOK:
                            stage_b(i - LOOK)
                    P.op('dve', lambda e: e.reciprocal(out=RC, in_=ps[acc + 2][:, 0:128]), reads=[("ps", acc + 2)], writes=["RC"])
                    P.op('dve', lambda e: e.tensor_tensor(out=yT[:, h, qs], in0=ps[acc][:, 0:128], in1=RC, op=ALU.mult), reads=[("ps", acc), "RC"], writes=["yT"])
            A.free("FK", "FV", "FQ", "PT", "FB", "RC")

        if stage >= 2:
            for g in range(8):
                fox_group(g)
        A.free("xTp", "xTh", "GC", "FG", "BC", "UP", "WW", "GD")

        if stage in (1, 2):
            src_, k_ = (hT, "hT") if stage == 1 else (yT, "yT")
            ev = P.dma('pool', lambda e: e.dma_start(out=out.rearrange("(c p) t -> p c t", p=128), in_=src_), reads=[k_])
            P.need('pool', ev)
            P.emit()
            return nc
        cv['per'] = 1
        mT = A.alloc("mT", [16, T], BF16)
        SG = A.alloc("SG", [2, T], BF16)
        T1 = A.alloc("T1", [512], F32)
        T2 = A.alloc("T2", [512], F32)
        if stage >= 3:
            for c in range(16):
                s = load_w([(w_in[:, C_GML + c * 128:C_GML + (c + 1) * 128], 128), (w_in[:, C_GFX + c * 128:C_GFX + (c + 1) * 128], 128)])
                for which in range(2):
                    for tb in range(2):
                        proj_fm(s, which * 128, xTo, "xTo", tb * 512, 512,
                                lambda b, ap, which=which, tb=tb: P.op('act', lambda e: e.activation(out=SG[:, which, tb * 512:(tb + 1) * 512], in_=ap, func=AF.Sigmoid),
                                                                       reads=[("ps", b)], writes=["SG"]))
                s = load_w([(w_bml[:, c * 128:(c + 1) * 128], 128), (w_bfx[:, c * 128:(c + 1) * 128], 128)])
                for tb in range(2):
                    proj_fm(s, 0, hT, "hT", tb * 512, 512,
                            lambda b, ap, tb=tb: P.op('dve', lambda e: e.tensor_tensor(out=T1, in0=ap, in1=SG[:, 0, tb * 512:(tb + 1) * 512], op=ALU.mult),
                                                      reads=[("ps", b), "SG"], writes=["T1"]))
                    proj_fm(s, 128, yT, "yT", tb * 512, 512,
                            lambda b, ap, tb=tb: P.op('dve', lambda e: e.tensor_tensor(out=T2, in0=ap, in1=SG[:, 1, tb * 512:(tb + 1) * 512], op=ALU.mult),
                                                      reads=[("ps", b), "SG"], writes=["T2"]))
                    P.op('dve', lambda e, c=c, tb=tb: e.tensor_tensor(out=mT[:, c, tb * 512:(tb + 1) * 512], in0=T1, in1=T2, op=ALU.add),
                         reads=["T1", "T2"], writes=["mT"])
        A.free("SG", "T1", "T2", "xTo")
        if stage >= 3:
            A.free("hT", "yT")

        X1 = A.alloc("X1", [NT, D], F32)
        X1T = A.alloc("X1T", [16, T], BF16)
        ST4 = A.alloc("ST4", [4, 6], F32)
        SM2 = A.alloc("SM2", [8], F32)

        def layer_norm_tiles(li, make_T):
            lnw = A.alloc("lnw", [D], F32)
            lnb = A.alloc("lnb", [D], F32)
            P.dma('sp', lambda e: e.dma_start(out=lnw, in_=ln_w[li:li + 1, :].broadcast_to([128, D])), writes=["lnw"])
            P.dma('sp', lambda e: e.dma_start(out=lnb, in_=ln_b[li:li + 1, :].broadcast_to([128, D])), writes=["lnb"])
            for tt in range(NT):
                xk = ("X1", tt)
                xr = X1[:, tt, :]
                for q4 in range(4):
                    P.op('dve', lambda e, q4=q4: e.bn_stats(out=ST4[:, q4, :], in_=xr[:, q4 * 512:(q4 + 1) * 512]), reads=[xk], writes=["ST4"])
                P.op('dve', lambda e: e.bn_aggr(out=SM2[:, 0:2], in_=ST4.rearrange("p a b -> p (a b)")), reads=["ST4"], writes=["SM2"])
                P.op('act', lambda e: e.activation(out=SM2[:, 2:3], in_=SM2[:, 1:2], func=AF.Sqrt, bias=LN_EPS), reads=["SM2"], writes=["SM2"])
                P.op('dve', lambda e: e.reciprocal(out=SM2[:, 3:4], in_=SM2[:, 2:3]), reads=["SM2"], writes=["SM2"])
                P.op('dve', lambda e: e.tensor_scalar(out=xr, in0=xr, scalar1=SM2[:, 0:1], scalar2=SM2[:, 3:4], op0=ALU.subtract, op1=ALU.mult),
                     reads=[xk, "SM2"], writes=[xk])
                P.op('dve', lambda e: e.tensor_tensor(out=xr, in0=xr, in1=lnw, op=ALU.mult), reads=[xk, "lnw"], writes=[xk])
                P.op('dve', lambda e: e.tensor_tensor(out=xr, in0=xr, in1=lnb, op=ALU.add), reads=[xk, "lnb"], writes=[xk])
                if make_T:
                    for g in range(4):
                        b = nb()
                        for j in range(4):
                            c = g * 4 + j
                            P.op('pe', lambda e, c=c, j=j, b=b: e.transpose(ps[b][:, j * 128:(j + 1) * 128], xr[:, c * 128:(c + 1) * 128], ident_f),
                                 reads=[xk, "cstf"], writes=[("ps", b)])
                        copy('act', X1T[:, 4 * g:4 * g + 4, tt * 128:(tt + 1) * 128], ps[b][:].rearrange("p (a b) -> p a b", a=4), [("ps", b)], ["X1T"])
            A.free("lnw", "lnb")

        def out_proj_residual(W, inT, inkey, first):
            if first:
                for tt in range(NT):
                    P.dma('sp', lambda e, tt=tt: e.dma_start(out=X1[:, tt, :], in_=x_own[tt * 128:(tt + 1) * 128, :]), writes=[("X1", tt)])
            for cb in range(8):
                s = load_w([(W[:, cb * 256:(cb + 1) * 256], 256)])
                for tt in range(NT):
                    proj_tm(s, 256, inT, inkey, tt,
                            lambda b, ap, tt=tt, cb=cb: P.op('dve', lambda e: e.scalar_tensor_tensor(
                                out=X1[:, tt, cb * 256:(cb + 1) * 256], in0=X1[:, tt, cb * 256:(cb + 1) * 256], scalar=DN_ALPHA, in1=ap, op0=ALU.mult, op1=ALU.add),
                                reads=[("ps", b), ("X1", tt)], writes=[("X1", tt)]))

        if stage >= 3:
            out_proj_residual(w_out, mT, "mT", True)
            layer_norm_tiles(0, True)
        A.free("mT")

        if stage >= 5:
            conv_chunks(NCH)
            A.free("CVS")
            cv['buf'] = None
        if stage >= 4:
            QXT = A.alloc("QXT", [16, T], BF16)
            OXT = A.alloc("OXT", [16, T], BF16)
            memT = A.alloc("memT", [16, 256], BF16)
            load_xT(mem, 256, memT, "memT")
            KXT = A.alloc("KXT", [16, 256], BF16)
            VX = A.alloc("VX", [2, D], BF16)
            for cb in range(8):
                s = load_w([(w_xkv[:, cb * 256:(cb + 1) * 256], 256)])
                for hh in range(2):
                    proj_fm(s, hh * 128, memT, "memT", 0, 256,
                            lambda b, ap, cb=cb, hh=hh: copy(evac_eng(), KXT[:, cb * 2 + hh, :], ap, [("ps", b)], ["KXT"]))
            for cb in range(8):
                s = load_w([(w_xkv[:, D + cb * 256:D + (cb + 1) * 256], 256)])
                for mt in range(2):
                    proj_tm(s, 256, memT, "memT", mt,
                            lambda b, ap, cb=cb, mt=mt: copy(evac_eng(), VX[:, mt, cb * 256:(cb + 1) * 256], ap, [("ps", b)], ["VX"]))
            A.free("memT")
            for cb in range(8):
                s = load_w([(w_xq[:, cb * 256:(cb + 1) * 256], 256)])
                for hh in range(2):
                    for tb in range(2):
                        proj_fm(s, hh * 128, X1T, "X1T", tb * 512, 512,
                                lambda b, ap, cb=cb, hh=hh, tb=tb: copy(evac_eng(), QXT[:, cb * 2 + hh, tb * 512:(tb + 1) * 512], ap, [("ps", b)], ["QXT"]))
            PX = A.alloc("PX", [2, 512], BF16)
            RX = A.alloc("RX", [512], F32)
            xscale = 512.0 ** -0.5
            for h in range(4):
                for tb in range(2):
                    ts_ = slice(tb * 512, (tb + 1) * 512)
                    for mt in range(2):
                        b = nb()
                        for dc in range(4):
                            P.op('pe', lambda e, dc=dc, b=b, mt=mt: e.matmul(ps[b][:, 0:512], lhsT=KXT[:, h * 4 + dc, mt * 128:(mt + 1) * 128], rhs=QXT[:, h * 4 + dc, ts_],
                                                                             start=(dc == 0), stop=(dc == 3)),
                                 reads=["KXT", "QXT"], writes=[("ps", b)])
                        P.op('act', lambda e, b=b, mt=mt: e.activation(out=PX[:, mt, :], in_=ps[b][:, 0:512], func=AF.Exp, scale=xscale),
                             reads=[("ps", b)], writes=[("PX", mt)])
                    for mt in range(2):
                        P.op('pe', lambda e, mt=mt: e.matmul(ps[4][:, 0:512], lhsT=ones_b, rhs=PX[:, mt, :], start=(mt == 0), stop=(mt == 1)),
                             reads=["cstb", ("PX", mt)], writes=[("ps", 4)])
                    P.op('dve', lambda e: e.reciprocal(out=RX, in_=ps[4][:, 0:512]), reads=[("ps", 4)], writes=["RX"])
                    for dc in range(4):
                        b = 5 + (dc % 2)
                        for mt in range(2):
                            P.op('pe', lambda e, dc=dc, mt=mt, b=b: e.matmul(ps[b][:, 0:512], lhsT=VX[:, mt, (h * 4 + dc) * 128:(h * 4 + dc + 1) * 128], rhs=PX[:, mt, :],
                                                                             start=(mt == 0), stop=(mt == 1)),
                                 reads=["VX", ("PX", 0), ("PX", 1)], writes=[("ps", b)])
                        P.op('dve', lambda e, dc=dc, b=b: e.tensor_tensor(out=OXT[:, h * 4 + dc, ts_], in0=ps[b][:, 0:512], in1=RX, op=ALU.mult),
                             reads=[("ps", b), "RX"], writes=["OXT"])
            A.free("KXT", "VX", "QXT", "PX", "RX")
            out_proj_residual(w_xo, OXT, "OXT", False)
            A.free("OXT")
            layer_norm_tiles(1, True)

        if stage >= 5:
            SKT = A.alloc("SKT", [16, 128], BF16)
            skt = A.alloc("skt", [2, 128], F32)
            for j in range(16):
                P.dma('sp', lambda e, j=j: e.dma_start(out=skt[:, j % 2, :], in_=subk[j * 128:(j + 1) * 128, :]), writes=[("skt", j % 2)])
                b = nb()
                P.op('pe', lambda e, b=b: e.transpose(ps[b][:, 0:128], skt[:, j % 2, :], ident_f), reads=[("skt", j % 2), "cstf"], writes=[("ps", b)])
                copy('dve', SKT[:, j, :], ps[b][:, 0:128], [("ps", b)], ["SKT"])
            A.free("skt")
            QPT = A.alloc("QPT", [16, T], BF16)
            for cb in range(8):
                s = load_w([(w_pq[:, cb * 256:(cb + 1) * 256], 256)])
                for hh in range(2):
                    for tb in range(2):
                        proj_fm(s, hh * 128, X1T, "X1T", tb * 512, 512,
                                lambda b, ap, cb=cb, hh=hh, tb=tb: copy(evac_eng(), QPT[:, cb * 2 + hh, tb * 512:(tb + 1) * 512], ap, [("ps", b)], ["QPT"]))
            A.free("X1T", "wt")
            SC = A.alloc("SC", [16, 128], F32)
            TS = A.alloc("TS", [16, 16], F32)
            TI = A.alloc("TI", [16, 16], U32)
            TIF = A.alloc("TIF", [16, 16], F32)
            CS = A.alloc("CS", [4, 256], F32)
            CI = A.alloc("CI", [4, 256], F32)
            BS = A.alloc("BS", [8, 16], F32)
            BP = A.alloc("BP", [4, 16], U32)
            BPF = A.alloc("BPF", [4, 16], F32)
            IDF = A.alloc("IDF", [128], F32)
            IDUa = A.alloc("IDUa", [NT, 128], U32)
            GWa = A.alloc("GWa", [NT, 128], F32)
            JK = A.alloc("JK", [4, 256], BF16)
            SM3 = A.alloc("SM3", [4, 4], F32)
            AAs = [A.alloc("AA%d" % i, [128], F32) for i in range(2)]
            HAs = [A.alloc("HA%d" % i, [128], F32) for i in range(2)]
            JK2 = A.alloc("JK2", [D], BF16)
            DG = A.alloc("DG", [4, 128], BF16)
            NSL = 7
            UBs = [A.alloc("UB%d" % i, [2 * D], BF16) for i in range(NSL)]

            NI = 4

            def select(tt):
                tsl = slice(tt * 128, (tt + 1) * 128)
                GW = GWa[:, tt, :].rearrange("p (a b) -> p a b", a=8)
                for q4 in range(4):
                    for jj in range(4):
                        j = q4 * 4 + jj
                        P.op('pe', lambda e: e.matmul(ps[4 + q4][:, jj * 128:(jj + 1) * 128], lhsT=QPT[:, j, tsl], rhs=SKT[:, j, :], start=True, stop=True),
                             reads=["QPT", "SKT"], writes=[("ps", 4 + q4)])
                    copy('act', SC[:, q4 * 4:(q4 + 1) * 4, :], ps[4 + q4][:].rearrange("p (a b) -> p a b", a=4), [("ps", 4 + q4)], [("SC", q4 * 4 + i) for i in range(4)])
                yield
                for j in range(16):
                    P.op('dve', lambda e: e.max(out=TS[:, j, 0:8], in_=SC[:, j, :]), reads=[("SC", j)], writes=[("TS", j)])
                    if j % 2:
                        yield
                for j in range(16):
                    P.op('dve', lambda e: e.max_index(out=TI[:, j, 0:8], in_max=TS[:, j, 0:8], in_values=SC[:, j, :]), reads=[("SC", j), ("TS", j)], writes=[("TI", j)])
                    if j % 2:
                        yield
                for j in range(16):
                    P.op('dve', lambda e: e.match_replace(out=SC[:, j, :], in_to_replace=TS[:, j, 0:8], in_values=SC[:, j, :], imm_value=-1e30), reads=[("SC", j), ("TS", j)], writes=[("SC", j)])
                    if j % 2:
                        yield
                for j in range(16):
                    P.op('dve', lambda e: e.max(out=TS[:, j, 8:16], in_=SC[:, j, :]), reads=[("SC", j)], writes=[("TS", j)])
                    if j % 2:
                        yield
                for j in range(16):
                    P.op('dve', lambda e: e.max_index(out=TI[:, j, 8:16], in_max=TS[:, j, 8:16], in_values=SC[:, j, :]), reads=[("SC", j), ("TS", j)], writes=[("TI", j)])
                    if j % 2:
                        yield
                P.op('dve', lambda e: e.tensor_copy(out=TIF, in_=TI), reads=[("TI", j) for j in range(16)], writes=["TIF"])
                yield
                TS4 = TS.rearrange("p (h two) k -> p h two k", two=2)
                TI4 = TIF.rearrange("p (h two) k -> p h two k", two=2)
                for h0 in range(0, 8, NI):
                    hs = [(h0 + i, i) for i in range(NI)]
                    for h, i in hs:
                        csv = CS[:, i, :].rearrange("p (a b) -> p a b", a=16)
                        P.op('dve', lambda e: e.tensor_tensor(out=csv, in0=TS4[:, h, 0, :].unsqueeze(2).to_broadcast([128, 16, 16]),
                                                              in1=TS4[:, h, 1, :].unsqueeze(1).to_broadcast([128, 16, 16]), op=ALU.add),
                             reads=[("TS", 2 * h), ("TS", 2 * h + 1)], writes=[("CS", i)])
                    yield
                    for h, i in hs:
                        civ = CI[:, i, :].rearrange("p (a b) -> p a b", a=16)
                        P.op('dve', lambda e: e.scalar_tensor_tensor(out=civ, in0=TI4[:, h, 0, :].unsqueeze(2).to_broadcast([128, 16, 16]), scalar=128.0,
                                                                     in1=TI4[:, h, 1, :].unsqueeze(1).to_broadcast([128, 16, 16]), op0=ALU.mult, op1=ALU.add),
                             reads=["TIF"], writes=[("CI", i)])
                    yield
                    for h, i in hs:
                        P.op('dve', lambda e: e.max(out=BS[:, h, 0:8], in_=CS[:, i, :]), reads=[("CS", i)], writes=[("BS", h)])
                    yield
                    for h, i in hs:
                        P.op('dve', lambda e: e.max_index(out=BP[:, i, 0:8], in_max=BS[:, h, 0:8], in_values=CS[:, i, :]), reads=[("CS", i), ("BS", h)], writes=[("BP", i)])
                    yield
                    for h, i in hs:
                        P.op('dve', lambda e: e.match_replace(out=CS[:, i, :], in_to_replace=BS[:, h, 0:8], in_values=CS[:, i, :], imm_value=-1e30), reads=[("CS", i), ("BS", h)], writes=[("CS", i)])
                    yield
                    for h, i in hs:
                        P.op('dve', lambda e: e.max(out=BS[:, h, 8:16], in_=CS[:, i, :]), reads=[("CS", i)], writes=[("BS", h)])
                    yield
                    for h, i in hs:
                        P.op('dve', lambda e: e.max_index(out=BP[:, i, 8:16], in_max=BS[:, h, 8:16], in_values=CS[:, i, :]), reads=[("CS", i), ("BS", h)], writes=[("BP", i)])
                    yield
                    for h, i in hs:
                        P.op('dve', lambda e: e.tensor_copy(out=BPF[:, i, :], in_=BP[:, i, :]), reads=[("BP", i)], writes=[("BPF", i)])
                    yield
                    for k in range(16):
                        for h, i in hs:
                            P.op('dve', lambda e: e.scalar_tensor_tensor(out=JK[:, i, :], in0=iota_f, scalar=BPF[:, i, k:k + 1], in1=CI[:, i, :], op0=ALU.is_equal, op1=ALU.mult,
                                                                         accum_out=IDF[:, h * 16 + k:h * 16 + k + 1]),
                                 reads=["cstf", ("BPF", i), ("CI", i)], writes=[("IDF", h * 16 + k)])
                            if i % 2:
                                yield
                    for h, i in hs:
                        P.op('dve', lambda e: e.tensor_scalar(out=SM3[:, i, 0:1], in0=BS[:, h, 0:1], scalar1=-1.0, scalar2=None, op0=ALU.mult), reads=[("BS", h)], writes=[("SM3", i)])
                    for h, i in hs:
                        P.op('act', lambda e: e.activation(out=GW[:, h, :], in_=BS[:, h, :], func=AF.Exp, bias=SM3[:, i, 0:1], accum_out=SM3[:, i, 1:2]),
                             reads=[("BS", h), ("SM3", i)], writes=[("GWa", tt, h), ("SM3", i)])
                    yield
                    for h, i in hs:
                        P.op('dve', lambda e: e.reciprocal(out=SM3[:, i, 2:3], in_=SM3[:, i, 1:2]), reads=[("SM3", i)], writes=[("SM3", i)])
                    for h, i in hs:
                        P.op('dve', lambda e: e.tensor_scalar(out=GW[:, h, :], in0=GW[:, h, :], scalar1=SM3[:, i, 2:3], scalar2=None, op0=ALU.mult),
                             reads=[("GWa", tt, h), ("SM3", i)], writes=[("GWa", tt, h)])
                    yield
                P.op('dve', lambda e: e.tensor_copy(out=IDUa[:, tt, :], in_=IDF), reads=[("IDF", c) for c in range(128)], writes=[("IDUa", tt)])

            P.op('dve', lambda e: e.memset(JK, 0.0), writes=["JK"])
            P.op('dve', lambda e: e.memset(JK2, 0.0), writes=["JK2"])
            for _ in select(0):
                pass
            conv_chunks(NCH)
            for kb in ("U16", "V16"):
                for ev in P.all_events(kb):
                    P.need('pool', ev)
            gsl = [0]
            GRP = 1

            def v_finish(tt):
                xr = X1[:, tt, :]
                for q4 in range(4):
                    P.op('dve', lambda e: e.scalar_tensor_tensor(out=xr[:, q4 * 512:(q4 + 1) * 512], in0=xr[:, q4 * 512:(q4 + 1) * 512], scalar=DN_ALPHA, in1=ps[q4][:, 0:512],
                                                                 op0=ALU.mult, op1=ALU.add), reads=[("X1", tt), ("ps", q4)], writes=[("X1", tt)])

            for tt in range(NT):
                gen = select(tt + 1) if tt + 1 < NT else iter(())
                for g0 in range(0, 128, GRP):
                    par = (g0 // GRP) % 2
                    AAp, HAp = AAs[par], HAs[par]
                    slots = []
                    for k in range(g0, g0 + GRP):
                        gsl[0] = (gsl[0] + 1) % NSL
                        sl = gsl[0]
                        slots.append(sl)
                        P.dma('pool', lambda e: e.indirect_dma_start(out=UBs[sl], out_offset=None, in_=uv16,
                                                                     in_offset=bass.IndirectOffsetOnAxis(ap=IDUa[:, tt, k:k + 1], axis=0)),
                              reads=[("IDUa", tt)], writes=["UB%d" % sl])
                        P.op('dve', lambda e: e.scalar_tensor_tensor(out=JK2, in0=UBs[sl][:, 0:D], scalar=1.0, in1=X1[:, tt, :], op0=ALU.mult, op1=ALU.mult, accum_out=AAp[:, k:k + 1]),
                             reads=["UB%d" % sl, ("X1", tt)], writes=[("AA%d" % par, k)])
                        next(gen, None)
                        next(gen, None)
                    P.op('act', lambda e: e.activation(out=HAp[:, g0:g0 + GRP], in_=AAp[:, g0:g0 + GRP], func=AF.Gelu), reads=[("AA%d" % par, kk) for kk in range(g0, g0 + GRP)], writes=["HA%d" % par])
                    P.op('dve', lambda e: e.tensor_tensor(out=HAp[:, g0:g0 + GRP], in0=HAp[:, g0:g0 + GRP], in1=GWa[:, tt, g0:g0 + GRP], op=ALU.mult),
                         reads=["HA%d" % par] + [("GWa", tt, hh_) for hh_ in range(8)], writes=["HA%d" % par])
                    for k, sl in zip(range(g0, g0 + GRP), slots):
                        ds = k % 4
                        P.op('act', lambda e: e.activation(out=DG[:, ds, :], in_=ident_b, func=AF.Copy, scale=HAp[:, k:k + 1]),
                             reads=["cstb", "HA%d" % par], writes=[("DG", ds)])
                        for q4 in range(4):
                            P.op('pe', lambda e: e.matmul(ps[q4][:, 0:512], lhsT=DG[:, ds, :], rhs=UBs[sl][:, D + q4 * 512:D + (q4 + 1) * 512], start=(k == 0), stop=(k == 127)),
                                 reads=[("DG", ds), "UB%d" % sl], writes=[("ps", q4)])
                for _ in gen:
                    pass
                v_finish(tt)
            A.free(*["UB%d" % i for i in range(NSL)])
            A.free("JK2", "DG", "AA0", "AA1", "HA0", "HA1", "QPT", "SC", "TS", "TI", "TIF", "CS", "CI", "BS", "BP", "BPF", "IDF", "JK", "SKT")
            layer_norm_tiles(2, False)

        evs = []
        if stage in (1, 2):
            src_, k_ = (hT, "hT") if stage == 1 else (yT, "yT")
            evs.append(P.dma('pool', lambda e: e.dma_start(out=out.rearrange("(c p) t -> p c t", p=128), in_=src_), reads=[k_]))
        for tt in range(NT if stage >= 3 else 0):
            evs.append(P.dma('sp', lambda e, tt=tt: e.dma_start(out=out[tt * 128:(tt + 1) * 128, :], in_=X1[:, tt, :]), reads=[("X1", tt)]))
        for ev in evs:
            P.need('sp', ev)
        P.emit()
    return nc


def make_in_maps(inp):
    x = np.asarray(inp["x"], np.float32)
    cst0 = np.concatenate([np.eye(128, dtype=np.float32), np.triu(np.ones((128, 128), np.float32)),
                           np.ones((128, 128), np.float32)], 1)
    shared = {
        "w_in": np.ascontiguousarray(inp["w_in"][0]), "conv_w": np.ascontiguousarray(np.asarray(inp["ml_conv_w"][0]).reshape(4, 16, 128).transpose(2, 1, 0).reshape(128, 64)),
        "conv_b": np.ascontiguousarray(np.asarray(inp["ml_conv_b"][0]).reshape(16, 128).T), "gate_b": np.ascontiguousarray(inp["ml_gate_b"][0].reshape(1, 8)),
        "norm_w": np.ascontiguousarray(inp["ml_norm_w"]), "fox_fb": np.ascontiguousarray(inp["fox_f_b"]),
        "w_bml": np.ascontiguousarray(inp["w_branch_ml"][0]), "w_bfx": np.ascontiguousarray(inp["w_branch_fox"][0]),
        "w_out": np.ascontiguousarray(inp["w_out"][0]), "w_xq": np.ascontiguousarray(inp["w_xq"][0]),
        "w_xkv": np.ascontiguousarray(inp["w_xkv"][0]), "w_xo": np.ascontiguousarray(inp["w_xo"][0]),
        "w_pq": np.ascontiguousarray(inp["peer_wq"][0]), "subk": np.ascontiguousarray(inp["peer_sub_keys"][0].reshape(2048, 128)),
        "peer_u": np.ascontiguousarray(inp["peer_u"][0]), "peer_v": np.ascontiguousarray(inp["peer_v"][0]),
        "ln_w": np.ascontiguousarray(inp["ln_w"][0]), "ln_b": np.ascontiguousarray(inp["ln_b"][0]),
    }
    shared = {k: np.asarray(v, np.float32) for k, v in shared.items()}
    maps = []
    for c in range(8):
        b, half = c // 2, c % 2
        m = dict(shared)
        m["x_own"] = np.ascontiguousarray(x[b, half * T:(half + 1) * T])
        m["x_prev"] = np.ascontiguousarray(x[b, 0:T]) if half == 1 else np.zeros((T, D), np.float32)
        m["mem"] = np.ascontiguousarray(np.asarray(inp["mem"], np.float32)[b])
        m["cst"] = np.concatenate([cst0, np.full((128, 128), float(half), np.float32),
                                   np.tile(np.arange(256, dtype=np.float32)[None, :], (128, 1))], 1)
        maps.append(m)
    return maps


def kernel(**inputs):
    nc = build_nc()
    maps = make_in_maps(inputs)
    res = run_bass_kernel_spmd(nc, maps, core_ids=list(range(8)))
    outp = np.zeros((4, 2048, D), np.float32)
    for c in range(8):
        b, half = c // 2, c % 2
        outp[b, half * T:(half + 1) * T] = res.results[c]["out"]
    return outp
```

```python
import math
import numpy as np
from contextlib import ExitStack
import concourse.bass as bass
import concourse.mybir as mybir
from concourse.bass_utils import run_bass_kernel_spmd

F32 = mybir.dt.float32
BF16 = mybir.dt.bfloat16
U32 = mybir.dt.uint32
U8 = mybir.dt.uint8
AF = mybir.ActivationFunctionType
ALU = mybir.AluOpType
AX = mybir.AxisListType
DTSIZE = {F32: 4, BF16: 2, U32: 4, U8: 1}

ENG = ['pe', 'act', 'dve', 'pool', 'sp']
NDS = 32
SAME_ENGINE_SYNC = ('act', 'dve', 'pool')

D = 2048
T = 1024
NT = 8
LN_EPS = 1e-5
DN_ALPHA = 2 ** 0.25
C_MLQ, C_MLK, C_MLV, C_MLO, C_MLI, C_MLF = 0, 1024, 2048, 4096, 6144, 6148
C_FXQ, C_FXK, C_FXV, C_FXF, C_GML, C_GFX = 6152, 8200, 10248, 12296, 12312, 14360
IN_W = 16408


def _base(k):
    return k[0] if isinstance(k, tuple) else k


class _Rec:
    def __init__(self):
        self.calls = []

    def __getattr__(self, name):
        def f(*a, **k):
            self.calls.append((name, a, k))
            return self
        return f


class Prog:
    def __init__(self, nc, es):
        self.nc = nc
        self.q = {e: [] for e in ENG}
        self.sem = {e: es.enter_context(nc.semaphore("s_" + e)) for e in ENG}
        self.cnt = {e: 0 for e in ENG}
        self.dsem = [es.enter_context(nc.semaphore("d%d" % i)) for i in range(NDS)]
        self.dcnt = [0] * NDS
        self.dnext = 0
        self.dnx = [0, 0]
        self.waited = {e: {} for e in ENG}
        self.hist = {}
        self.inherit = {}

    def need(self, eng, ev):
        sem, val = ev
        if sem is self.sem.get(eng) and eng not in SAME_ENGINE_SYNC:
            return
        w = self.waited[eng]
        if w.get(sem.name, 0) >= val:
            return
        w[sem.name] = val
        self.q[eng].append(('wait', sem, val))

    def _deps(self, eng, reads, writes):
        evs = []
        for k in reads:
            h = self.hist.get(k)
            if h and h['w']:
                evs.append(h['w'])
        for k in writes:
            h = self.hist.get(k)
            if h:
                if h['w']:
                    evs.append(h['w'])
                evs.extend(h['r'].values())
            evs.extend(self.inherit.get(_base(k), ()))
        for ev in evs:
            self.need(eng, ev)

    def _record(self, me, reads, writes):
        for k in reads:
            h = self.hist.setdefault(k, {'w': None, 'r': {}})
            h['r'][me[0].name] = me
        for k in writes:
            self.hist[k] = {'w': me, 'r': {}}

    def op(self, eng, fn, reads=(), writes=()):
        self._deps(eng, reads, writes)
        self.cnt[eng] += 1
        me = (self.sem[eng], self.cnt[eng])
        r = _Rec()
        fn(r)
        assert len(r.calls) == 1
        self.q[eng].append(('op', r.calls[0], self.sem[eng], 1))
        self._record(me, reads, writes)
        return me

    def dma(self, eng, fn, reads=(), writes=()):
        half = NDS // 2
        pi = 0 if eng == 'sp' else 1
        i = pi * half + self.dnx[pi]
        self.dnx[pi] = (self.dnx[pi] + 1) % half
        self._deps(eng, reads, writes)
        self.need(eng, (self.dsem[i], self.dcnt[i]))
        self.dcnt[i] += 16
        me = (self.dsem[i], self.dcnt[i])
        r = _Rec()
        fn(r)
        assert len(r.calls) == 1
        self.q[eng].append(('op', r.calls[0], self.dsem[i], 16))
        self._record(me, reads, writes)
        return me

    def all_events(self, keybase):
        evs = []
        for k, h in self.hist.items():
            if _base(k) == keybase:
                if h['w']:
                    evs.append(h['w'])
                evs.extend(h['r'].values())
        evs.extend(self.inherit.get(keybase, ()))
        return evs

    def emit(self):
        nc = self.nc
        q = self.q

        def replay(engobj, e):
            for it in q[e]:
                if it[0] == 'wait':
                    engobj.wait_ge(it[1], it[2])
                else:
                    name, a, k = it[1]
                    ins = getattr(engobj, name)(*a, **k)
                    ins.then_inc(it[2], it[3])

        with nc.Block() as block:
            @block.tensor
            def _(pe):
                replay(pe, 'pe')

            @block.scalar
            def _(act):
                replay(act, 'act')

            @block.vector
            def _(dve):
                replay(dve, 'dve')

            @block.gpsimd
            def _(pool):
                replay(pool, 'pool')

            @block.sync
            def _(sp):
                replay(sp, 'sp')


class Arena:
    def __init__(self, nc, es, prog, nbytes):
        self.prog = prog
        self.cap = nbytes
        self.t = es.enter_context(nc.sbuf_tensor("arena", [128, nbytes], U8))
        self.live = {}
        self.dead = []

    def alloc(self, key, shape, dtype):
        size = (int(np.prod(shape)) * DTSIZE[dtype] + 63) // 64 * 64
        ivs = sorted(self.live.values())
        lo = 0
        for (l2, h2) in ivs:
            if lo + size <= l2:
                break
            lo = max(lo, h2)
        hi = lo + size
        assert hi <= self.cap, ("arena overflow", key, size, sorted((v, k) for k, v in self.live.items()))
        assert key not in self.live
        self.live[key] = (lo, hi)
        best = {}
        nd = []
        for (l2, h2, evs) in self.dead:
            if l2 < hi and lo < h2:
                for (s, v) in evs:
                    if best.get(s.name, (None, 0))[1] < v:
                        best[s.name] = (s, v)
            nd.append((l2, h2, evs))
        self.dead = nd
        self.prog.inherit[key] = list(best.values())
        for k in [k for k in self.prog.hist if _base(k) == key]:
            del self.prog.hist[k]
        ap = self.t[:, lo:lo + int(np.prod(shape)) * DTSIZE[dtype]].bitcast(dtype)
        if len(shape) == 2:
            ap = ap.rearrange("p (a b) -> p a b", a=shape[0], b=shape[1])
        elif len(shape) == 3:
            ap = ap.rearrange("p (a b c) -> p a b c", a=shape[0], b=shape[1], c=shape[2])
        return ap

    def free(self, *keys):
        for key in keys:
            lo, hi = self.live.pop(key)
            self.dead.append((lo, hi, self.prog.all_events(key)))


def build_nc(stage=99):
    nc = bass.Bass("TRN2", target_bir_lowering=False)
    dt_in = lambda n, s: nc.dram_tensor(n, s, F32, kind="ExternalInput").ap()
    x_own = dt_in("x_own", [T, D])
    x_prev = dt_in("x_prev", [T, D])
    mem = dt_in("mem", [256, D])
    cst = dt_in("cst", [128, 6 * 128])
    w_in = dt_in("w_in", [D, IN_W])
    conv_w = dt_in("conv_w", [128, 64])
    conv_b = dt_in("conv_b", [128, 16])
    gate_b = dt_in("gate_b", [1, 8])
    norm_w = dt_in("norm_w", [1, 2048])
    fox_fb = dt_in("fox_fb", [1, 16])
    w_bml = dt_in("w_bml", [D, D])
    w_bfx = dt_in("w_bfx", [D, D])
    w_out = dt_in("w_out", [D, D])
    w_xq = dt_in("w_xq", [D, D])
    w_xkv = dt_in("w_xkv", [D, 2 * D])
    w_xo = dt_in("w_xo", [D, D])
    w_pq = dt_in("w_pq", [D, D])
    subk = dt_in("subk", [16 * 128, 128])
    peer_u = dt_in("peer_u", [16384 if stage >= 5 else 128, D])
    peer_v = dt_in("peer_v", [16384 if stage >= 5 else 128, D])
    ln_w = dt_in("ln_w", [3, D])
    ln_b = dt_in("ln_b", [3, D])
    out = nc.dram_tensor("out", [T, D], F32, kind="ExternalOutput").ap()
    uv16 = nc.dram_tensor("uv16", [16384, 2 * D], BF16).ap() if stage >= 5 else None

    with ExitStack() as es:
        P = Prog(nc, es)
        A = Arena(nc, es, P, 207 * 1024)
        ps = [es.enter_context(nc.psum_tensor("ps%d" % i, [128, 512], F32)) for i in range(8)]
        psb = [p[:].bitcast(BF16) for p in ps]
        rot = [0]

        def nb():
            rot[0] = (rot[0] + 1) % 4
            return rot[0]

        altc = [0]

        def evac_eng():
            altc[0] ^= 1
            return 'act' if altc[0] else 'dve'

        def copy(eng, o, i, reads, writes):
            if eng == 'act':
                P.op('act', lambda e: e.activation(out=o, in_=i, func=AF.Copy), reads=reads, writes=writes)
            else:
                P.op(eng, lambda e: e.tensor_copy(out=o, in_=i), reads=reads, writes=writes)

        cstf = A.alloc("cstf", [6, 128], F32)
        P.dma('sp', lambda e: e.dma_start(out=cstf, in_=cst.rearrange("p (a b) -> p a b", a=6)), writes=["cstf"])
        iota_f = cstf[:, 4:6, :].rearrange("p a b -> p (a b)")
        ident_f, tri_f, ones_f, vfl_f = cstf[:, 0, :], cstf[:, 1, :], cstf[:, 2, :], cstf[:, 3, :]
        cstb = A.alloc("cstb", [4, 128], BF16)
        P.op('dve', lambda e: e.tensor_copy(out=cstb, in_=cstf[:, 0:4, :]), reads=["cstf"], writes=["cstb"])
        ident_b, tri_b, ones_b, vfl_b = cstb[:, 0, :], cstb[:, 1, :], cstb[:, 2, :], cstb[:, 3, :]
        CST = ["cstf", "cstb"]

        convw = A.alloc("convw", [16, 4], F32)
        convb = A.alloc("convb", [16], F32)
        P.dma('sp', lambda e: e.dma_start(out=convw, in_=conv_w.rearrange("p (t j) -> p t j", j=4)), writes=["convw"])
        P.dma('sp', lambda e: e.dma_start(out=convb, in_=conv_b), writes=["convb"])
        gbias = A.alloc("gbias", [24], F32)
        P.dma('sp', lambda e: e.dma_start(out=gbias[:, 0:8], in_=gate_b.broadcast_to([128, 8])), writes=["gbias"])
        P.dma('sp', lambda e: e.dma_start(out=gbias[:, 8:24], in_=fox_fb.broadcast_to([128, 16])), writes=["gbias"])

        WSL = 2
        wt = A.alloc("wt", [WSL, 16, 256], BF16)
        wslot = [0]

        NCH = 256
        cv = {'next': 0, 'per': 0, 'buf': None}

        def conv_chunks(n):
            if stage < 5 or cv['buf'] is None:
                return
            for _ in range(n):
                c = cv['next']
                if c >= NCH:
                    return
                cv['next'] += 1
                src, dst, key = (peer_u, uv16[:, 0:D], "U16") if c < 128 else (peer_v, uv16[:, D:2 * D], "V16")
                r0 = (c % 128) * 128
                sl = c % 4
                CVS = cv['buf']
                P.dma('pool', lambda e: e.dma_start(out=CVS[:, sl, :], in_=src[r0:r0 + 128, :]), writes=[("CVS", sl)])
                P.dma('sp', lambda e: e.dma_start(out=dst[r0:r0 + 128, :], in_=CVS[:, sl, :]), reads=[("CVS", sl)], writes=[(key, c % 128)])

        def load_w(parts):
            s = wslot[0]
            wslot[0] = (s + 1) % WSL
            off = 0
            for (src, n) in parts:
                P.dma('pool', lambda e, src=src, n=n, off=off: e.dma_start(
                    out=wt[:, s, :, off:off + n], in_=src.rearrange("(c p) n -> p c n", p=128)),
                    writes=[("wt", s)])
                off += n
            conv_chunks(cv['per'])
            return s

        def proj_fm(s, c0, xT, xkey, t0, tn, evac):
            b = nb()
            for c in range(16):
                P.op('pe', lambda e, c=c: e.matmul(ps[b][:, 0:tn], lhsT=wt[:, s, c, c0:c0 + 128], rhs=xT[:, c, t0:t0 + tn],
                                                    start=(c == 0), stop=(c == 15)),
                     reads=[("wt", s), xkey], writes=[("ps", b)])
            evac(b, ps[b][:, 0:tn])

        def proj_tm(s, n, xT, xkey, tt, evac):
            b = nb()
            for c in range(16):
                P.op('pe', lambda e, c=c: e.matmul(ps[b][:, 0:n], lhsT=xT[:, c, tt * 128:(tt + 1) * 128], rhs=wt[:, s, c, 0:n],
                                                    start=(c == 0), stop=(c == 15)),
                     reads=[("wt", s), xkey], writes=[("ps", b)])
            evac(b, ps[b][:, 0:n])

        xTp = A.alloc("xTp", [16, T], BF16)
        xTo = A.alloc("xTo", [16, T], BF16)
        xTh = A.alloc("xTh", [16, 4], BF16)

        def load_xT(src, rows, dst, dkey, srckey_reads=()):
            xt = A.alloc("xt", [2, D], F32)
            for tt in range(rows // 128):
                sl = tt % 2
                P.dma('sp', lambda e, tt=tt, sl=sl: e.dma_start(out=xt[:, sl, :], in_=src[tt * 128:(tt + 1) * 128, :]),
                      writes=[("xt", sl)])
                for g in range(4):
                    b = nb()
                    for j in range(4):
                        c = g * 4 + j
                        P.op('pe', lambda e, c=c, j=j, b=b, sl=sl: e.transpose(ps[b][:, j * 128:(j + 1) * 128], xt[:, sl, c * 128:(c + 1) * 128], ident_f),
                             reads=[("xt", sl), "cstf"], writes=[("ps", b)])
                    copy(evac_eng(), dst[:, 4 * g:4 * g + 4, tt * 128:(tt + 1) * 128],
                         ps[b][:].rearrange("p (a b) -> p a b", a=4), [("ps", b)], [dkey])
            A.free("xt")

        load_xT(x_prev, T, xTp, "xTp")
        P.op('dve', lambda e: e.tensor_copy(out=xTh[:, :, 0:3], in_=xTp[:, :, T - 3:T]), reads=["xTp"], writes=["xTh"])
        load_xT(x_own, T, xTo, "xTo")

        G = A.alloc("G", [16, 24], F32)
        sg = load_w([(w_in[:, C_MLI:C_MLI + 8], 8), (w_in[:, C_FXF:C_FXF + 16], 16)])
        for tt in range(16):
            xT, xk = (xTp, "xTp") if tt < 8 else (xTo, "xTo")
            proj_tm(sg, 24, xT, xk, tt % 8,
                    lambda b, ap, tt=tt: P.op('dve', lambda e: e.tensor_tensor(out=G[:, tt, :], in0=ap, in1=gbias, op=ALU.add),
                                              reads=[("ps", b), "gbias"], writes=["G"]))
        LF = A.alloc("LF", [16, 20], F32)
        P.op('act', lambda e: e.activation(out=LF, in_=G[:, :, 4:24], func=AF.Exp, scale=-1.0), reads=["G"], writes=["LF"])
        P.op('act', lambda e: e.activation(out=LF, in_=LF, func=AF.Ln, bias=1.0), reads=["LF"], writes=["LF"])
        P.op('dve', lambda e: e.tensor_scalar(out=LF, in0=LF, scalar1=-1.0, scalar2=None, op0=ALU.mult), reads=["LF"], writes=["LF"])
        BC = A.alloc("BC", [16, 20], F32)
        TOT = A.alloc("TOT", [16, 20], F32)
        P.op('pe', lambda e: e.matmul(ps[4][:, 0:320], lhsT=tri_f, rhs=LF.rearrange("p a b -> p (a b)"), start=True, stop=True),
             reads=["LF", "cstf"], writes=[("ps", 4)])
        P.op('pe', lambda e: e.matmul(ps[5][:, 0:320], lhsT=ones_f, rhs=LF.rearrange("p a b -> p (a b)"), start=True, stop=True),
             reads=["LF", "cstf"], writes=[("ps", 5)])
        P.op('dve', lambda e: e.tensor_copy(out=BC.rearrange("p a b -> p (a b)"), in_=ps[4][:, 0:320]), reads=[("ps", 4)], writes=["BC"])
        P.op('dve', lambda e: e.tensor_copy(out=TOT.rearrange("p a b -> p (a b)"), in_=ps[5][:, 0:320]), reads=[("ps", 5)], writes=["TOT"])
        UP = A.alloc("UP", [16, 4], F32)
        WW = A.alloc("WW", [16, 4], F32)
        GD = A.alloc("GD", [16, 4], F32)
        P.op('dve', lambda e: e.tensor_tensor(out=UP, in0=G[:, :, 0:4], in1=BC[:, :, 0:4], op=ALU.subtract), reads=["G", "BC"], writes=["UP"])
        P.op('act', lambda e: e.activation(out=UP, in_=UP, func=AF.Exp, bias=-math.log(16.0)), reads=["UP"], writes=["UP"])
        P.op('act', lambda e: e.activation(out=WW, in_=BC[:, :, 0:4], func=AF.Exp), reads=["BC"], writes=["WW"])
        P.op('act', lambda e: e.activation(out=GD, in_=TOT[:, :, 0:4], func=AF.Exp), reads=["TOT"], writes=["GD"])
        GC = A.alloc("GC", [17, 16], F32)
        FG = A.alloc("FG", [16, 16], F32)
        P.op('dve', lambda e: e.memset(GC[:, 0, :], 0.0), writes=["GC"])
        for j in range(16):
            P.op('dve', lambda e, j=j: e.tensor_tensor(out=GC[:, j + 1, :], in0=GC[:, j, :], in1=TOT[:, j, 4:20], op=ALU.add),
                 reads=["GC", "TOT"], writes=["GC"])
        P.op('dve', lambda e: e.tensor_tensor(out=FG, in0=GC[:, 0:16, :], in1=BC[:, :, 4:20], op=ALU.add), reads=["GC", "BC"], writes=["FG"])
        A.free("G", "LF", "TOT")

        hT = A.alloc("hT", [16, T], BF16)
        yT = A.alloc("yT", [16, T], BF16)

        normw = A.alloc("normw", [2048], F32)
        P.dma('sp', lambda e: e.dma_start(out=normw, in_=norm_w.broadcast_to([128, 2048])), writes=["normw"])
        QT = A.alloc("QT", [2, T], BF16)
        KT = A.alloc("KT", [2, T], BF16)
        VA = A.alloc("VA", [8, 520], BF16)
        SO = A.alloc("SO", [8, 512], BF16)
        PRE = A.alloc("PRE", [T + 4], F32)
        ACC = A.alloc("ACC", [T], F32)
        KH = A.alloc("KH", [2, 4], F32)
        CF = A.alloc("CF", [2, 520], F32)
        CB = A.alloc("CB", [2, 520], BF16)
        MT = A.alloc("MT", [128], BF16)
        KU = A.alloc("KU", [256], BF16)
        HH = A.alloc("HH", [512], F32)
        HB = A.alloc("HB", [512], BF16)
        SM = A.alloc("SM", [16], F32)
        ST = A.alloc("ST", [6], F32)

        def convsilu(xT, xk, col0, ct, dst, dkey, dc, halo):
            raise NotImplementedError

        def mlstm_qk(h, own):
            xT, xk = (xTo, "xTo") if own else (xTp, "xTp")
            todo = ([("q", C_MLQ, 0, QT, "QT")] if own else []) + [("k", C_MLK, 8, KT, "KT")]
            for (nm, cb, ctb, dst, dkey) in todo:
                s = load_w([(w_in[:, cb + h * 256: cb + (h + 1) * 256], 256)])
                for dc in range(2):
                    ct = ctb + h * 2 + dc
                    if not own:
                        P.op('dve', lambda e: e.memset(PRE[:, 0:3], 0.0), writes=["PRE"])
                    elif nm == "k":
                        P.op('dve', lambda e, dc=dc: e.tensor_copy(out=PRE[:, 0:3], in_=KH[:, dc, 0:3]), reads=["KH"], writes=["PRE"])
                    else:
                        b = nb()
                        for c in range(16):
                            P.op('pe', lambda e, c=c, dc=dc, b=b: e.matmul(ps[b][:, 0:3], lhsT=wt[:, s, c, dc * 128:(dc + 1) * 128], rhs=xTh[:, c, 0:3],
                                                                           start=(c == 0), stop=(c == 15)),
                                 reads=[("wt", s), "xTh"], writes=[("ps", b)])
                        P.op('dve', lambda e, b=b: e.tensor_copy(out=PRE[:, 0:3], in_=ps[b][:, 0:3]), reads=[("ps", b)], writes=["PRE"])
                    for tb in range(2):
                        proj_fm(s, dc * 128, xT, xk, tb * 512, 512,
                                lambda b, ap, tb=tb: copy('act', PRE[:, 3 + tb * 512: 3 + (tb + 1) * 512], ap, [("ps", b)], ["PRE"]))
                    if nm == "k" and not own:
                        P.op('dve', lambda e, dc=dc: e.tensor_copy(out=KH[:, dc, 0:3], in_=PRE[:, T:T + 3]), reads=["PRE"], writes=["KH"])
                    P.op('dve', lambda e, ct=ct: e.tensor_scalar(out=ACC, in0=PRE[:, 0:T], scalar1=convw[:, ct, 0:1], scalar2=None, op0=ALU.mult),
                         reads=["PRE", "convw"], writes=["ACC"])
                    for j in range(1, 4):
                        P.op('dve', lambda e, ct=ct, j=j: e.scalar_tensor_tensor(out=ACC, in0=PRE[:, j:T + j], scalar=convw[:, ct, j:j + 1], in1=ACC,
                                                                              op0=ALU.mult, op1=ALU.add),
                             reads=["PRE", "convw", "ACC"], writes=["ACC"])
                    P.op('act', lambda e, ct=ct, dc=dc, dst=dst: e.activation(out=dst[:, dc, :], in_=ACC, func=AF.Silu, bias=convb[:, ct:ct + 1]),
                         reads=["ACC", "convb"], writes=[dkey])

        def mlstm_v(h, own):
            xT, xk = (xTo, "xTo") if own else (xTp, "xTp")
            for hf in range(2):
                s = load_w([(w_in[:, C_MLV + h * 512 + hf * 256: C_MLV + h * 512 + (hf + 1) * 256], 256)])
                for tt in range(8):
                    proj_tm(s, 256, xT, xk, tt,
                            lambda b, ap, tt=tt, hf=hf: copy(evac_eng(), VA[:, tt, hf * 256:(hf + 1) * 256], ap, [("ps", b)], ["VA"]))
            if own:
                P.op('dve', lambda e: e.memset(VA[:, :, 512:513], 1.0), writes=["VA"])
            else:
                for tt in range(8):
                    P.op('dve', lambda e, tt=tt: e.tensor_copy(out=VA[:, tt, 512:513], in_=vfl_f[:, 0:1]), reads=["cstf"], writes=["VA"])

        def mlstm_o(h):
            for hf in range(2):
                s = load_w([(w_in[:, C_MLO + h * 512 + hf * 256: C_MLO + h * 512 + (hf + 1) * 256], 256)])
                for tt in range(8):
                    proj_tm(s, 256, xTo, "xTo", tt,
                            lambda b, ap, tt=tt, hf=hf: P.op('act', lambda e: e.activation(out=SO[:, tt, hf * 256:(hf + 1) * 256], in_=ap, func=AF.Sigmoid),
                                                             reads=[("ps", b)], writes=["SO"]))

        def mlstm_scan(h, own):
            for j in range(8):
                gj = j + (8 if own else 0)
                cs = slice(j * 128, (j + 1) * 128)
                if own:
                    for dc in range(2):
                        P.op('pe', lambda e, dc=dc: e.matmul(ps[4][:, 0:128], lhsT=KT[:, dc, cs], rhs=QT[:, dc, cs], start=(dc == 0), stop=(dc == 1)),
                             reads=["KT", "QT"], writes=[("ps", 4)])
                    P.op('dve', lambda e: e.scalar_tensor_tensor(out=MT, in0=ps[4][:, 0:128], scalar=UP[:, gj, h:h + 1], in1=tri_f, op0=ALU.mult, op1=ALU.mult),
                         reads=[("ps", 4), "UP", "cstf"], writes=["MT"])
                    P.op('pe', lambda e: e.matmul(ps[5][:, 0:512], lhsT=MT, rhs=VA[:, j, 0:512], start=True, stop=False),
                         reads=["MT", "VA"], writes=[("ps", 5)])
                    for dc in range(2):
                        P.op('pe', lambda e, dc=dc: e.matmul(ps[5][:, 0:512], lhsT=QT[:, dc, cs], rhs=CB[:, dc, 0:512], start=False, stop=(dc == 1)),
                             reads=["QT", "CB"], writes=[("ps", 5)])
                    P.op('pe', lambda e: e.matmul(ps[6][:, 0:1], lhsT=MT, rhs=VA[:, j, 512:513], start=True, stop=False),
                         reads=["MT", "VA"], writes=[("ps", 6)])
                    for dc in range(2):
                        P.op('pe', lambda e, dc=dc: e.matmul(ps[6][:, 0:1], lhsT=QT[:, dc, cs], rhs=CB[:, dc, 512:513], start=False, stop=(dc == 1)),
                             reads=["QT", "CB"], writes=[("ps", 6)])
                    P.op('dve', lambda e: e.tensor_tensor(out=SM[:, 0:1], in0=ps[6][:, 0:1], in1=WW[:, gj, h:h + 1], op=ALU.mult),
                         reads=[("ps", 6), "WW"], writes=["SM"])
                    P.op('act', lambda e: e.activation(out=SM[:, 1:2], in_=SM[:, 0:1], func=AF.Abs), reads=["SM"], writes=["SM"])
                    P.op('dve', lambda e: e.tensor_scalar(out=SM[:, 1:2], in0=SM[:, 1:2], scalar1=1.0, scalar2=None, op0=ALU.max), reads=["SM"], writes=["SM"])
                    P.op('dve', lambda e: e.reciprocal(out=SM[:, 2:3], in_=SM[:, 1:2]), reads=["SM"], writes=["SM"])
                    P.op('dve', lambda e: e.tensor_tensor(out=SM[:, 3:4], in0=SM[:, 2:3], in1=WW[:, gj, h:h + 1], op=ALU.mult), reads=["SM", "WW"], writes=["SM"])
                    P.op('dve', lambda e: e.tensor_scalar(out=HH, in0=ps[5][:, 0:512], scalar1=SM[:, 3:4], scalar2=None, op0=ALU.mult),
                         reads=[("ps", 5), "SM"], writes=["HH"])
                    P.op('dve', lambda e: e.bn_stats(out=ST, in_=HH), reads=["HH"], writes=["ST"])
                    P.op('dve', lambda e: e.bn_aggr(out=SM[:, 4:6], in_=ST), reads=["ST"], writes=["SM"])
                    P.op('act', lambda e: e.activation(out=SM[:, 6:7], in_=SM[:, 5:6], func=AF.Sqrt, bias=LN_EPS), reads=["SM"], writes=["SM"])
                    P.op('dve', lambda e: e.reciprocal(out=SM[:, 7:8], in_=SM[:, 6:7]), reads=["SM"], writes=["SM"])
                    P.op('dve', lambda e: e.tensor_scalar(out=HH, in0=HH, scalar1=SM[:, 4:5], scalar2=SM[:, 7:8], op0=ALU.subtract, op1=ALU.mult),
                         reads=["HH", "SM"], writes=["HH"])
                    P.op('dve', lambda e: e.tensor_tensor(out=HH, in0=HH, in1=normw[:, h * 512:(h + 1) * 512], op=ALU.mult), reads=["HH", "normw"], writes=["HH"])
                    P.op('dve', lambda e: e.tensor_tensor(out=HB, in0=HH, in1=SO[:, j, :], op=ALU.mult), reads=["HH", "SO"], writes=["HB"])
                    for q4 in range(4):
                        P.op('pe', lambda e, q4=q4: e.transpose(psb[7][:, q4 * 128:(q4 + 1) * 128], HB[:, q4 * 128:(q4 + 1) * 128], ident_b),
                             reads=["HB", "cstb"], writes=[("ps", 7)])
                    copy('act', hT[:, h * 4:(h + 1) * 4, cs], psb[7][:, 0:512].rearrange("p (a b) -> p a b", a=4), [("ps", 7)], ["hT"])
                    if j == 7:
                        continue
                b = nb()
                for dc in range(2):
                    P.op('pe', lambda e, dc=dc, b=b: e.transpose(psb[b][:, dc * 128:(dc + 1) * 128], KT[:, dc, cs], ident_b),
                         reads=["KT", "cstb"], writes=[("ps", b)])
                P.op('dve', lambda e, b=b: e.tensor_scalar(out=KU, in0=psb[b][:, 0:256], scalar1=UP[:, gj, h:h + 1], scalar2=None, op0=ALU.mult),
                     reads=[("ps", b), "UP"], writes=["KU"])
                first = (not own) and j == 0
                for dc in range(2):
                    b = nb()
                    P.op('pe', lambda e, dc=dc, b=b: e.matmul(ps[b][:, 0:512], lhsT=KU[:, dc * 128:(dc + 1) * 128], rhs=VA[:, j, 0:512], start=True, stop=True),
                         reads=["KU", "VA"], writes=[("ps", b)])
                    P.op('pe', lambda e, dc=dc: e.matmul(ps[6][:, 8 + dc:9 + dc], lhsT=KU[:, dc * 128:(dc + 1) * 128], rhs=VA[:, j, 512:513], start=True, stop=True),
                         reads=["KU", "VA"], writes=[("ps", 6)])
                    if first:
                        P.op('dve', lambda e, dc=dc, b=b: e.tensor_scalar(out=CF[:, dc, 0:512], in0=ps[b][:, 0:512], scalar1=GD[:, gj, h:h + 1], scalar2=None, op0=ALU.mult),
                             reads=[("ps", b), "GD"], writes=["CF"])
                        P.op('dve', lambda e, dc=dc: e.tensor_scalar(out=CF[:, dc, 512:513], in0=ps[6][:, 8 + dc:9 + dc], scalar1=GD[:, gj, h:h + 1], scalar2=None, op0=ALU.mult),
                             reads=[("ps", 6), "GD"], writes=["CF"])
                    else:
                        P.op('dve', lambda e, dc=dc: e.tensor_scalar(out=CF[:, dc, 0:513], in0=CF[:, dc, 0:513], scalar1=GD[:, gj, h:h + 1], scalar2=None, op0=ALU.mult),
                             reads=["CF", "GD"], writes=["CF"])
                        P.op('dve', lambda e, dc=dc, b=b: e.scalar_tensor_tensor(out=CF[:, dc, 0:512], in0=ps[b][:, 0:512], scalar=GD[:, gj, h:h + 1], in1=CF[:, dc, 0:512],
                                                                               op0=ALU.mult, op1=ALU.add),
                             reads=[("ps", b), "GD", "CF"], writes=["CF"])
                        P.op('dve', lambda e, dc=dc: e.scalar_tensor_tensor(out=CF[:, dc, 512:513], in0=ps[6][:, 8 + dc:9 + dc], scalar=GD[:, gj, h:h + 1], in1=CF[:, dc, 512:513],
                                                                          op0=ALU.mult, op1=ALU.add),
                             reads=[("ps", 6), "GD", "CF"], writes=["CF"])
                copy('act', CB[:, :, 0:513], CF[:, :, 0:513], ["CF"], ["CB"])

        if stage >= 1:
            for h in range(4):
                mlstm_qk(h, False)
                mlstm_v(h, False)
                mlstm_scan(h, False)
                mlstm_qk(h, True)
                mlstm_v(h, True)
                mlstm_o(h)
                mlstm_scan(h, True)
        A.free("QT", "KT", "VA", "SO", "PRE", "ACC", "KH", "CF", "CB", "MT", "KU", "HH", "HB", "ST", "SM", "normw")

        if stage >= 5:
            cv['buf'] = A.alloc("CVS", [4, D], BF16)
            cv['per'] = 5
        def fox_group(g):
            FK = A.alloc("FK", [2, 2 * T], BF16)
            FV = A.alloc("FV", [16, 256], BF16)
            FQ = A.alloc("FQ", [2, T], BF16)
            PT = A.alloc("PT", [4, 128], BF16)
            FB = A.alloc("FB", [2, 8, 16], F32)
            RC = A.alloc("RC", [128], F32)
            c0 = g * 256
            for (xT, xk, tb0) in ((xTp, "xTp", 0), (xTo, "xTo", 8)):
                s = load_w([(w_in[:, C_FXK + c0:C_FXK + c0 + 256], 256)])
                for hh in range(2):
                    for tb in range(2):
                        proj_fm(s, hh * 128, xT, xk, tb * 512, 512,
                                lambda b, ap, hh=hh, tb=tb, tb0=tb0: copy(evac_eng(), FK[:, hh, tb0 * 128 + tb * 512: tb0 * 128 + (tb + 1) * 512], ap, [("ps", b)], ["FK"]))
                s = load_w([(w_in[:, C_FXV + c0:C_FXV + c0 + 256], 256)])
                for tt in range(8):
                    proj_tm(s, 256, xT, xk, tt,
                            lambda b, ap, tt=tt, tb0=tb0: copy(evac_eng(), FV[:, tb0 + tt, :], ap, [("ps", b)], ["FV"]))
            s = load_w([(w_in[:, C_FXQ + c0:C_FXQ + c0 + 256], 256)])
            for hh in range(2):
                for tb in range(2):
                    proj_fm(s, hh * 128, xTo, "xTo", tb * 512, 512,
                            lambda b, ap, hh=hh, tb=tb: copy(evac_eng(), FQ[:, hh, tb * 512:(tb + 1) * 512], ap, [("ps", b)], ["FQ"]))
            scale = 128.0 ** -0.5
            LOOK = 2
            for hh in range(2):
                h = g * 2 + hh
                for jq in range(8):
                    gq = 8 + jq
                    P.op('dve', lambda e: e.tensor_scalar(out=FB[:, hh, jq, 0:gq + 1], in0=FG[:, 0:gq + 1, h], scalar1=-1.0, scalar2=GC[:, gq, h:h + 1], op0=ALU.mult, op1=ALU.add),
                         reads=["FG", "GC"], writes=[("FB", hh)])
                for jq in range(8):
                    gq = 8 + jq
                    qs = slice(jq * 128, (jq + 1) * 128)
                    acc = 4 + (jq % 2)
                    n = gq + 1

                    def stage_a(jk):
                        b = nb()
                        sl = jk % 4
                        P.op('pe', lambda e: e.matmul(ps[b][:, 0:128], lhsT=FK[:, hh, jk * 128:(jk + 1) * 128], rhs=FQ[:, hh, qs], start=True, stop=True),
                             reads=["FK", "FQ"], writes=[("ps", b)])
                        P.op('act', lambda e: e.activation(out=PT[:, sl, :], in_=ps[b][:, 0:128], func=AF.Exp, bias=FB[:, hh, jq, jk:jk + 1], scale=scale),
                             reads=[("ps", b), ("FB", hh)], writes=[("PT", sl)])
                        if jk == gq:
                            P.op('dve', lambda e: e.tensor_tensor(out=PT[:, sl, :], in0=PT[:, sl, :], in1=tri_b, op=ALU.mult),
                                 reads=[("PT", sl), "cstb"], writes=[("PT", sl)])

                    def stage_b(jk):
                        sl = jk % 4
                        P.op('pe', lambda e: e.matmul(ps[acc][:, 0:128], lhsT=FV[:, jk, hh * 128:(hh + 1) * 128], rhs=PT[:, sl, :], start=(jk == 0), stop=(jk == gq)),
                             reads=["FV", ("PT", sl)], writes=[("ps", acc)])
                        P.op('pe', lambda e: e.matmul(ps[acc + 2][:, 0:128], lhsT=(vfl_b if jk < 8 else ones_b), rhs=PT[:, sl, :], start=(jk == 0), stop=(jk == gq)),
                             reads=["cstb", ("PT", sl)], writes=[("ps", acc + 2)])

                    for i in range(n + LOOK):
                        if i < n:
                            stage_a(i)
                        if i >= LOOK:
                            stage_b(i - LOOK)
                    P.op('dve', lambda e: e.reciprocal(out=RC, in_=ps[acc + 2][:, 0:128]), reads=[("ps", acc + 2)], writes=["RC"])
                    P.op('dve', lambda e: e.tensor_tensor(out=yT[:, h, qs], in0=ps[acc][:, 0:128], in1=RC, op=ALU.mult), reads=[("ps", acc), "RC"], writes=["yT"])
            A.free("FK", "FV", "FQ", "PT", "FB", "RC")

        if stage >= 2:
            for g in range(8):
                fox_group(g)
        A.free("xTp", "xTh", "GC", "FG", "BC", "UP", "WW", "GD")

        if stage in (1, 2):
            src_, k_ = (hT, "hT") if stage == 1 else (yT, "yT")
            ev = P.dma('pool', lambda e: e.dma_start(out=out.rearrange("(c p) t -> p c t", p=128), in_=src_), reads=[k_])
            P.need('pool', ev)
            P.emit()
            return nc
        cv['per'] = 1
        mT = A.alloc("mT", [16, T], BF16)
        SG = A.alloc("SG", [2, T], BF16)
        T1 = A.alloc("T1", [512], F32)
        T2 = A.alloc("T2", [512], F32)
        if stage >= 3:
            for c in range(16):
                s = load_w([(w_in[:, C_GML + c * 128:C_GML + (c + 1) * 128], 128), (w_in[:, C_GFX + c * 128:C_GFX + (c + 1) * 128], 128)])
                for which in range(2):
                    for tb in range(2):
                        proj_fm(s, which * 128, xTo, "xTo", tb * 512, 512,
                                lambda b, ap, which=which, tb=tb: P.op('act', lambda e: e.activation(out=SG[:, which, tb * 512:(tb + 1) * 512], in_=ap, func=AF.Sigmoid),
                                                                       reads=[("ps", b)], writes=["SG"]))
                s = load_w([(w_bml[:, c * 128:(c + 1) * 128], 128), (w_bfx[:, c * 128:(c + 1) * 128], 128)])
                for tb in range(2):
                    proj_fm(s, 0, hT, "hT", tb * 512, 512,
                            lambda b, ap, tb=tb: P.op('dve', lambda e: e.tensor_tensor(out=T1, in0=ap, in1=SG[:, 0, tb * 512:(tb + 1) * 512], op=ALU.mult),
                                                      reads=[("ps", b), "SG"], writes=["T1"]))
                    proj_fm(s, 128, yT, "yT", tb * 512, 512,
                            lambda b, ap, tb=tb: P.op('dve', lambda e: e.tensor_tensor(out=T2, in0=ap, in1=SG[:, 1, tb * 512:(tb + 1) * 512], op=ALU.mult),
                                                      reads=[("ps", b), "SG"], writes=["T2"]))
                    P.op('dve', lambda e, c=c, tb=tb: e.tensor_tensor(out=mT[:, c, tb * 512:(tb + 1) * 512], in0=T1, in1=T2, op=ALU.add),
                         reads=["T1", "T2"], writes=["mT"])
        A.free("SG", "T1", "T2", "xTo")
        if stage >= 3:
            A.free("hT", "yT")

        X1 = A.alloc("X1", [NT, D], F32)
        X1T = A.alloc("X1T", [16, T], BF16)
        ST4 = A.alloc("ST4", [4, 6], F32)
        SM2 = A.alloc("SM2", [8], F32)

        def layer_norm_tiles(li, make_T):
            lnw = A.alloc("lnw", [D], F32)
            lnb = A.alloc("lnb", [D], F32)
            P.dma('sp', lambda e: e.dma_start(out=lnw, in_=ln_w[li:li + 1, :].broadcast_to([128, D])), writes=["lnw"])
            P.dma('sp', lambda e: e.dma_start(out=lnb, in_=ln_b[li:li + 1, :].broadcast_to([128, D])), writes=["lnb"])
            for tt in range(NT):
                xk = ("X1", tt)
                xr = X1[:, tt, :]
                for q4 in range(4):
                    P.op('dve', lambda e, q4=q4: e.bn_stats(out=ST4[:, q4, :], in_=xr[:, q4 * 512:(q4 + 1) * 512]), reads=[xk], writes=["ST4"])
                P.op('dve', lambda e: e.bn_aggr(out=SM2[:, 0:2], in_=ST4.rearrange("p a b -> p (a b)")), reads=["ST4"], writes=["SM2"])
                P.op('act', lambda e: e.activation(out=SM2[:, 2:3], in_=SM2[:, 1:2], func=AF.Sqrt, bias=LN_EPS), reads=["SM2"], writes=["SM2"])
                P.op('dve', lambda e: e.reciprocal(out=SM2[:, 3:4], in_=SM2[:, 2:3]), reads=["SM2"], writes=["SM2"])
                P.op('dve', lambda e: e.tensor_scalar(out=xr, in0=xr, scalar1=SM2[:, 0:1], scalar2=SM2[:, 3:4], op0=ALU.subtract, op1=ALU.mult),
                     reads=[xk, "SM2"], writes=[xk])
                P.op('dve', lambda e: e.tensor_tensor(out=xr, in0=xr, in1=lnw, op=ALU.mult), reads=[xk, "lnw"], writes=[xk])
                P.op('dve', lambda e: e.tensor_tensor(out=xr, in0=xr, in1=lnb, op=ALU.add), reads=[xk, "lnb"], writes=[xk])
                if make_T:
                    for g in range(4):
                        b = nb()
                        for j in range(4):
                            c = g * 4 + j
                            P.op('pe', lambda e, c=c, j=j, b=b: e.transpose(ps[b][:, j * 128:(j + 1) * 128], xr[:, c * 128:(c + 1) * 128], ident_f),
                                 reads=[xk, "cstf"], writes=[("ps", b)])
                        copy('act', X1T[:, 4 * g:4 * g + 4, tt * 128:(tt + 1) * 128], ps[b][:].rearrange("p (a b) -> p a b", a=4), [("ps", b)], ["X1T"])
            A.free("lnw", "lnb")

        def out_proj_residual(W, inT, inkey, first):
            if first:
                for tt in range(NT):
                    P.dma('sp', lambda e, tt=tt: e.dma_start(out=X1[:, tt, :], in_=x_own[tt * 128:(tt + 1) * 128, :]), writes=[("X1", tt)])
            for cb in range(8):
                s = load_w([(W[:, cb * 256:(cb + 1) * 256], 256)])
                for tt in range(NT):
                    proj_tm(s, 256, inT, inkey, tt,
                            lambda b, ap, tt=tt, cb=cb: P.op('dve', lambda e: e.scalar_tensor_tensor(
                                out=X1[:, tt, cb * 256:(cb + 1) * 256], in0=X1[:, tt, cb * 256:(cb + 1) * 256], scalar=DN_ALPHA, in1=ap, op0=ALU.mult, op1=ALU.add),
                                reads=[("ps", b), ("X1", tt)], writes=[("X1", tt)]))

        if stage >= 3:
            out_proj_residual(w_out, mT, "mT", True)
            layer_norm_tiles(0, True)
        A.free("mT")

        if stage >= 5:
            conv_chunks(NCH)
            A.free("CVS")
            cv['buf'] = None
        if stage >= 4:
            QXT = A.alloc("QXT", [16, T], BF16)
            OXT = A.alloc("OXT", [16, T], BF16)
            memT = A.alloc("memT", [16, 256], BF16)
            load_xT(mem, 256, memT, "memT")
            KXT = A.alloc("KXT", [16, 256], BF16)
            VX = A.alloc("VX", [2, D], BF16)
            for cb in range(8):
                s = load_w([(w_xkv[:, cb * 256:(cb + 1) * 256], 256)])
                for hh in range(2):
                    proj_fm(s, hh * 128, memT, "memT", 0, 256,
                            lambda b, ap, cb=cb, hh=hh: copy(evac_eng(), KXT[:, cb * 2 + hh, :], ap, [("ps", b)], ["KXT"]))
            for cb in range(8):
                s = load_w([(w_xkv[:, D + cb * 256:D + (cb + 1) * 256], 256)])
                for mt in range(2):
                    proj_tm(s, 256, memT, "memT", mt,
                            lambda b, ap, cb=cb, mt=mt: copy(evac_eng(), VX[:, mt, cb * 256:(cb + 1) * 256], ap, [("ps", b)], ["VX"]))
            A.free("memT")
            for cb in range(8):
                s = load_w([(w_xq[:, cb * 256:(cb + 1) * 256], 256)])
                for hh in range(2):
                    for tb in range(2):
                        proj_fm(s, hh * 128, X1T, "X1T", tb * 512, 512,
                                lambda b, ap, cb=cb, hh=hh, tb=tb: copy(evac_eng(), QXT[:, cb * 2 + hh, tb * 512:(tb + 1) * 512], ap, [("ps", b)], ["QXT"]))
            PX = A.alloc("PX", [2, 512], BF16)
            RX = A.alloc("RX", [512], F32)
            xscale = 512.0 ** -0.5
            for h in range(4):
                for tb in range(2):
                    ts_ = slice(tb * 512, (tb + 1) * 512)
                    for mt in range(2):
                        b = nb()
                        for dc in range(4):
                            P.op('pe', lambda e, dc=dc, b=b, mt=mt: e.matmul(ps[b][:, 0:512], lhsT=KXT[:, h * 4 + dc, mt * 128:(mt + 1) * 128], rhs=QXT[:, h * 4 + dc, ts_],
                                                                             start=(dc == 0), stop=(dc == 3)),
                                 reads=["KXT", "QXT"], writes=[("ps", b)])
                        P.op('act', lambda e, b=b, mt=mt: e.activation(out=PX[:, mt, :], in_=ps[b][:, 0:512], func=AF.Exp, scale=xscale),
                             reads=[("ps", b)], writes=[("PX", mt)])
                    for mt in range(2):
                        P.op('pe', lambda e, mt=mt: e.matmul(ps[4][:, 0:512], lhsT=ones_b, rhs=PX[:, mt, :], start=(mt == 0), stop=(mt == 1)),
                             reads=["cstb", ("PX", mt)], writes=[("ps", 4)])
                    P.op('dve', lambda e: e.reciprocal(out=RX, in_=ps[4][:, 0:512]), reads=[("ps", 4)], writes=["RX"])
                    for dc in range(4):
                        b = 5 + (dc % 2)
                        for mt in range(2):
                            P.op('pe', lambda e, dc=dc, mt=mt, b=b: e.matmul(ps[b][:, 0:512], lhsT=VX[:, mt, (h * 4 + dc) * 128:(h * 4 + dc + 1) * 128], rhs=PX[:, mt, :],
                                                                             start=(mt == 0), stop=(mt == 1)),
                                 reads=["VX", ("PX", 0), ("PX", 1)], writes=[("ps", b)])
                        P.op('dve', lambda e, dc=dc, b=b: e.tensor_tensor(out=OXT[:, h * 4 + dc, ts_], in0=ps[b][:, 0:512], in1=RX, op=ALU.mult),
                             reads=[("ps", b), "RX"], writes=["OXT"])
            A.free("KXT", "VX", "QXT", "PX", "RX")
            out_proj_residual(w_xo, OXT, "OXT", False)
            A.free("OXT")
            layer_norm_tiles(1, True)

        if stage >= 5:
            SKT = A.alloc("SKT", [16, 128], BF16)
            skt = A.alloc("skt", [2, 128], F32)
            for j in range(16):
                P.dma('sp', lambda e, j=j: e.dma_start(out=skt[:, j % 2, :], in_=subk[j * 128:(j + 1) * 128, :]), writes=[("skt", j % 2)])
                b = nb()
                P.op('pe', lambda e, b=b: e.transpose(ps[b][:, 0:128], skt[:, j % 2, :], ident_f), reads=[("skt", j % 2), "cstf"], writes=[("ps", b)])
                copy('dve', SKT[:, j, :], ps[b][:, 0:128], [("ps", b)], ["SKT"])
            A.free("skt")
            QPT = A.alloc("QPT", [16, T], BF16)
            for cb in range(8):
                s = load_w([(w_pq[:, cb * 256:(cb + 1) * 256], 256)])
                for hh in range(2):
                    for tb in range(2):
                        proj_fm(s, hh * 128, X1T, "X1T", tb * 512, 512,
                                lambda b, ap, cb=cb, hh=hh, tb=tb: copy(evac_eng(), QPT[:, cb * 2 + hh, tb * 512:(tb + 1) * 512], ap, [("ps", b)], ["QPT"]))
            A.free("X1T", "wt")
            SC = A.alloc("SC", [16, 128], F32)
            TS = A.alloc("TS", [16, 16], F32)
            TI = A.alloc("TI", [16, 16], U32)
            TIF = A.alloc("TIF", [16, 16], F32)
            CS = A.alloc("CS", [4, 256], F32)
            CI = A.alloc("CI", [4, 256], F32)
            BS = A.alloc("BS", [8, 16], F32)
            BP = A.alloc("BP", [4, 16], U32)
            BPF = A.alloc("BPF", [4, 16], F32)
            IDF = A.alloc("IDF", [128], F32)
            IDUa = A.alloc("IDUa", [NT, 128], U32)
            GWa = A.alloc("GWa", [NT, 128], F32)
            JK = A.alloc("JK", [4, 256], BF16)
            SM3 = A.alloc("SM3", [4, 4], F32)
            AAs = [A.alloc("AA%d" % i, [128], F32) for i in range(2)]
            HAs = [A.alloc("HA%d" % i, [128], F32) for i in range(2)]
            JK2 = A.alloc("JK2", [D], BF16)
            DG = A.alloc("DG", [4, 128], BF16)
            NSL = 7
            UBs = [A.alloc("UB%d" % i, [2 * D], BF16) for i in range(NSL)]

            NI = 4

            def select(tt):
                tsl = slice(tt * 128, (tt + 1) * 128)
                GW = GWa[:, tt, :].rearrange("p (a b) -> p a b", a=8)
                for q4 in range(4):
                    for jj in range(4):
                        j = q4 * 4 + jj
                        P.op('pe', lambda e: e.matmul(ps[4 + q4][:, jj * 128:(jj + 1) * 128], lhsT=QPT[:, j, tsl], rhs=SKT[:, j, :], start=True, stop=True),
                             reads=["QPT", "SKT"], writes=[("ps", 4 + q4)])
                    copy('act', SC[:, q4 * 4:(q4 + 1) * 4, :], ps[4 + q4][:].rearrange("p (a b) -> p a b", a=4), [("ps", 4 + q4)], [("SC", q4 * 4 + i) for i in range(4)])
                yield
                for j in range(16):
                    P.op('dve', lambda e: e.max(out=TS[:, j, 0:8], in_=SC[:, j, :]), reads=[("SC", j)], writes=[("TS", j)])
                    if j % 2:
                        yield
                for j in range(16):
                    P.op('dve', lambda e: e.max_index(out=TI[:, j, 0:8], in_max=TS[:, j, 0:8], in_values=SC[:, j, :]), reads=[("SC", j), ("TS", j)], writes=[("TI", j)])
                    if j % 2:
                        yield
                for j in range(16):
                    P.op('dve', lambda e: e.match_replace(out=SC[:, j, :], in_to_replace=TS[:, j, 0:8], in_values=SC[:, j, :], imm_value=-1e30), reads=[("SC", j), ("TS", j)], writes=[("SC", j)])
                    if j % 2:
                        yield
                for j in range(16):
                    P.op('dve', lambda e: e.max(out=TS[:, j, 8:16], in_=SC[:, j, :]), reads=[("SC", j)], writes=[("TS", j)])
                    if j % 2:
                        yield
                for j in range(16):
                    P.op('dve', lambda e: e.max_index(out=TI[:, j, 8:16], in_max=TS[:, j, 8:16], in_values=SC[:, j, :]), reads=[("SC", j), ("TS", j)], writes=[("TI", j)])
                    if j % 2:
                        yield
                P.op('dve', lambda e: e.tensor_copy(out=TIF, in_=TI), reads=[("TI", j) for j in range(16)], writes=["TIF"])
                yield
                TS4 = TS.rearrange("p (h two) k -> p h two k", two=2)
                TI4 = TIF.rearrange("p (h two) k -> p h two k", two=2)
                for h0 in range(0, 8, NI):
                    hs = [(h0 + i, i) for i in range(NI)]
                    for h, i in hs:
                        csv = CS[:, i, :].rearrange("p (a b) -> p a b", a=16)
                        P.op('dve', lambda e: e.tensor_tensor(out=csv, in0=TS4[:, h, 0, :].unsqueeze(2).to_broadcast([128, 16, 16]),
                                                              in1=TS4[:, h, 1, :].unsqueeze(1).to_broadcast([128, 16, 16]), op=ALU.add),
                             reads=[("TS", 2 * h), ("TS", 2 * h + 1)], writes=[("CS", i)])
                    yield
                    for h, i in hs:
                        civ = CI[:, i, :].rearrange("p (a b) -> p a b", a=16)
                        P.op('dve', lambda e: e.scalar_tensor_tensor(out=civ, in0=TI4[:, h, 0, :].unsqueeze(2).to_broadcast([128, 16, 16]), scalar=128.0,
                                                                     in1=TI4[:, h, 1, :].unsqueeze(1).to_broadcast([128, 16, 16]), op0=ALU.mult, op1=ALU.add),
                             reads=["TIF"], writes=[("CI", i)])
                    yield
                    for h, i in hs:
                        P.op('dve', lambda e: e.max(out=BS[:, h, 0:8], in_=CS[:, i, :]), reads=[("CS", i)], writes=[("BS", h)])
                    yield
                    for h, i in hs:
                        P.op('dve', lambda e: e.max_index(out=BP[:, i, 0:8], in_max=BS[:, h, 0:8], in_values=CS[:, i, :]), reads=[("CS", i), ("BS", h)], writes=[("BP", i)])
                    yield
                    for h, i in hs:
                        P.op('dve', lambda e: e.match_replace(out=CS[:, i, :], in_to_replace=BS[:, h, 0:8], in_values=CS[:, i, :], imm_value=-1e30), reads=[("CS", i), ("BS", h)], writes=[("CS", i)])
                    yield
                    for h, i in hs:
                        P.op('dve', lambda e: e.max(out=BS[:, h, 8:16], in_=CS[:, i, :]), reads=[("CS", i)], writes=[("BS", h)])
                    yield
                    for h, i in hs:
                        P.op('dve', lambda e: e.max_index(out=BP[:, i, 8:16], in_max=BS[:, h, 8:16], in_values=CS[:, i, :]), reads=[("CS", i), ("BS", h)], writes=[("BP", i)])
                    yield
                    for h, i in hs:
                        P.op('dve', lambda e: e.tensor_copy(out=BPF[:, i, :], in_=BP[:, i, :]), reads=[("BP", i)], writes=[("BPF", i)])
                    yield
                    for k in range(16):
                        for h, i in hs:
                            P.op('dve', lambda e: e.scalar_tensor_tensor(out=JK[:, i, :], in0=iota_f, scalar=BPF[:, i, k:k + 1], in1=CI[:, i, :], op0=ALU.is_equal, op1=ALU.mult,
                                                                         accum_out=IDF[:, h * 16 + k:h * 16 + k + 1]),
                                 reads=["cstf", ("BPF", i), ("CI", i)], writes=[("IDF", h * 16 + k)])
                            if i % 2:
                                yield
                    for h, i in hs:
                        P.op('dve', lambda e: e.tensor_scalar(out=SM3[:, i, 0:1], in0=BS[:, h, 0:1], scalar1=-1.0, scalar2=None, op0=ALU.mult), reads=[("BS", h)], writes=[("SM3", i)])
                    for h, i in hs:
                        P.op('act', lambda e: e.activation(out=GW[:, h, :], in_=BS[:, h, :], func=AF.Exp, bias=SM3[:, i, 0:1], accum_out=SM3[:, i, 1:2]),
                             reads=[("BS", h), ("SM3", i)], writes=[("GWa", tt, h), ("SM3", i)])
                    yield
                    for h, i in hs:
                        P.op('dve', lambda e: e.reciprocal(out=SM3[:, i, 2:3], in_=SM3[:, i, 1:2]), reads=[("SM3", i)], writes=[("SM3", i)])
                    for h, i in hs:
                        P.op('dve', lambda e: e.tensor_scalar(out=GW[:, h, :], in0=GW[:, h, :], scalar1=SM3[:, i, 2:3], scalar2=None, op0=ALU.mult),
                             reads=[("GWa", tt, h), ("SM3", i)], writes=[("GWa", tt, h)])
                    yield
                P.op('dve', lambda e: e.tensor_copy(out=IDUa[:, tt, :], in_=IDF), reads=[("IDF", c) for c in range(128)], writes=[("IDUa", tt)])

            P.op('dve', lambda e: e.memset(JK, 0.0), writes=["JK"])
            P.op('dve', lambda e: e.memset(JK2, 0.0), writes=["JK2"])
            for _ in select(0):
                pass
            conv_chunks(NCH)
            for kb in ("U16", "V16"):
                for ev in P.all_events(kb):
                    P.need('pool', ev)
            gsl = [0]
            GRP = 2

            def v_finish(tt):
                xr = X1[:, tt, :]
                for q4 in range(4):
                    P.op('dve', lambda e: e.scalar_tensor_tensor(out=xr[:, q4 * 512:(q4 + 1) * 512], in0=xr[:, q4 * 512:(q4 + 1) * 512], scalar=DN_ALPHA, in1=ps[q4][:, 0:512],
                                                                 op0=ALU.mult, op1=ALU.add), reads=[("X1", tt), ("ps", q4)], writes=[("X1", tt)])

            for tt in range(NT):
                gen = select(tt + 1) if tt + 1 < NT else iter(())
                for g0 in range(0, 128, GRP):
                    par = (g0 // GRP) % 2
                    AAp, HAp = AAs[par], HAs[par]
                    slots = []
                    for k in range(g0, g0 + GRP):
                        gsl[0] = (gsl[0] + 1) % NSL
                        sl = gsl[0]
                        slots.append(sl)
                        P.dma('pool', lambda e: e.indirect_dma_start(out=UBs[sl], out_offset=None, in_=uv16,
                                                                     in_offset=bass.IndirectOffsetOnAxis(ap=IDUa[:, tt, k:k + 1], axis=0)),
                              reads=[("IDUa", tt)], writes=["UB%d" % sl])
                        P.op('dve', lambda e: e.scalar_tensor_tensor(out=JK2, in0=UBs[sl][:, 0:D], scalar=1.0, in1=X1[:, tt, :], op0=ALU.mult, op1=ALU.mult, accum_out=AAp[:, k:k + 1]),
                             reads=["UB%d" % sl, ("X1", tt)], writes=[("AA%d" % par, k)])
                        next(gen, None)
                        next(gen, None)
                    P.op('act', lambda e: e.activation(out=HAp[:, g0:g0 + GRP], in_=AAp[:, g0:g0 + GRP], func=AF.Gelu), reads=[("AA%d" % par, kk) for kk in range(g0, g0 + GRP)], writes=["HA%d" % par])
                    P.op('dve', lambda e: e.tensor_tensor(out=HAp[:, g0:g0 + GRP], in0=HAp[:, g0:g0 + GRP], in1=GWa[:, tt, g0:g0 + GRP], op=ALU.mult),
                         reads=["HA%d" % par] + [("GWa", tt, hh_) for hh_ in range(8)], writes=["HA%d" % par])
                    for k, sl in zip(range(g0, g0 + GRP), slots):
                        ds = k % 4
                        P.op('act', lambda e: e.activation(out=DG[:, ds, :], in_=ident_b, func=AF.Copy, scale=HAp[:, k:k + 1]),
                             reads=["cstb", "HA%d" % par], writes=[("DG", ds)])
                        for q4 in range(4):
                            P.op('pe', lambda e: e.matmul(ps[q4][:, 0:512], lhsT=DG[:, ds, :], rhs=UBs[sl][:, D + q4 * 512:D + (q4 + 1) * 512], start=(k == 0), stop=(k == 127)),
                                 reads=[("DG", ds), "UB%d" % sl], writes=[("ps", q4)])
                for _ in gen:
                    pass
                v_finish(tt)
            A.free(*["UB%d" % i for i in range(NSL)])
            A.free("JK2", "DG", "AA0", "AA1", "HA0", "HA1", "QPT", "SC", "TS", "TI", "TIF", "CS", "CI", "BS", "BP", "BPF", "IDF", "JK", "SKT")
            layer_norm_tiles(2, False)

        evs = []
        if stage in (1, 2):
            src_, k_ = (hT, "hT") if stage == 1 else (yT, "yT")
            evs.append(P.dma('pool', lambda e: e.dma_start(out=out.rearrange("(c p) t -> p c t", p=128), in_=src_), reads=[k_]))
        for tt in range(NT if stage >= 3 else 0):
            evs.append(P.dma('sp', lambda e, tt=tt: e.dma_start(out=out[tt * 128:(tt + 1) * 128, :], in_=X1[:, tt, :]), reads=[("X1", tt)]))
        for ev in evs:
            P.need('sp', ev)
        P.emit()
    return nc


def make_in_maps(inp):
    x = np.asarray(inp["x"], np.float32)
    cst0 = np.concatenate([np.eye(128, dtype=np.float32), np.triu(np.ones((128, 128), np.float32)),
                           np.ones((128, 128), np.float32)], 1)
    shared = {
        "w_in": np.ascontiguousarray(inp["w_in"][0]), "conv_w": np.ascontiguousarray(np.asarray(inp["ml_conv_w"][0]).reshape(4, 16, 128).transpose(2, 1, 0).reshape(128, 64)),
        "conv_b": np.ascontiguousarray(np.asarray(inp["ml_conv_b"][0]).reshape(16, 128).T), "gate_b": np.ascontiguousarray(inp["ml_gate_b"][0].reshape(1, 8)),
        "norm_w": np.ascontiguousarray(inp["ml_norm_w"]), "fox_fb": np.ascontiguousarray(inp["fox_f_b"]),
        "w_bml": np.ascontiguousarray(inp["w_branch_ml"][0]), "w_bfx": np.ascontiguousarray(inp["w_branch_fox"][0]),
        "w_out": np.ascontiguousarray(inp["w_out"][0]), "w_xq": np.ascontiguousarray(inp["w_xq"][0]),
        "w_xkv": np.ascontiguousarray(inp["w_xkv"][0]), "w_xo": np.ascontiguousarray(inp["w_xo"][0]),
        "w_pq": np.ascontiguousarray(inp["peer_wq"][0]), "subk": np.ascontiguousarray(inp["peer_sub_keys"][0].reshape(2048, 128)),
        "peer_u": np.ascontiguousarray(inp["peer_u"][0]), "peer_v": np.ascontiguousarray(inp["peer_v"][0]),
        "ln_w": np.ascontiguousarray(inp["ln_w"][0]), "ln_b": np.ascontiguousarray(inp["ln_b"][0]),
    }
    shared = {k: np.asarray(v, np.float32) for k, v in shared.items()}
    maps = []
    for c in range(8):
        b, half = c // 2, c % 2
        m = dict(shared)
        m["x_own"] = np.ascontiguousarray(x[b, half * T:(half + 1) * T])
        m["x_prev"] = np.ascontiguousarray(x[b, 0:T]) if half == 1 else np.zeros((T, D), np.float32)
        m["mem"] = np.ascontiguousarray(np.asarray(inp["mem"], np.float32)[b])
        m["cst"] = np.concatenate([cst0, np.full((128, 128), float(half), np.float32),
                                   np.tile(np.arange(256, dtype=np.float32)[None, :], (128, 1))], 1)
        maps.append(m)
    return maps


def kernel(**inputs):
    nc = build_nc()
    maps = make_in_maps(inputs)
    res = run_bass_kernel_spmd(nc, maps, core_ids=list(range(8)))
    outp = np.zeros((4, 2048, D), np.float32)
    for c in range(8):
        b, half = c // 2, c % 2
        outp[b, half * T:(half + 1) * T] = res.results[c]["out"]
    return outp
```

```python
import math
import numpy as np
from contextlib import ExitStack
import concourse.bass as bass
import concourse.mybir as mybir
from concourse.bass_utils import run_bass_kernel_spmd

F32 = mybir.dt.float32
BF16 = mybir.dt.bfloat16
U32 = mybir.dt.uint32
U8 = mybir.dt.uint8
AF = mybir.ActivationFunctionType
ALU = mybir.AluOpType
AX = mybir.AxisListType
DTSIZE = {F32: 4, BF16: 2, U32: 4, U8: 1}

ENG = ['pe', 'act', 'dve', 'pool', 'sp']
NDS = 32
SAME_ENGINE_SYNC = ('act', 'dve', 'pool')

D = 2048
T = 1024
NT = 8
LN_EPS = 1e-5
DN_ALPHA = 2 ** 0.25
C_MLQ, C_MLK, C_MLV, C_MLO, C_MLI, C_MLF = 0, 1024, 2048, 4096, 6144, 6148
C_FXQ, C_FXK, C_FXV, C_FXF, C_GML, C_GFX = 6152, 8200, 10248, 12296, 12312, 14360
IN_W = 16408


def _base(k):
    return k[0] if isinstance(k, tuple) else k


class _Rec:
    def __init__(self):
        self.calls = []

    def __getattr__(self, name):
        def f(*a, **k):
            self.calls.append((name, a, k))
            return self
        return f


class Prog:
    def __init__(self, nc, es):
        self.nc = nc
        self.q = {e: [] for e in ENG}
        self.sem = {e: es.enter_context(nc.semaphore("s_" + e)) for e in ENG}
        self.cnt = {e: 0 for e in ENG}
        self.dsem = [es.enter_context(nc.semaphore("d%d" % i)) for i in range(NDS)]
        self.dcnt = [0] * NDS
        self.dnext = 0
        self.dnx = [0, 0]
        self.waited = {e: {} for e in ENG}
        self.hist = {}
        self.inherit = {}

    def need(self, eng, ev):
        sem, val = ev
        if sem is self.sem.get(eng) and eng not in SAME_ENGINE_SYNC:
            return
        w = self.waited[eng]
        if w.get(sem.name, 0) >= val:
            return
        w[sem.name] = val
        self.q[eng].append(('wait', sem, val))

    def _deps(self, eng, reads, writes):
        evs = []
        for k in reads:
            h = self.hist.get(k)
            if h and h['w']:
                evs.append(h['w'])
        for k in writes:
            h = self.hist.get(k)
            if h:
                if h['w']:
                    evs.append(h['w'])
                evs.extend(h['r'].values())
            evs.extend(self.inherit.get(_base(k), ()))
        for ev in evs:
            self.need(eng, ev)

    def _record(self, me, reads, writes):
        for k in reads:
            h = self.hist.setdefault(k, {'w': None, 'r': {}})
            h['r'][me[0].name] = me
        for k in writes:
            self.hist[k] = {'w': me, 'r': {}}

    def op(self, eng, fn, reads=(), writes=()):
        self._deps(eng, reads, writes)
        self.cnt[eng] += 1
        me = (self.sem[eng], self.cnt[eng])
        r = _Rec()
        fn(r)
        assert len(r.calls) == 1
        self.q[eng].append(('op', r.calls[0], self.sem[eng], 1))
        self._record(me, reads, writes)
        return me

    def dma(self, eng, fn, reads=(), writes=()):
        half = NDS // 2
        pi = 0 if eng == 'sp' else 1
        i = pi * half + self.dnx[pi]
        self.dnx[pi] = (self.dnx[pi] + 1) % half
        self._deps(eng, reads, writes)
        self.need(eng, (self.dsem[i], self.dcnt[i]))
        self.dcnt[i] += 16
        me = (self.dsem[i], self.dcnt[i])
        r = _Rec()
        fn(r)
        assert len(r.calls) == 1
        self.q[eng].append(('op', r.calls[0], self.dsem[i], 16))
        self._record(me, reads, writes)
        return me

    def all_events(self, keybase):
        evs = []
        for k, h in self.hist.items():
            if _base(k) == keybase:
                if h['w']:
                    evs.append(h['w'])
                evs.extend(h['r'].values())
        evs.extend(self.inherit.get(keybase, ()))
        return evs

    def emit(self):
        nc = self.nc
        q = self.q

        def replay(engobj, e):
            for it in q[e]:
                if it[0] == 'wait':
                    engobj.wait_ge(it[1], it[2])
                else:
                    name, a, k = it[1]
                    ins = getattr(engobj, name)(*a, **k)
                    ins.then_inc(it[2], it[3])

        with nc.Block() as block:
            @block.tensor
            def _(pe):
                replay(pe, 'pe')

            @block.scalar
            def _(act):
                replay(act, 'act')

            @block.vector
            def _(dve):
                replay(dve, 'dve')

            @block.gpsimd
            def _(pool):
                replay(pool, 'pool')

            @block.sync
            def _(sp):
                replay(sp, 'sp')


class Arena:
    def __init__(self, nc, es, prog, nbytes):
        self.prog = prog
        self.cap = nbytes
        self.t = es.enter_context(nc.sbuf_tensor("arena", [128, nbytes], U8))
        self.live = {}
        self.dead = []

    def alloc(self, key, shape, dtype):
        size = (int(np.prod(shape)) * DTSIZE[dtype] + 63) // 64 * 64
        ivs = sorted(self.live.values())
        lo = 0
        for (l2, h2) in ivs:
            if lo + size <= l2:
                break
            lo = max(lo, h2)
        hi = lo + size
        assert hi <= self.cap, ("arena overflow", key, size, sorted((v, k) for k, v in self.live.items()))
        assert key not in self.live
        self.live[key] = (lo, hi)
        best = {}
        nd = []
        for (l2, h2, evs) in self.dead:
            if l2 < hi and lo < h2:
                for (s, v) in evs:
                    if best.get(s.name, (None, 0))[1] < v:
                        best[s.name] = (s, v)
            nd.append((l2, h2, evs))
        self.dead = nd
        self.prog.inherit[key] = list(best.values())
        for k in [k for k in self.prog.hist if _base(k) == key]:
            del self.prog.hist[k]
        ap = self.t[:, lo:lo + int(np.prod(shape)) * DTSIZE[dtype]].bitcast(dtype)
        if len(shape) == 2:
            ap = ap.rearrange("p (a b) -> p a b", a=shape[0], b=shape[1])
        elif len(shape) == 3:
            ap = ap.rearrange("p (a b c) -> p a b c", a=shape[0], b=shape[1], c=shape[2])
        return ap

    def free(self, *keys):
        for key in keys:
            lo, hi = self.live.pop(key)
            self.dead.append((lo, hi, self.prog.all_events(key)))


def build_nc(stage=99):
    nc = bass.Bass("TRN2", target_bir_lowering=False)
    dt_in = lambda n, s: nc.dram_tensor(n, s, F32, kind="ExternalInput").ap()
    x_own = dt_in("x_own", [T, D])
    x_prev = dt_in("x_prev", [T, D])
    mem = dt_in("mem", [256, D])
    cst = dt_in("cst", [128, 6 * 128])
    w_in = dt_in("w_in", [D, IN_W])
    conv_w = dt_in("conv_w", [128, 64])
    conv_b = dt_in("conv_b", [128, 16])
    gate_b = dt_in("gate_b", [1, 8])
    norm_w = dt_in("norm_w", [1, 2048])
    fox_fb = dt_in("fox_fb", [1, 16])
    w_bml = dt_in("w_bml", [D, D])
    w_bfx = dt_in("w_bfx", [D, D])
    w_out = dt_in("w_out", [D, D])
    w_xq = dt_in("w_xq", [D, D])
    w_xkv = dt_in("w_xkv", [D, 2 * D])
    w_xo = dt_in("w_xo", [D, D])
    w_pq = dt_in("w_pq", [D, D])
    subk = dt_in("subk", [16 * 128, 128])
    peer_u = dt_in("peer_u", [16384 if stage >= 5 else 128, D])
    peer_v = dt_in("peer_v", [16384 if stage >= 5 else 128, D])
    ln_w = dt_in("ln_w", [3, D])
    ln_b = dt_in("ln_b", [3, D])
    out = nc.dram_tensor("out", [T, D], F32, kind="ExternalOutput").ap()
    uv16 = nc.dram_tensor("uv16", [16384, 2 * D], BF16).ap() if stage >= 5 else None

    with ExitStack() as es:
        P = Prog(nc, es)
        A = Arena(nc, es, P, 207 * 1024)
        ps = [es.enter_context(nc.psum_tensor("ps%d" % i, [128, 512], F32)) for i in range(8)]
        psb = [p[:].bitcast(BF16) for p in ps]
        rot = [0]

        def nb():
            rot[0] = (rot[0] + 1) % 4
            return rot[0]

        altc = [0]

        def evac_eng():
            altc[0] ^= 1
            return 'act' if altc[0] else 'dve'

        def copy(eng, o, i, reads, writes):
            if eng == 'act':
                P.op('act', lambda e: e.activation(out=o, in_=i, func=AF.Copy), reads=reads, writes=writes)
            else:
                P.op(eng, lambda e: e.tensor_copy(out=o, in_=i), reads=reads, writes=writes)

        cstf = A.alloc("cstf", [6, 128], F32)
        P.dma('sp', lambda e: e.dma_start(out=cstf, in_=cst.rearrange("p (a b) -> p a b", a=6)), writes=["cstf"])
        iota_f = cstf[:, 4:6, :].rearrange("p a b -> p (a b)")
        ident_f, tri_f, ones_f, vfl_f = cstf[:, 0, :], cstf[:, 1, :], cstf[:, 2, :], cstf[:, 3, :]
        cstb = A.alloc("cstb", [4, 128], BF16)
        P.op('dve', lambda e: e.tensor_copy(out=cstb, in_=cstf[:, 0:4, :]), reads=["cstf"], writes=["cstb"])
        ident_b, tri_b, ones_b, vfl_b = cstb[:, 0, :], cstb[:, 1, :], cstb[:, 2, :], cstb[:, 3, :]
        CST = ["cstf", "cstb"]

        convw = A.alloc("convw", [16, 4], F32)
        convb = A.alloc("convb", [16], F32)
        P.dma('sp', lambda e: e.dma_start(out=convw, in_=conv_w.rearrange("p (t j) -> p t j", j=4)), writes=["convw"])
        P.dma('sp', lambda e: e.dma_start(out=convb, in_=conv_b), writes=["convb"])
        gbias = A.alloc("gbias", [24], F32)
        P.dma('sp', lambda e: e.dma_start(out=gbias[:, 0:8], in_=gate_b.broadcast_to([128, 8])), writes=["gbias"])
        P.dma('sp', lambda e: e.dma_start(out=gbias[:, 8:24], in_=fox_fb.broadcast_to([128, 16])), writes=["gbias"])

        wts = [A.alloc("wt%d" % i, [16, 256], BF16) for i in range(2)]
        wslot = [0]

        NCH = 256
        cv = {'next': 0, 'per': 0, 'buf': None}

        def conv_chunks(n):
            if stage < 5 or cv['buf'] is None:
                return
            for _ in range(n):
                c = cv['next']
                if c >= NCH:
                    return
                cv['next'] += 1
                src, dst, key = (peer_u, uv16[:, 0:D], "U16") if c < 128 else (peer_v, uv16[:, D:2 * D], "V16")
                r0 = (c % 128) * 128
                sl = c % 4
                CVS = cv['buf']
                P.dma('pool', lambda e: e.dma_start(out=CVS[:, sl, :], in_=src[r0:r0 + 128, :]), writes=[("CVS", sl)])
                P.dma('sp', lambda e: e.dma_start(out=dst[r0:r0 + 128, :], in_=CVS[:, sl, :]), reads=[("CVS", sl)], writes=[(key, c % 128)])

        def load_w(parts):
            conv_chunks(cv['per'])
            s = wslot[0]
            wslot[0] = (s + 1) % len(wts)
            off = 0
            for (src, n) in parts:
                P.dma('pool', lambda e, src=src, n=n, off=off: e.dma_start(
                    out=wts[s][:, :, off:off + n], in_=src.rearrange("(c p) n -> p c n", p=128)),
                    writes=["wt%d" % s])
                off += n
            return s

        def proj_fm(s, c0, xT, xkey, t0, tn, evac):
            b = nb()
            for c in range(16):
                P.op('pe', lambda e, c=c: e.matmul(ps[b][:, 0:tn], lhsT=wts[s][:, c, c0:c0 + 128], rhs=xT[:, c, t0:t0 + tn],
                                                    start=(c == 0), stop=(c == 15)),
                     reads=["wt%d" % s, xkey], writes=[("ps", b)])
            evac(b, ps[b][:, 0:tn])

        def proj_tm(s, n, xT, xkey, tt, evac):
            b = nb()
            for c in range(16):
                P.op('pe', lambda e, c=c: e.matmul(ps[b][:, 0:n], lhsT=xT[:, c, tt * 128:(tt + 1) * 128], rhs=wts[s][:, c, 0:n],
                                                    start=(c == 0), stop=(c == 15)),
                     reads=["wt%d" % s, xkey], writes=[("ps", b)])
            evac(b, ps[b][:, 0:n])

        xTp = A.alloc("xTp", [16, T], BF16)
        xTo = A.alloc("xTo", [16, T], BF16)
        xTh = A.alloc("xTh", [16, 4], BF16)

        def load_xT(src, rows, dst, dkey, srckey_reads=()):
            xt = A.alloc("xt", [2, D], F32)
            for tt in range(rows // 128):
                sl = tt % 2
                P.dma('sp', lambda e, tt=tt, sl=sl: e.dma_start(out=xt[:, sl, :], in_=src[tt * 128:(tt + 1) * 128, :]),
                      writes=[("xt", sl)])
                for g in range(4):
                    b = nb()
                    for j in range(4):
                        c = g * 4 + j
                        P.op('pe', lambda e, c=c, j=j, b=b, sl=sl: e.transpose(ps[b][:, j * 128:(j + 1) * 128], xt[:, sl, c * 128:(c + 1) * 128], ident_f),
                             reads=[("xt", sl), "cstf"], writes=[("ps", b)])
                    copy(evac_eng(), dst[:, 4 * g:4 * g + 4, tt * 128:(tt + 1) * 128],
                         ps[b][:].rearrange("p (a b) -> p a b", a=4), [("ps", b)], [dkey])
            A.free("xt")

        load_xT(x_prev, T, xTp, "xTp")
        P.op('dve', lambda e: e.tensor_copy(out=xTh[:, :, 0:3], in_=xTp[:, :, T - 3:T]), reads=["xTp"], writes=["xTh"])
        load_xT(x_own, T, xTo, "xTo")

        G = A.alloc("G", [16, 24], F32)
        sg = load_w([(w_in[:, C_MLI:C_MLI + 8], 8), (w_in[:, C_FXF:C_FXF + 16], 16)])
        for tt in range(16):
            xT, xk = (xTp, "xTp") if tt < 8 else (xTo, "xTo")
            proj_tm(sg, 24, xT, xk, tt % 8,
                    lambda b, ap, tt=tt: P.op('dve', lambda e: e.tensor_tensor(out=G[:, tt, :], in0=ap, in1=gbias, op=ALU.add),
                                              reads=[("ps", b), "gbias"], writes=["G"]))
        LF = A.alloc("LF", [16, 20], F32)
        P.op('act', lambda e: e.activation(out=LF, in_=G[:, :, 4:24], func=AF.Exp, scale=-1.0), reads=["G"], writes=["LF"])
        P.op('act', lambda e: e.activation(out=LF, in_=LF, func=AF.Ln, bias=1.0), reads=["LF"], writes=["LF"])
        P.op('dve', lambda e: e.tensor_scalar(out=LF, in0=LF, scalar1=-1.0, scalar2=None, op0=ALU.mult), reads=["LF"], writes=["LF"])
        BC = A.alloc("BC", [16, 20], F32)
        TOT = A.alloc("TOT", [16, 20], F32)
        P.op('pe', lambda e: e.matmul(ps[4][:, 0:320], lhsT=tri_f, rhs=LF.rearrange("p a b -> p (a b)"), start=True, stop=True),
             reads=["LF", "cstf"], writes=[("ps", 4)])
        P.op('pe', lambda e: e.matmul(ps[5][:, 0:320], lhsT=ones_f, rhs=LF.rearrange("p a b -> p (a b)"), start=True, stop=True),
             reads=["LF", "cstf"], writes=[("ps", 5)])
        P.op('dve', lambda e: e.tensor_copy(out=BC.rearrange("p a b -> p (a b)"), in_=ps[4][:, 0:320]), reads=[("ps", 4)], writes=["BC"])
        P.op('dve', lambda e: e.tensor_copy(out=TOT.rearrange("p a b -> p (a b)"), in_=ps[5][:, 0:320]), reads=[("ps", 5)], writes=["TOT"])
        UP = A.alloc("UP", [16, 4], F32)
        WW = A.alloc("WW", [16, 4], F32)
        GD = A.alloc("GD", [16, 4], F32)
        P.op('dve', lambda e: e.tensor_tensor(out=UP, in0=G[:, :, 0:4], in1=BC[:, :, 0:4], op=ALU.subtract), reads=["G", "BC"], writes=["UP"])
        P.op('act', lambda e: e.activation(out=UP, in_=UP, func=AF.Exp, bias=-math.log(16.0)), reads=["UP"], writes=["UP"])
        P.op('act', lambda e: e.activation(out=WW, in_=BC[:, :, 0:4], func=AF.Exp), reads=["BC"], writes=["WW"])
        P.op('act', lambda e: e.activation(out=GD, in_=TOT[:, :, 0:4], func=AF.Exp), reads=["TOT"], writes=["GD"])
        GC = A.alloc("GC", [17, 16], F32)
        FG = A.alloc("FG", [16, 16], F32)
        P.op('dve', lambda e: e.memset(GC[:, 0, :], 0.0), writes=["GC"])
        for j in range(16):
            P.op('dve', lambda e, j=j: e.tensor_tensor(out=GC[:, j + 1, :], in0=GC[:, j, :], in1=TOT[:, j, 4:20], op=ALU.add),
                 reads=["GC", "TOT"], writes=["GC"])
        P.op('dve', lambda e: e.tensor_tensor(out=FG, in0=GC[:, 0:16, :], in1=BC[:, :, 4:20], op=ALU.add), reads=["GC", "BC"], writes=["FG"])
        A.free("G", "LF", "TOT")

        hT = A.alloc("hT", [16, T], BF16)
        yT = A.alloc("yT", [16, T], BF16)

        normw = A.alloc("normw", [2048], F32)
        P.dma('sp', lambda e: e.dma_start(out=normw, in_=norm_w.broadcast_to([128, 2048])), writes=["normw"])
        QT = A.alloc("QT", [2, T], BF16)
        KT = A.alloc("KT", [2, T], BF16)
        VA = A.alloc("VA", [8, 520], BF16)
        SO = A.alloc("SO", [8, 512], BF16)
        PRE = A.alloc("PRE", [T + 4], F32)
        ACC = A.alloc("ACC", [T], F32)
        KH = A.alloc("KH", [2, 4], F32)
        CF = A.alloc("CF", [2, 520], F32)
        CB = A.alloc("CB", [2, 520], BF16)
        MT = A.alloc("MT", [128], BF16)
        KU = A.alloc("KU", [256], BF16)
        HH = A.alloc("HH", [512], F32)
        HB = A.alloc("HB", [512], BF16)
        SM = A.alloc("SM", [16], F32)
        ST = A.alloc("ST", [6], F32)

        def convsilu(xT, xk, col0, ct, dst, dkey, dc, halo):
            raise NotImplementedError

        def mlstm_qk(h, own):
            xT, xk = (xTo, "xTo") if own else (xTp, "xTp")
            todo = ([("q", C_MLQ, 0, QT, "QT")] if own else []) + [("k", C_MLK, 8, KT, "KT")]
            for (nm, cb, ctb, dst, dkey) in todo:
                s = load_w([(w_in[:, cb + h * 256: cb + (h + 1) * 256], 256)])
                for dc in range(2):
                    ct = ctb + h * 2 + dc
                    if not own:
                        P.op('dve', lambda e: e.memset(PRE[:, 0:3], 0.0), writes=["PRE"])
                    elif nm == "k":
                        P.op('dve', lambda e, dc=dc: e.tensor_copy(out=PRE[:, 0:3], in_=KH[:, dc, 0:3]), reads=["KH"], writes=["PRE"])
                    else:
                        b = nb()
                        for c in range(16):
                            P.op('pe', lambda e, c=c, dc=dc, b=b: e.matmul(ps[b][:, 0:3], lhsT=wts[s][:, c, dc * 128:(dc + 1) * 128], rhs=xTh[:, c, 0:3],
                                                                           start=(c == 0), stop=(c == 15)),
                                 reads=["wt%d" % s, "xTh"], writes=[("ps", b)])
                        P.op('dve', lambda e, b=b: e.tensor_copy(out=PRE[:, 0:3], in_=ps[b][:, 0:3]), reads=[("ps", b)], writes=["PRE"])
                    for tb in range(2):
                        proj_fm(s, dc * 128, xT, xk, tb * 512, 512,
                                lambda b, ap, tb=tb: copy('act', PRE[:, 3 + tb * 512: 3 + (tb + 1) * 512], ap, [("ps", b)], ["PRE"]))
                    if nm == "k" and not own:
                        P.op('dve', lambda e, dc=dc: e.tensor_copy(out=KH[:, dc, 0:3], in_=PRE[:, T:T + 3]), reads=["PRE"], writes=["KH"])
                    P.op('dve', lambda e, ct=ct: e.tensor_scalar(out=ACC, in0=PRE[:, 0:T], scalar1=convw[:, ct, 0:1], scalar2=None, op0=ALU.mult),
                         reads=["PRE", "convw"], writes=["ACC"])
                    for j in range(1, 4):
                        P.op('dve', lambda e, ct=ct, j=j: e.scalar_tensor_tensor(out=ACC, in0=PRE[:, j:T + j], scalar=convw[:, ct, j:j + 1], in1=ACC,
                                                                              op0=ALU.mult, op1=ALU.add),
                             reads=["PRE", "convw", "ACC"], writes=["ACC"])
                    P.op('act', lambda e, ct=ct, dc=dc, dst=dst: e.activation(out=dst[:, dc, :], in_=ACC, func=AF.Silu, bias=convb[:, ct:ct + 1]),
                         reads=["ACC", "convb"], writes=[dkey])

        def mlstm_v(h, own):
            xT, xk = (xTo, "xTo") if own else (xTp, "xTp")
            for hf in range(2):
                s = load_w([(w_in[:, C_MLV + h * 512 + hf * 256: C_MLV + h * 512 + (hf + 1) * 256], 256)])
                for tt in range(8):
                    proj_tm(s, 256, xT, xk, tt,
                            lambda b, ap, tt=tt, hf=hf: copy(evac_eng(), VA[:, tt, hf * 256:(hf + 1) * 256], ap, [("ps", b)], ["VA"]))
            if own:
                P.op('dve', lambda e: e.memset(VA[:, :, 512:513], 1.0), writes=["VA"])
            else:
                for tt in range(8):
                    P.op('dve', lambda e, tt=tt: e.tensor_copy(out=VA[:, tt, 512:513], in_=vfl_f[:, 0:1]), reads=["cstf"], writes=["VA"])

        def mlstm_o(h):
            for hf in range(2):
                s = load_w([(w_in[:, C_MLO + h * 512 + hf * 256: C_MLO + h * 512 + (hf + 1) * 256], 256)])
                for tt in range(8):
                    proj_tm(s, 256, xTo, "xTo", tt,
                            lambda b, ap, tt=tt, hf=hf: P.op('act', lambda e: e.activation(out=SO[:, tt, hf * 256:(hf + 1) * 256], in_=ap, func=AF.Sigmoid),
                                                             reads=[("ps", b)], writes=["SO"]))

        def mlstm_scan(h, own):
            for j in range(8):
                gj = j + (8 if own else 0)
                cs = slice(j * 128, (j + 1) * 128)
                if own:
                    for dc in range(2):
                        P.op('pe', lambda e, dc=dc: e.matmul(ps[4][:, 0:128], lhsT=KT[:, dc, cs], rhs=QT[:, dc, cs], start=(dc == 0), stop=(dc == 1)),
                             reads=["KT", "QT"], writes=[("ps", 4)])
                    P.op('dve', lambda e: e.scalar_tensor_tensor(out=MT, in0=ps[4][:, 0:128], scalar=UP[:, gj, h:h + 1], in1=tri_f, op0=ALU.mult, op1=ALU.mult),
                         reads=[("ps", 4), "UP", "cstf"], writes=["MT"])
                    P.op('pe', lambda e: e.matmul(ps[5][:, 0:512], lhsT=MT, rhs=VA[:, j, 0:512], start=True, stop=False),
                         reads=["MT", "VA"], writes=[("ps", 5)])
                    for dc in range(2):
                        P.op('pe', lambda e, dc=dc: e.matmul(ps[5][:, 0:512], lhsT=QT[:, dc, cs], rhs=CB[:, dc, 0:512], start=False, stop=(dc == 1)),
                             reads=["QT", "CB"], writes=[("ps", 5)])
                    P.op('pe', lambda e: e.matmul(ps[6][:, 0:1], lhsT=MT, rhs=VA[:, j, 512:513], start=True, stop=False),
                         reads=["MT", "VA"], writes=[("ps", 6)])
                    for dc in range(2):
                        P.op('pe', lambda e, dc=dc: e.matmul(ps[6][:, 0:1], lhsT=QT[:, dc, cs], rhs=CB[:, dc, 512:513], start=False, stop=(dc == 1)),
                             reads=["QT", "CB"], writes=[("ps", 6)])
                    P.op('dve', lambda e: e.tensor_tensor(out=SM[:, 0:1], in0=ps[6][:, 0:1], in1=WW[:, gj, h:h + 1], op=ALU.mult),
                         reads=[("ps", 6), "WW"], writes=["SM"])
                    P.op('act', lambda e: e.activation(out=SM[:, 1:2], in_=SM[:, 0:1], func=AF.Abs), reads=["SM"], writes=["SM"])
                    P.op('dve', lambda e: e.tensor_scalar(out=SM[:, 1:2], in0=SM[:, 1:2], scalar1=1.0, scalar2=None, op0=ALU.max), reads=["SM"], writes=["SM"])
                    P.op('dve', lambda e: e.reciprocal(out=SM[:, 2:3], in_=SM[:, 1:2]), reads=["SM"], writes=["SM"])
                    P.op('dve', lambda e: e.tensor_tensor(out=SM[:, 3:4], in0=SM[:, 2:3], in1=WW[:, gj, h:h + 1], op=ALU.mult), reads=["SM", "WW"], writes=["SM"])
                    P.op('dve', lambda e: e.tensor_scalar(out=HH, in0=ps[5][:, 0:512], scalar1=SM[:, 3:4], scalar2=None, op0=ALU.mult),
                         reads=[("ps", 5), "SM"], writes=["HH"])
                    P.op('dve', lambda e: e.bn_stats(out=ST, in_=HH), reads=["HH"], writes=["ST"])
                    P.op('dve', lambda e: e.bn_aggr(out=SM[:, 4:6], in_=ST), reads=["ST"], writes=["SM"])
                    P.op('act', lambda e: e.activation(out=SM[:, 6:7], in_=SM[:, 5:6], func=AF.Sqrt, bias=LN_EPS), reads=["SM"], writes=["SM"])
                    P.op('dve', lambda e: e.reciprocal(out=SM[:, 7:8], in_=SM[:, 6:7]), reads=["SM"], writes=["SM"])
                    P.op('dve', lambda e: e.tensor_scalar(out=HH, in0=HH, scalar1=SM[:, 4:5], scalar2=SM[:, 7:8], op0=ALU.subtract, op1=ALU.mult),
                         reads=["HH", "SM"], writes=["HH"])
                    P.op('dve', lambda e: e.tensor_tensor(out=HH, in0=HH, in1=normw[:, h * 512:(h + 1) * 512], op=ALU.mult), reads=["HH", "normw"], writes=["HH"])
                    P.op('dve', lambda e: e.tensor_tensor(out=HB, in0=HH, in1=SO[:, j, :], op=ALU.mult), reads=["HH", "SO"], writes=["HB"])
                    for q4 in range(4):
                        P.op('pe', lambda e, q4=q4: e.transpose(psb[7][:, q4 * 128:(q4 + 1) * 128], HB[:, q4 * 128:(q4 + 1) * 128], ident_b),
                             reads=["HB", "cstb"], writes=[("ps", 7)])
                    copy('act', hT[:, h * 4:(h + 1) * 4, cs], psb[7][:, 0:512].rearrange("p (a b) -> p a b", a=4), [("ps", 7)], ["hT"])
                    if j == 7:
                        continue
                b = nb()
                for dc in range(2):
                    P.op('pe', lambda e, dc=dc, b=b: e.transpose(psb[b][:, dc * 128:(dc + 1) * 128], KT[:, dc, cs], ident_b),
                         reads=["KT", "cstb"], writes=[("ps", b)])
                P.op('dve', lambda e, b=b: e.tensor_scalar(out=KU, in0=psb[b][:, 0:256], scalar1=UP[:, gj, h:h + 1], scalar2=None, op0=ALU.mult),
                     reads=[("ps", b), "UP"], writes=["KU"])
                first = (not own) and j == 0
                for dc in range(2):
                    b = nb()
                    P.op('pe', lambda e, dc=dc, b=b: e.matmul(ps[b][:, 0:512], lhsT=KU[:, dc * 128:(dc + 1) * 128], rhs=VA[:, j, 0:512], start=True, stop=True),
                         reads=["KU", "VA"], writes=[("ps", b)])
                    P.op('pe', lambda e, dc=dc: e.matmul(ps[6][:, 8 + dc:9 + dc], lhsT=KU[:, dc * 128:(dc + 1) * 128], rhs=VA[:, j, 512:513], start=True, stop=True),
                         reads=["KU", "VA"], writes=[("ps", 6)])
                    if first:
                        P.op('dve', lambda e, dc=dc, b=b: e.tensor_scalar(out=CF[:, dc, 0:512], in0=ps[b][:, 0:512], scalar1=GD[:, gj, h:h + 1], scalar2=None, op0=ALU.mult),
                             reads=[("ps", b), "GD"], writes=["CF"])
                        P.op('dve', lambda e, dc=dc: e.tensor_scalar(out=CF[:, dc, 512:513], in0=ps[6][:, 8 + dc:9 + dc], scalar1=GD[:, gj, h:h + 1], scalar2=None, op0=ALU.mult),
                             reads=[("ps", 6), "GD"], writes=["CF"])
                    else:
                        P.op('dve', lambda e, dc=dc: e.tensor_scalar(out=CF[:, dc, 0:513], in0=CF[:, dc, 0:513], scalar1=GD[:, gj, h:h + 1], scalar2=None, op0=ALU.mult),
                             reads=["CF", "GD"], writes=["CF"])
                        P.op('dve', lambda e, dc=dc, b=b: e.scalar_tensor_tensor(out=CF[:, dc, 0:512], in0=ps[b][:, 0:512], scalar=GD[:, gj, h:h + 1], in1=CF[:, dc, 0:512],
                                                                               op0=ALU.mult, op1=ALU.add),
                             reads=[("ps", b), "GD", "CF"], writes=["CF"])
                        P.op('dve', lambda e, dc=dc: e.scalar_tensor_tensor(out=CF[:, dc, 512:513], in0=ps[6][:, 8 + dc:9 + dc], scalar=GD[:, gj, h:h + 1], in1=CF[:, dc, 512:513],
                                                                          op0=ALU.mult, op1=ALU.add),
                             reads=[("ps", 6), "GD", "CF"], writes=["CF"])
                copy('act', CB[:, :, 0:513], CF[:, :, 0:513], ["CF"], ["CB"])

        if stage >= 1:
            for h in range(4):
                mlstm_qk(h, False)
                mlstm_v(h, False)
                mlstm_scan(h, False)
                mlstm_qk(h, True)
                mlstm_v(h, True)
                mlstm_o(h)
                mlstm_scan(h, True)
        A.free("QT", "KT", "VA", "SO", "PRE", "ACC", "KH", "CF", "CB", "MT", "KU", "HH", "HB", "ST", "SM", "normw")

        if stage >= 5:
            cv['buf'] = A.alloc("CVS", [4, D], BF16)
            cv['per'] = 5
        wts.append(A.alloc("wt2", [16, 256], BF16))
        def fox_group(g):
            FK = A.alloc("FK", [2, 2 * T], BF16)
            FV = A.alloc("FV", [16, 256], BF16)
            FQ = A.alloc("FQ", [2, T], BF16)
            PT = A.alloc("PT", [4, 128], BF16)
            FB = A.alloc("FB", [2, 8, 16], F32)
            RC = A.alloc("RC", [128], F32)
            c0 = g * 256
            for (xT, xk, tb0) in ((xTp, "xTp", 0), (xTo, "xTo", 8)):
                s = load_w([(w_in[:, C_FXK + c0:C_FXK + c0 + 256], 256)])
                for hh in range(2):
                    for tb in range(2):
                        proj_fm(s, hh * 128, xT, xk, tb * 512, 512,
                                lambda b, ap, hh=hh, tb=tb, tb0=tb0: copy(evac_eng(), FK[:, hh, tb0 * 128 + tb * 512: tb0 * 128 + (tb + 1) * 512], ap, [("ps", b)], ["FK"]))
                s = load_w([(w_in[:, C_FXV + c0:C_FXV + c0 + 256], 256)])
                for tt in range(8):
                    proj_tm(s, 256, xT, xk, tt,
                            lambda b, ap, tt=tt, tb0=tb0: copy(evac_eng(), FV[:, tb0 + tt, :], ap, [("ps", b)], ["FV"]))
            s = load_w([(w_in[:, C_FXQ + c0:C_FXQ + c0 + 256], 256)])
            for hh in range(2):
                for tb in range(2):
                    proj_fm(s, hh * 128, xTo, "xTo", tb * 512, 512,
                            lambda b, ap, hh=hh, tb=tb: copy(evac_eng(), FQ[:, hh, tb * 512:(tb + 1) * 512], ap, [("ps", b)], ["FQ"]))
            scale = 128.0 ** -0.5
            LOOK = 2
            for hh in range(2):
                h = g * 2 + hh
                for jq in range(8):
                    gq = 8 + jq
                    P.op('dve', lambda e: e.tensor_scalar(out=FB[:, hh, jq, 0:gq + 1], in0=FG[:, 0:gq + 1, h], scalar1=-1.0, scalar2=GC[:, gq, h:h + 1], op0=ALU.mult, op1=ALU.add),
                         reads=["FG", "GC"], writes=[("FB", hh)])
                for jq in range(8):
                    gq = 8 + jq
                    qs = slice(jq * 128, (jq + 1) * 128)
                    acc = 4 + (jq % 2)
                    n = gq + 1

                    def stage_a(jk):
                        b = nb()
                        sl = jk % 4
                        P.op('pe', lambda e: e.matmul(ps[b][:, 0:128], lhsT=FK[:, hh, jk * 128:(jk + 1) * 128], rhs=FQ[:, hh, qs], start=True, stop=True),
                             reads=["FK", "FQ"], writes=[("ps", b)])
                        P.op('act', lambda e: e.activation(out=PT[:, sl, :], in_=ps[b][:, 0:128], func=AF.Exp, bias=FB[:, hh, jq, jk:jk + 1], scale=scale),
                             reads=[("ps", b), ("FB", hh)], writes=[("PT", sl)])
                        if jk == gq:
                            P.op('dve', lambda e: e.tensor_tensor(out=PT[:, sl, :], in0=PT[:, sl, :], in1=tri_b, op=ALU.mult),
                                 reads=[("PT", sl), "cstb"], writes=[("PT", sl)])

                    def stage_b(jk):
                        sl = jk % 4
                        P.op('pe', lambda e: e.matmul(ps[acc][:, 0:128], lhsT=FV[:, jk, hh * 128:(hh + 1) * 128], rhs=PT[:, sl, :], start=(jk == 0), stop=(jk == gq)),
                             reads=["FV", ("PT", sl)], writes=[("ps", acc)])
                        P.op('pe', lambda e: e.matmul(ps[acc + 2][:, 0:128], lhsT=(vfl_b if jk < 8 else ones_b), rhs=PT[:, sl, :], start=(jk == 0), stop=(jk == gq)),
                             reads=["cstb", ("PT", sl)], writes=[("ps", acc + 2)])

                    for i in range(n + LOOK):
                        if i < n:
                            stage_a(i)
                        if i >= LOOK:
                            stage_b(i - LOOK)
                    P.op('dve', lambda e: e.reciprocal(out=RC, in_=ps[acc + 2][:, 0:128]), reads=[("ps", acc + 2)], writes=["RC"])
                    P.op('dve', lambda e: e.tensor_tensor(out=yT[:, h, qs], in0=ps[acc][:, 0:128], in1=RC, op=ALU.mult), reads=[("ps", acc), "RC"], writes=["yT"])
            A.free("FK", "FV", "FQ", "PT", "FB", "RC")

        if stage >= 2:
            for g in range(8):
                fox_group(g)
        A.free("xTp", "xTh", "GC", "FG", "BC", "UP", "WW", "GD")

        if stage in (1, 2):
            src_, k_ = (hT, "hT") if stage == 1 else (yT, "yT")
            ev = P.dma('pool', lambda e: e.dma_start(out=out.rearrange("(c p) t -> p c t", p=128), in_=src_), reads=[k_])
            P.need('pool', ev)
            P.emit()
            return nc
        cv['per'] = 1
        mT = A.alloc("mT", [16, T], BF16)
        SG = A.alloc("SG", [2, T], BF16)
        T1 = A.alloc("T1", [512], F32)
        T2 = A.alloc("T2", [512], F32)
        if stage >= 3:
            for c in range(16):
                s = load_w([(w_in[:, C_GML + c * 128:C_GML + (c + 1) * 128], 128), (w_in[:, C_GFX + c * 128:C_GFX + (c + 1) * 128], 128)])
                for which in range(2):
                    for tb in range(2):
                        proj_fm(s, which * 128, xTo, "xTo", tb * 512, 512,
                                lambda b, ap, which=which, tb=tb: P.op('act', lambda e: e.activation(out=SG[:, which, tb * 512:(tb + 1) * 512], in_=ap, func=AF.Sigmoid),
                                                                       reads=[("ps", b)], writes=["SG"]))
                s = load_w([(w_bml[:, c * 128:(c + 1) * 128], 128), (w_bfx[:, c * 128:(c + 1) * 128], 128)])
                for tb in range(2):
                    proj_fm(s, 0, hT, "hT", tb * 512, 512,
                            lambda b, ap, tb=tb: P.op('dve', lambda e: e.tensor_tensor(out=T1, in0=ap, in1=SG[:, 0, tb * 512:(tb + 1) * 512], op=ALU.mult),
                                                      reads=[("ps", b), "SG"], writes=["T1"]))
                    proj_fm(s, 128, yT, "yT", tb * 512, 512,
                            lambda b, ap, tb=tb: P.op('dve', lambda e: e.tensor_tensor(out=T2, in0=ap, in1=SG[:, 1, tb * 512:(tb + 1) * 512], op=ALU.mult),
                                                      reads=[("ps", b), "SG"], writes=["T2"]))
                    P.op('dve', lambda e, c=c, tb=tb: e.tensor_tensor(out=mT[:, c, tb * 512:(tb + 1) * 512], in0=T1, in1=T2, op=ALU.add),
                         reads=["T1", "T2"], writes=["mT"])
        A.free("SG", "T1", "T2", "xTo")
        if stage >= 3:
            A.free("hT", "yT")

        X1 = A.alloc("X1", [NT, D], F32)
        X1T = A.alloc("X1T", [16, T], BF16)
        ST4 = A.alloc("ST4", [4, 6], F32)
        SM2 = A.alloc("SM2", [8], F32)

        def layer_norm_tiles(li, make_T):
            lnw = A.alloc("lnw", [D], F32)
            lnb = A.alloc("lnb", [D], F32)
            P.dma('sp', lambda e: e.dma_start(out=lnw, in_=ln_w[li:li + 1, :].broadcast_to([128, D])), writes=["lnw"])
            P.dma('sp', lambda e: e.dma_start(out=lnb, in_=ln_b[li:li + 1, :].broadcast_to([128, D])), writes=["lnb"])
            for tt in range(NT):
                xk = ("X1", tt)
                xr = X1[:, tt, :]
                for q4 in range(4):
                    P.op('dve', lambda e, q4=q4: e.bn_stats(out=ST4[:, q4, :], in_=xr[:, q4 * 512:(q4 + 1) * 512]), reads=[xk], writes=["ST4"])
                P.op('dve', lambda e: e.bn_aggr(out=SM2[:, 0:2], in_=ST4.rearrange("p a b -> p (a b)")), reads=["ST4"], writes=["SM2"])
                P.op('act', lambda e: e.activation(out=SM2[:, 2:3], in_=SM2[:, 1:2], func=AF.Sqrt, bias=LN_EPS), reads=["SM2"], writes=["SM2"])
                P.op('dve', lambda e: e.reciprocal(out=SM2[:, 3:4], in_=SM2[:, 2:3]), reads=["SM2"], writes=["SM2"])
                P.op('dve', lambda e: e.tensor_scalar(out=xr, in0=xr, scalar1=SM2[:, 0:1], scalar2=SM2[:, 3:4], op0=ALU.subtract, op1=ALU.mult),
                     reads=[xk, "SM2"], writes=[xk])
                P.op('dve', lambda e: e.tensor_tensor(out=xr, in0=xr, in1=lnw, op=ALU.mult), reads=[xk, "lnw"], writes=[xk])
                P.op('dve', lambda e: e.tensor_tensor(out=xr, in0=xr, in1=lnb, op=ALU.add), reads=[xk, "lnb"], writes=[xk])
                if make_T:
                    for g in range(4):
                        b = nb()
                        for j in range(4):
                            c = g * 4 + j
                            P.op('pe', lambda e, c=c, j=j, b=b: e.transpose(ps[b][:, j * 128:(j + 1) * 128], xr[:, c * 128:(c + 1) * 128], ident_f),
                                 reads=[xk, "cstf"], writes=[("ps", b)])
                        copy('act', X1T[:, 4 * g:4 * g + 4, tt * 128:(tt + 1) * 128], ps[b][:].rearrange("p (a b) -> p a b", a=4), [("ps", b)], ["X1T"])
            A.free("lnw", "lnb")

        def out_proj_residual(W, inT, inkey, first):
            if first:
                for tt in range(NT):
                    P.dma('sp', lambda e, tt=tt: e.dma_start(out=X1[:, tt, :], in_=x_own[tt * 128:(tt + 1) * 128, :]), writes=[("X1", tt)])
            for cb in range(8):
                s = load_w([(W[:, cb * 256:(cb + 1) * 256], 256)])
                for tt in range(NT):
                    proj_tm(s, 256, inT, inkey, tt,
                            lambda b, ap, tt=tt, cb=cb: P.op('dve', lambda e: e.scalar_tensor_tensor(
                                out=X1[:, tt, cb * 256:(cb + 1) * 256], in0=X1[:, tt, cb * 256:(cb + 1) * 256], scalar=DN_ALPHA, in1=ap, op0=ALU.mult, op1=ALU.add),
                                reads=[("ps", b), ("X1", tt)], writes=[("X1", tt)]))

        if stage >= 3:
            out_proj_residual(w_out, mT, "mT", True)
            layer_norm_tiles(0, True)
        A.free("mT")

        if stage >= 5:
            conv_chunks(NCH)
            A.free("CVS")
            cv['buf'] = None
        A.free("wt2")
        wts.pop()
        wslot[0] %= len(wts)
        if stage >= 4:
            QXT = A.alloc("QXT", [16, T], BF16)
            OXT = A.alloc("OXT", [16, T], BF16)
            memT = A.alloc("memT", [16, 256], BF16)
            load_xT(mem, 256, memT, "memT")
            KXT = A.alloc("KXT", [16, 256], BF16)
            VX = A.alloc("VX", [2, D], BF16)
            for cb in range(8):
                s = load_w([(w_xkv[:, cb * 256:(cb + 1) * 256], 256)])
                for hh in range(2):
                    proj_fm(s, hh * 128, memT, "memT", 0, 256,
                            lambda b, ap, cb=cb, hh=hh: copy(evac_eng(), KXT[:, cb * 2 + hh, :], ap, [("ps", b)], ["KXT"]))
            for cb in range(8):
                s = load_w([(w_xkv[:, D + cb * 256:D + (cb + 1) * 256], 256)])
                for mt in range(2):
                    proj_tm(s, 256, memT, "memT", mt,
                            lambda b, ap, cb=cb, mt=mt: copy(evac_eng(), VX[:, mt, cb * 256:(cb + 1) * 256], ap, [("ps", b)], ["VX"]))
            A.free("memT")
            for cb in range(8):
                s = load_w([(w_xq[:, cb * 256:(cb + 1) * 256], 256)])
                for hh in range(2):
                    for tb in range(2):
                        proj_fm(s, hh * 128, X1T, "X1T", tb * 512, 512,
                                lambda b, ap, cb=cb, hh=hh, tb=tb: copy(evac_eng(), QXT[:, cb * 2 + hh, tb * 512:(tb + 1) * 512], ap, [("ps", b)], ["QXT"]))
            PX = A.alloc("PX", [2, 512], BF16)
            RX = A.alloc("RX", [512], F32)
            xscale = 512.0 ** -0.5
            for h in range(4):
                for tb in range(2):
                    ts_ = slice(tb * 512, (tb + 1) * 512)
                    for mt in range(2):
                        b = nb()
                        for dc in range(4):
                            P.op('pe', lambda e, dc=dc, b=b, mt=mt: e.matmul(ps[b][:, 0:512], lhsT=KXT[:, h * 4 + dc, mt * 128:(mt + 1) * 128], rhs=QXT[:, h * 4 + dc, ts_],
                                                                             start=(dc == 0), stop=(dc == 3)),
                                 reads=["KXT", "QXT"], writes=[("ps", b)])
                        P.op('act', lambda e, b=b, mt=mt: e.activation(out=PX[:, mt, :], in_=ps[b][:, 0:512], func=AF.Exp, scale=xscale),
                             reads=[("ps", b)], writes=[("PX", mt)])
                    for mt in range(2):
                        P.op('pe', lambda e, mt=mt: e.matmul(ps[4][:, 0:512], lhsT=ones_b, rhs=PX[:, mt, :], start=(mt == 0), stop=(mt == 1)),
                             reads=["cstb", ("PX", mt)], writes=[("ps", 4)])
                    P.op('dve', lambda e: e.reciprocal(out=RX, in_=ps[4][:, 0:512]), reads=[("ps", 4)], writes=["RX"])
                    for dc in range(4):
                        b = 5 + (dc % 2)
                        for mt in range(2):
                            P.op('pe', lambda e, dc=dc, mt=mt, b=b: e.matmul(ps[b][:, 0:512], lhsT=VX[:, mt, (h * 4 + dc) * 128:(h * 4 + dc + 1) * 128], rhs=PX[:, mt, :],
                                                                             start=(mt == 0), stop=(mt == 1)),
                                 reads=["VX", ("PX", 0), ("PX", 1)], writes=[("ps", b)])
                        P.op('dve', lambda e, dc=dc, b=b: e.tensor_tensor(out=OXT[:, h * 4 + dc, ts_], in0=ps[b][:, 0:512], in1=RX, op=ALU.mult),
                             reads=[("ps", b), "RX"], writes=["OXT"])
            A.free("KXT", "VX", "QXT", "PX", "RX")
            out_proj_residual(w_xo, OXT, "OXT", False)
            A.free("OXT")
            layer_norm_tiles(1, True)

        if stage >= 5:
            SKT = A.alloc("SKT", [16, 128], BF16)
            skt = A.alloc("skt", [2, 128], F32)
            for j in range(16):
                P.dma('sp', lambda e, j=j: e.dma_start(out=skt[:, j % 2, :], in_=subk[j * 128:(j + 1) * 128, :]), writes=[("skt", j % 2)])
                b = nb()
                P.op('pe', lambda e, b=b: e.transpose(ps[b][:, 0:128], skt[:, j % 2, :], ident_f), reads=[("skt", j % 2), "cstf"], writes=[("ps", b)])
                copy('dve', SKT[:, j, :], ps[b][:, 0:128], [("ps", b)], ["SKT"])
            A.free("skt")
            QPT = A.alloc("QPT", [16, T], BF16)
            for cb in range(8):
                s = load_w([(w_pq[:, cb * 256:(cb + 1) * 256], 256)])
                for hh in range(2):
                    for tb in range(2):
                        proj_fm(s, hh * 128, X1T, "X1T", tb * 512, 512,
                                lambda b, ap, cb=cb, hh=hh, tb=tb: copy(evac_eng(), QPT[:, cb * 2 + hh, tb * 512:(tb + 1) * 512], ap, [("ps", b)], ["QPT"]))
            A.free("X1T", *["wt%d" % i for i in range(len(wts))])
            SC = A.alloc("SC", [16, 128], F32)
            TS = A.alloc("TS", [16, 16], F32)
            TI = A.alloc("TI", [16, 16], U32)
            TIF = A.alloc("TIF", [16, 16], F32)
            CS = A.alloc("CS", [4, 256], F32)
            CI = A.alloc("CI", [4, 256], F32)
            BS = A.alloc("BS", [8, 16], F32)
            BP = A.alloc("BP", [4, 16], U32)
            BPF = A.alloc("BPF", [4, 16], F32)
            IDF = A.alloc("IDF", [128], F32)
            IDUa = A.alloc("IDUa", [NT, 128], U32)
            GWa = A.alloc("GWa", [NT, 128], F32)
            JK = A.alloc("JK", [4, 256], BF16)
            SM3 = A.alloc("SM3", [4, 4], F32)
            AAs = [A.alloc("AA%d" % i, [128], F32) for i in range(2)]
            HAs = [A.alloc("HA%d" % i, [128], F32) for i in range(2)]
            JK2 = A.alloc("JK2", [D], BF16)
            DG = A.alloc("DG", [4, 128], BF16)
            NSL = 7
            UBs = [A.alloc("UB%d" % i, [2 * D], BF16) for i in range(NSL)]

            NI = 4

            def select(tt):
                tsl = slice(tt * 128, (tt + 1) * 128)
                GW = GWa[:, tt, :].rearrange("p (a b) -> p a b", a=8)
                for q4 in range(4):
                    for jj in range(4):
                        j = q4 * 4 + jj
                        P.op('pe', lambda e: e.matmul(ps[4 + q4][:, jj * 128:(jj + 1) * 128], lhsT=QPT[:, j, tsl], rhs=SKT[:, j, :], start=True, stop=True),
                             reads=["QPT", "SKT"], writes=[("ps", 4 + q4)])
                    copy('act', SC[:, q4 * 4:(q4 + 1) * 4, :], ps[4 + q4][:].rearrange("p (a b) -> p a b", a=4), [("ps", 4 + q4)], [("SC", q4 * 4 + i) for i in range(4)])
                yield
                for j in range(16):
                    P.op('dve', lambda e: e.max(out=TS[:, j, 0:8], in_=SC[:, j, :]), reads=[("SC", j)], writes=[("TS", j)])
                    if j % 2:
                        yield
                for j in range(16):
                    P.op('dve', lambda e: e.max_index(out=TI[:, j, 0:8], in_max=TS[:, j, 0:8], in_values=SC[:, j, :]), reads=[("SC", j), ("TS", j)], writes=[("TI", j)])
                    if j % 2:
                        yield
                for j in range(16):
                    P.op('dve', lambda e: e.match_replace(out=SC[:, j, :], in_to_replace=TS[:, j, 0:8], in_values=SC[:, j, :], imm_value=-1e30), reads=[("SC", j), ("TS", j)], writes=[("SC", j)])
                    if j % 2:
                        yield
                for j in range(16):
                    P.op('dve', lambda e: e.max(out=TS[:, j, 8:16], in_=SC[:, j, :]), reads=[("SC", j)], writes=[("TS", j)])
                    if j % 2:
                        yield
                for j in range(16):
                    P.op('dve', lambda e: e.max_index(out=TI[:, j, 8:16], in_max=TS[:, j, 8:16], in_values=SC[:, j, :]), reads=[("SC", j), ("TS", j)], writes=[("TI", j)])
                    if j % 2:
                        yield
                P.op('dve', lambda e: e.tensor_copy(out=TIF, in_=TI), reads=[("TI", j) for j in range(16)], writes=["TIF"])
                yield
                TS4 = TS.rearrange("p (h two) k -> p h two k", two=2)
                TI4 = TIF.rearrange("p (h two) k -> p h two k", two=2)
                for h0 in range(0, 8, NI):
                    hs = [(h0 + i, i) for i in range(NI)]
                    for h, i in hs:
                        csv = CS[:, i, :].rearrange("p (a b) -> p a b", a=16)
                        P.op('dve', lambda e: e.tensor_tensor(out=csv, in0=TS4[:, h, 0, :].unsqueeze(2).to_broadcast([128, 16, 16]),
                                                              in1=TS4[:, h, 1, :].unsqueeze(1).to_broadcast([128, 16, 16]), op=ALU.add),
                             reads=[("TS", 2 * h), ("TS", 2 * h + 1)], writes=[("CS", i)])
                    yield
                    for h, i in hs:
                        civ = CI[:, i, :].rearrange("p (a b) -> p a b", a=16)
                        P.op('dve', lambda e: e.scalar_tensor_tensor(out=civ, in0=TI4[:, h, 0, :].unsqueeze(2).to_broadcast([128, 16, 16]), scalar=128.0,
                                                                     in1=TI4[:, h, 1, :].unsqueeze(1).to_broadcast([128, 16, 16]), op0=ALU.mult, op1=ALU.add),
                             reads=["TIF"], writes=[("CI", i)])
                    yield
                    for h, i in hs:
                        P.op('dve', lambda e: e.max(out=BS[:, h, 0:8], in_=CS[:, i, :]), reads=[("CS", i)], writes=[("BS", h)])
                    yield
                    for h, i in hs:
                        P.op('dve', lambda e: e.max_index(out=BP[:, i, 0:8], in_max=BS[:, h, 0:8], in_values=CS[:, i, :]), reads=[("CS", i), ("BS", h)], writes=[("BP", i)])
                    yield
                    for h, i in hs:
                        P.op('dve', lambda e: e.match_replace(out=CS[:, i, :], in_to_replace=BS[:, h, 0:8], in_values=CS[:, i, :], imm_value=-1e30), reads=[("CS", i), ("BS", h)], writes=[("CS", i)])
                    yield
                    for h, i in hs:
                        P.op('dve', lambda e: e.max(out=BS[:, h, 8:16], in_=CS[:, i, :]), reads=[("CS", i)], writes=[("BS", h)])
                    yield
                    for h, i in hs:
                        P.op('dve', lambda e: e.max_index(out=BP[:, i, 8:16], in_max=BS[:, h, 8:16], in_values=CS[:, i, :]), reads=[("CS", i), ("BS", h)], writes=[("BP", i)])
                    yield
                    for h, i in hs:
                        P.op('dve', lambda e: e.tensor_copy(out=BPF[:, i, :], in_=BP[:, i, :]), reads=[("BP", i)], writes=[("BPF", i)])
                    yield
                    for k in range(16):
                        for h, i in hs:
                            P.op('dve', lambda e: e.scalar_tensor_tensor(out=JK[:, i, :], in0=iota_f, scalar=BPF[:, i, k:k + 1], in1=CI[:, i, :], op0=ALU.is_equal, op1=ALU.mult,
                                                                         accum_out=IDF[:, h * 16 + k:h * 16 + k + 1]),
                                 reads=["cstf", ("BPF", i), ("CI", i)], writes=[("IDF", h * 16 + k)])
                            if i % 2:
                                yield
                    for h, i in hs:
                        P.op('dve', lambda e: e.tensor_scalar(out=SM3[:, i, 0:1], in0=BS[:, h, 0:1], scalar1=-1.0, scalar2=None, op0=ALU.mult), reads=[("BS", h)], writes=[("SM3", i)])
                    for h, i in hs:
                        P.op('act', lambda e: e.activation(out=GW[:, h, :], in_=BS[:, h, :], func=AF.Exp, bias=SM3[:, i, 0:1], accum_out=SM3[:, i, 1:2]),
                             reads=[("BS", h), ("SM3", i)], writes=[("GWa", tt, h), ("SM3", i)])
                    yield
                    for h, i in hs:
                        P.op('dve', lambda e: e.reciprocal(out=SM3[:, i, 2:3], in_=SM3[:, i, 1:2]), reads=[("SM3", i)], writes=[("SM3", i)])
                    for h, i in hs:
                        P.op('dve', lambda e: e.tensor_scalar(out=GW[:, h, :], in0=GW[:, h, :], scalar1=SM3[:, i, 2:3], scalar2=None, op0=ALU.mult),
                             reads=[("GWa", tt, h), ("SM3", i)], writes=[("GWa", tt, h)])
                    yield
                P.op('dve', lambda e: e.tensor_copy(out=IDUa[:, tt, :], in_=IDF), reads=[("IDF", c) for c in range(128)], writes=[("IDUa", tt)])

            P.op('dve', lambda e: e.memset(JK, 0.0), writes=["JK"])
            P.op('dve', lambda e: e.memset(JK2, 0.0), writes=["JK2"])
            for _ in select(0):
                pass
            conv_chunks(NCH)
            for kb in ("U16", "V16"):
                for ev in P.all_events(kb):
                    P.need('pool', ev)
            gsl = [0]
            GRP = 2

            def v_finish(tt):
                xr = X1[:, tt, :]
                for q4 in range(4):
                    P.op('dve', lambda e: e.scalar_tensor_tensor(out=xr[:, q4 * 512:(q4 + 1) * 512], in0=xr[:, q4 * 512:(q4 + 1) * 512], scalar=DN_ALPHA, in1=ps[q4][:, 0:512],
                                                                 op0=ALU.mult, op1=ALU.add), reads=[("X1", tt), ("ps", q4)], writes=[("X1", tt)])

            for tt in range(NT):
                gen = select(tt + 1) if tt + 1 < NT else iter(())
                for g0 in range(0, 128, GRP):
                    par = (g0 // GRP) % 2
                    AAp, HAp = AAs[par], HAs[par]
                    slots = []
                    for k in range(g0, g0 + GRP):
                        gsl[0] = (gsl[0] + 1) % NSL
                        sl = gsl[0]
                        slots.append(sl)
                        P.dma('pool', lambda e: e.indirect_dma_start(out=UBs[sl], out_offset=None, in_=uv16,
                                                                     in_offset=bass.IndirectOffsetOnAxis(ap=IDUa[:, tt, k:k + 1], axis=0)),
                              reads=[("IDUa", tt)], writes=["UB%d" % sl])
                        P.op('dve', lambda e: e.scalar_tensor_tensor(out=JK2, in0=UBs[sl][:, 0:D], scalar=1.0, in1=X1[:, tt, :], op0=ALU.mult, op1=ALU.mult, accum_out=AAp[:, k:k + 1]),
                             reads=["UB%d" % sl, ("X1", tt)], writes=[("AA%d" % par, k)])
                        next(gen, None)
                        next(gen, None)
                    P.op('act', lambda e: e.activation(out=HAp[:, g0:g0 + GRP], in_=AAp[:, g0:g0 + GRP], func=AF.Gelu), reads=[("AA%d" % par, kk) for kk in range(g0, g0 + GRP)], writes=["HA%d" % par])
                    P.op('dve', lambda e: e.tensor_tensor(out=HAp[:, g0:g0 + GRP], in0=HAp[:, g0:g0 + GRP], in1=GWa[:, tt, g0:g0 + GRP], op=ALU.mult),
                         reads=["HA%d" % par] + [("GWa", tt, hh_) for hh_ in range(8)], writes=["HA%d" % par])
                    for k, sl in zip(range(g0, g0 + GRP), slots):
                        ds = k % 4
                        P.op('act', lambda e: e.activation(out=DG[:, ds, :], in_=ident_b, func=AF.Copy, scale=HAp[:, k:k + 1]),
                             reads=["cstb", "HA%d" % par], writes=[("DG", ds)])
                        for q4 in range(4):
                            P.op('pe', lambda e: e.matmul(ps[q4][:, 0:512], lhsT=DG[:, ds, :], rhs=UBs[sl][:, D + q4 * 512:D + (q4 + 1) * 512], start=(k == 0), stop=(k == 127)),
                                 reads=[("DG", ds), "UB%d" % sl], writes=[("ps", q4)])
                for _ in gen:
                    pass
                v_finish(tt)
            A.free(*["UB%d" % i for i in range(NSL)])
            A.free("JK2", "DG", "AA0", "AA1", "HA0", "HA1", "QPT", "SC", "TS", "TI", "TIF", "CS", "CI", "BS", "BP", "BPF", "IDF", "JK", "SKT")
            layer_norm_tiles(2, False)

        evs = []
        if stage in (1, 2):
            src_, k_ = (hT, "hT") if stage == 1 else (yT, "yT")
            evs.append(P.dma('pool', lambda e: e.dma_start(out=out.rearrange("(c p) t -> p c t", p=128), in_=src_), reads=[k_]))
        for tt in range(NT if stage >= 3 else 0):
            evs.append(P.dma('sp', lambda e, tt=tt: e.dma_start(out=out[tt * 128:(tt + 1) * 128, :], in_=X1[:, tt, :]), reads=[("X1", tt)]))
        for ev in evs:
            P.need('sp', ev)
        P.emit()
    return nc


def make_in_maps(inp):
    x = np.asarray(inp["x"], np.float32)
    cst0 = np.concatenate([np.eye(128, dtype=np.float32), np.triu(np.ones((128, 128), np.float32)),
                           np.ones((128, 128), np.float32)], 1)
    shared = {
        "w_in": np.ascontiguousarray(inp["w_in"][0]), "conv_w": np.ascontiguousarray(np.asarray(inp["ml_conv_w"][0]).reshape(4, 16, 128).transpose(2, 1, 0).reshape(128, 64)),
        "conv_b": np.ascontiguousarray(np.asarray(inp["ml_conv_b"][0]).reshape(16, 128).T), "gate_b": np.ascontiguousarray(inp["ml_gate_b"][0].reshape(1, 8)),
        "norm_w": np.ascontiguousarray(inp["ml_norm_w"]), "fox_fb": np.ascontiguousarray(inp["fox_f_b"]),
        "w_bml": np.ascontiguousarray(inp["w_branch_ml"][0]), "w_bfx": np.ascontiguousarray(inp["w_branch_fox"][0]),
        "w_out": np.ascontiguousarray(inp["w_out"][0]), "w_xq": np.ascontiguousarray(inp["w_xq"][0]),
        "w_xkv": np.ascontiguousarray(inp["w_xkv"][0]), "w_xo": np.ascontiguousarray(inp["w_xo"][0]),
        "w_pq": np.ascontiguousarray(inp["peer_wq"][0]), "subk": np.ascontiguousarray(inp["peer_sub_keys"][0].reshape(2048, 128)),
        "peer_u": np.ascontiguousarray(inp["peer_u"][0]), "peer_v": np.ascontiguousarray(inp["peer_v"][0]),
        "ln_w": np.ascontiguousarray(inp["ln_w"][0]), "ln_b": np.ascontiguousarray(inp["ln_b"][0]),
    }
    shared = {k: np.asarray(v, np.float32) for k, v in shared.items()}
    maps = []
    for c in range(8):
        b, half = c // 2, c % 2
        m = dict(shared)
        m["x_own"] = np.ascontiguousarray(x[b, half * T:(half + 1) * T])
        m["x_prev"] = np.ascontiguousarray(x[b, 0:T]) if half == 1 else np.zeros((T, D), np.float32)
        m["mem"] = np.ascontiguousarray(np.asarray(inp["mem"], np.float32)[b])
        m["cst"] = np.concatenate([cst0, np.full((128, 128), float(half), np.float32),
                                   np.tile(np.arange(256, dtype=np.float32)[None, :], (128, 1))], 1)
        maps.append(m)
    return maps


def kernel(**inputs):
    nc = build_nc()
    maps = make_in_maps(inputs)
    res = run_bass_kernel_spmd(nc, maps, core_ids=list(range(8)))
    outp = np.zeros((4, 2048, D), np.float32)
    for c in range(8):
        b, half = c // 2, c % 2
        outp[b, half * T:(half + 1) * T] = res.results[c]["out"]
    return outp
```
